# Optimizing a Trainium2 kernel written in Bass

```python
import math
import jax, jax.numpy as jnp
from jax import lax
import numpy as np

D_MODEL = 1024
BATCH = 8
SEQ = 4096
DEPTH = 1
DEC_BATCH = 32
DEC_SEQ = 16
PAST_LEN = 2048

CHUNK = 64
Q_BLOCK = 128
EPS = 1e-6
GLA_HEADS = 4
GLA_DK = 64
GLA_DV = 128
GLA_RANK = 16
GLA_TAU = 16.0
DIFF_HEADS = 4
DIFF_DH = 64
DIFF_DV = 128
D_FF = 4 * D_MODEL
W_Q_G = GLA_HEADS * GLA_DK
W_K_G = GLA_HEADS * GLA_DK
W_V_G = GLA_HEADS * GLA_DV
W_R_G = GLA_HEADS * GLA_DV
W_A_G = GLA_RANK
W_Q_D = DIFF_HEADS * 2 * DIFF_DH
W_K_D = DIFF_HEADS * 2 * DIFF_DH
W_V_D = DIFF_HEADS * DIFF_DV
SPLITS = tuple(np.cumsum([W_Q_G, W_K_G, W_V_G, W_R_G, W_A_G, W_Q_D, W_K_D]).tolist())
D_IN = W_Q_G + W_K_G + W_V_G + W_R_G + W_A_G + W_Q_D + W_K_D + W_V_D
D_MIX = W_V_G + W_V_D

kernel_name = "hybrid_gla_diffattn_streaming_step"


def rms(x, g):
    xf = x.astype(jnp.float32)
    y = xf * lax.rsqrt(jnp.mean(xf * xf, axis=-1, keepdims=True) + EPS)
    return (y * g.astype(jnp.float32)).astype(x.dtype)


def alibi_slopes():
    return jnp.exp2(-8.0 / DIFF_HEADS * jnp.arange(1, DIFF_HEADS + 1, dtype=jnp.float32))


def gla_chunk(S, q, k, v, g):
    L = q.shape[2]
    b = jnp.cumsum(g, axis=2)
    mask = jnp.tril(jnp.ones((L, L), dtype=bool))
    diff = b[:, :, :, None, :] - b[:, :, None, :, :]
    decay = jnp.exp(jnp.where(mask[None, None, :, :, None], diff, -jnp.inf))
    A = jnp.einsum('bhtd,bhsd,bhtsd->bhts', q, k, decay)
    o = jnp.einsum('bhts,bhsv->bhtv', A, v) + jnp.einsum('bhtd,bhdv->bhtv', q * jnp.exp(b), S)
    bL = b[:, :, -1]
    S_new = jnp.exp(bL)[..., None] * S + jnp.einsum('bhsd,bhsv->bhdv', k * jnp.exp(bL[:, :, None, :] - b), v)
    return S_new, o


def gla_scan(S0, q, k, v, g):
    B, T, H, _ = q.shape
    L = min(CHUNK, T)
    n = T // L

    def to_chunks(a):
        return a.reshape(B, n, L, H, a.shape[-1]).transpose(1, 0, 3, 2, 4)

    def step(S, c):
        return gla_chunk(S, *c)

    S, o = lax.scan(step, S0, (to_chunks(q), to_chunks(k), to_chunks(v), to_chunks(g)))
    o = o.transpose(1, 0, 3, 2, 4).reshape(B, T, H, o.shape[-1])
    return S, o


def diff_attend(q, k, v, q_pos, k_pos, lam, slopes):
    q = q.astype(jnp.float32)
    k = k.astype(jnp.float32)
    scale = DIFF_DH ** -0.5
    s1 = jnp.einsum('bqhd,bkhd->bhqk', q[..., :DIFF_DH], k[..., :DIFF_DH]) * scale
    s2 = jnp.einsum('bqhd,bkhd->bhqk', q[..., DIFF_DH:], k[..., DIFF_DH:]) * scale
    mask = (k_pos[None, :] // CHUNK) <= (q_pos[:, None] // CHUNK)
    dist = jnp.abs(q_pos[:, None] - k_pos[None, :]).astype(jnp.float32)
    bias = jnp.where(mask[None], -slopes[:, None, None] * dist[None], -jnp.inf)
    p = jax.nn.softmax(s1 + bias, axis=-1) - lam * jax.nn.softmax(s2 + bias, axis=-1)
    return jnp.einsum('bhqk,bkhv->bqhv', p, v.astype(jnp.float32))


def layer_forward(x, S0, past_k, past_v, start, lam_init, w_in, w_gate_up, b_gate, g_gla_out,
                  lam_q1, lam_k1, lam_q2, lam_k2, g_subln, w_out, g_pre_mix, g_post_mix,
                  g_pre_ffn, g_post_ffn, w_ff_up, w_ff_down):
    B, T, _ = x.shape
    h = rms(x, g_pre_mix)
    z = h @ w_in
    q_g, k_g, v_g, r_g, a_g, q_d, k_d, v_d = jnp.split(z, SPLITS, axis=-1)
    gate = jax.nn.log_sigmoid((a_g @ w_gate_up + b_gate).astype(jnp.float32)) / GLA_TAU
    qg = q_g.reshape(B, T, GLA_HEADS, GLA_DK).astype(jnp.float32) * (GLA_DK ** -0.5)
    kg = k_g.reshape(B, T, GLA_HEADS, GLA_DK).astype(jnp.float32)
    vg = v_g.reshape(B, T, GLA_HEADS, GLA_DV).astype(jnp.float32)
    gg = gate.reshape(B, T, GLA_HEADS, GLA_DK)
    S_new, o_g = gla_scan(S0.astype(jnp.float32), qg, kg, vg, gg)
    rg = r_g.reshape(B, T, GLA_HEADS, GLA_DV).astype(jnp.float32)
    o_g = rms(o_g, g_gla_out) * jax.nn.silu(rg)
    qd = q_d.reshape(B, T, DIFF_HEADS, 2 * DIFF_DH)
    kd = k_d.reshape(B, T, DIFF_HEADS, 2 * DIFF_DH)
    vd = v_d.reshape(B, T, DIFF_HEADS, DIFF_DV)
    lam = (jnp.exp(jnp.sum(lam_q1.astype(jnp.float32) * lam_k1.astype(jnp.float32)))
           - jnp.exp(jnp.sum(lam_q2.astype(jnp.float32) * lam_k2.astype(jnp.float32))) + lam_init)
    slopes = alibi_slopes()
    if past_k is None:
        k_pos = jnp.arange(T, dtype=jnp.int32)
        nq = T // Q_BLOCK
        qb = qd.reshape(B, nq, Q_BLOCK, DIFF_HEADS, 2 * DIFF_DH).transpose(1, 0, 2, 3, 4)
        pb = k_pos.reshape(nq, Q_BLOCK)
        ob = lax.map(lambda a: diff_attend(a[0], kd, vd, a[1], k_pos, lam, slopes), (qb, pb))
        o_d = ob.transpose(1, 0, 2, 3, 4).reshape(B, T, DIFF_HEADS, DIFF_DV)
    else:
        keys = jnp.concatenate([past_k.astype(kd.dtype), kd], axis=1)
        vals = jnp.concatenate([past_v.astype(vd.dtype), vd], axis=1)
        k_pos = jnp.arange(keys.shape[1], dtype=jnp.int32)
        q_pos = start + jnp.arange(T, dtype=jnp.int32)
        o_d = diff_attend(qd, keys, vals, q_pos, k_pos, lam, slopes)
    o_d = rms(o_d, g_subln) * (1.0 - lam_init)
    o = jnp.concatenate([o_g.reshape(B, T, W_V_G), o_d.reshape(B, T, W_V_D)], axis=-1).astype(x.dtype)
    x = x + rms(o @ w_out, g_post_mix)
    f = rms(x, g_pre_ffn)
    f = jnp.square(jax.nn.relu(f @ w_ff_up)) @ w_ff_down
    x = x + rms(f, g_post_ffn)
    return x, kd, vd, S_new.astype(x.dtype)


def setup_inputs(seed: int = 0) -> dict:
    key = jax.random.key(seed)
    ks = jax.random.split(key, 24)
    f32 = jnp.float32
    nrm = lambda k, s, sc: jax.random.normal(k, s, f32) * sc
    gain = lambda k, n: 1.0 + 0.01 * jax.random.normal(k, (DEPTH, n), f32)
    return {
        "x_prompt": nrm(ks[0], (BATCH, SEQ, D_MODEL), 1.0),
        "x_sample": nrm(ks[1], (DEC_BATCH, DEC_SEQ, D_MODEL), 1.0),
        "cache_k": nrm(ks[2], (DEPTH, DEC_BATCH, PAST_LEN, DIFF_HEADS, 2 * DIFF_DH), 1.0),
        "cache_v": nrm(ks[3], (DEPTH, DEC_BATCH, PAST_LEN, DIFF_HEADS, DIFF_DV), 1.0),
        "state_gla": nrm(ks[4], (DEPTH, DEC_BATCH, GLA_HEADS, GLA_DK, GLA_DV), 1.0),
        "w_in": nrm(ks[5], (DEPTH, D_MODEL, D_IN), D_MODEL ** -0.5),
        "w_gate_up": nrm(ks[6], (DEPTH, GLA_RANK, GLA_HEADS * GLA_DK), GLA_RANK ** -0.5),
        "b_gate": nrm(ks[7], (DEPTH, GLA_HEADS * GLA_DK), 0.01),
        "g_gla_out": gain(ks[8], GLA_DV),
        "lam_q1": nrm(ks[9], (DEPTH, DIFF_DH), 0.1),
        "lam_k1": nrm(ks[10], (DEPTH, DIFF_DH), 0.1),
        "lam_q2": nrm(ks[11], (DEPTH, DIFF_DH), 0.1),
        "lam_k2": nrm(ks[12], (DEPTH, DIFF_DH), 0.1),
        "g_subln": gain(ks[13], DIFF_DV),
        "w_out": nrm(ks[14], (DEPTH, D_MIX, D_MODEL), D_MIX ** -0.5),
        "g_pre_mix": gain(ks[15], D_MODEL),
        "g_post_mix": gain(ks[16], D_MODEL),
        "g_pre_ffn": gain(ks[17], D_MODEL),
        "g_post_ffn": gain(ks[18], D_MODEL),
        "w_ff_up": nrm(ks[19], (DEPTH, D_MODEL, D_FF), D_MODEL ** -0.5),
        "w_ff_down": nrm(ks[20], (DEPTH, D_FF, D_MODEL), D_FF ** -0.5),
    }


def reference(x_prompt, x_sample, cache_k, cache_v, state_gla, w_in, w_gate_up, b_gate, g_gla_out,
              lam_q1, lam_k1, lam_q2, lam_k2, g_subln, w_out, g_pre_mix, g_post_mix,
              g_pre_ffn, g_post_ffn, w_ff_up, w_ff_down):
    yp = x_prompt
    ys = x_sample
    kp_l, vp_l, sp_l, ksm_l, vsm_l, ss_l = [], [], [], [], [], []
    for l in range(DEPTH):
        lam_init = 0.8 - 0.6 * math.exp(-0.3 * l)
        params = (w_in[l], w_gate_up[l], b_gate[l], g_gla_out[l], lam_q1[l], lam_k1[l], lam_q2[l],
                  lam_k2[l], g_subln[l], w_out[l], g_pre_mix[l], g_post_mix[l], g_pre_ffn[l],
                  g_post_ffn[l], w_ff_up[l], w_ff_down[l])
        S0 = jnp.zeros((yp.shape[0], GLA_HEADS, GLA_DK, GLA_DV), jnp.float32)
        yp, kp, vp, sp = layer_forward(yp, S0, None, None, 0, lam_init, *params)
        ys, ksm, vsm, ss = layer_forward(ys, state_gla[l], cache_k[l], cache_v[l], cache_k.shape[2],
                                         lam_init, *params)
        kp_l.append(kp); vp_l.append(vp); sp_l.append(sp)
        ksm_l.append(ksm); vsm_l.append(vsm); ss_l.append(ss)
    k_prompt = jnp.stack(kp_l, axis=0)
    v_prompt = jnp.stack(vp_l, axis=0)
    gla_prompt = jnp.stack(sp_l, axis=0)
    k_sample = jnp.stack(ksm_l, axis=0)
    v_sample = jnp.stack(vsm_l, axis=0)
    gla_sample = jnp.stack(ss_l, axis=0)
    return (yp, ys, k_prompt, v_prompt, gla_prompt, k_sample, v_sample, gla_sample)
```

```python
import math
import numpy as np
import ml_dtypes
import concourse.bass as bass
import concourse.mybir as mybir
from concourse.bass_utils import run_bass_kernel_spmd

F32 = mybir.dt.float32
BF16 = mybir.dt.bfloat16
AF = mybir.ActivationFunctionType
ALU = mybir.AluOpType
AX = mybir.AxisListType

EPS = 1e-6
LAM_INIT = 0.8 - 0.6 * math.exp(-0.3 * 0)
NEG = -30000.0
import os as _os_hop
HOP_NS = float(_os_hop.environ.get("KHOP", "250"))
ENGS = ("pe", "act", "dve", "pool", "sp")
COMPUTE = ("pe", "act", "dve", "pool")


class Buf:
    __slots__ = ("name", "w", "rs", "rd", "excl")

    def __init__(self, name):
        self.name = name
        self.excl = False
        self.w = None
        self.rs = []
        self.rd = []


class Op:
    __slots__ = ("eng", "seq", "fn", "dma", "sig", "waits", "sem", "semval", "clk", "id", "deps", "cost", "lat", "phase", "ridx", "tag", "t0", "rdy", "blk", "prev")


class Prog:
    limit = None
    want_tags = False

    def __init__(self, nc, n_dma_sems=28):
        self.nc = nc
        self.pending = []
        self.final = {e: [] for e in ENGS}
        self.gorder = []
        self.nds = n_dma_sems
        self.nid = 0
        self.phase = 0
        self.sched = True

    def add(self, eng, fn, reads=(), writes=(), dma=False, extra_deps=(), cost=100.0, lat=None):
        if self.limit is not None and fn is not None and self.nid >= self.limit:
            return None
        op = Op()
        op.eng = eng; op.fn = fn; op.dma = dma; op.sig = False; op.waits = []
        op.id = self.nid; self.nid += 1
        op.cost = cost; op.lat = (cost + HOP_NS) if lat is None else lat
        op.phase = self.phase
        if self.want_tags:
            import sys as _s
            f = _s._getframe(2)
            op.tag = (f.f_lineno, f.f_back.f_lineno if f.f_back is not None else 0)
        deps = {}
        for b in reads:
            if b.w is not None:
                deps[b.w.id] = b.w
            if b.excl:
                for r in b.rs:
                    if r.eng != eng:
                        deps[r.id] = r
        for b in writes:
            if b.w is not None:
                deps[b.w.id] = b.w
            for r in b.rs:
                deps[r.id] = r
            for r in b.rd:
                deps[r.id] = r
        for p in extra_deps:
            deps[p.id] = p
        deps.pop(op.id, None)
        op.deps = list(deps.values())
        self.pending.append(op)
        for b in reads:
            if dma:
                b.rd.append(op)
            else:
                b.rs.append(op)
        for b in writes:
            b.w = op; b.rs = []; b.rd = []
        return op

    def schedule_phase(self):
        import heapq
        ops = self.pending
        self.pending = []
        n = len(ops)
        for i, op in enumerate(ops):
            op.ridx = i
        succ = [[] for _ in range(n)]
        nd = [0] * n
        for i, op in enumerate(ops):
            for p in op.deps:
                if p.phase == self.phase:
                    succ[p.ridx].append(i)
                    nd[i] += 1
        order = []
        if not self.sched:
            order = list(range(n))
        else:
            ready = {e: [] for e in ENGS}
            busy = {e: False for e in ENGS}
            ev = []
            cnt = [0]

            lastop = {e: None for e in ENGS}
            rdy_t = [0.0] * n
            blk = [None] * n

            def try_start(e, t):
                if busy[e] or not ready[e]:
                    return
                i = inv[heapq.heappop(ready[e])]
                op = ops[i]
                busy[e] = True
                op.t0 = t; op.rdy = rdy_t[i]; op.blk = blk[i]; op.prev = lastop[e]; lastop[e] = op
                order.append(i)
                cnt[0] += 1
                heapq.heappush(ev, (t + op.cost, 0, cnt[0], i))
                heapq.heappush(ev, (t + op.lat, 1, cnt[0], i))

            import os as _os8
            PRI = _os8.environ.get("KPRI", "idx")
            if PRI != "idx":
                bl_ = [0.0] * n
                for i in range(n - 1, -1, -1):
                    m = 0.0
                    for s in succ[i]:
                        if bl_[s] > m:
                            m = bl_[s]
                    bl_[i] = m + ops[i].lat
                W_ = float(PRI)
                key = [(-(bl_[i]) + W_ * i * 100.0) for i in range(n)]
                order_key = sorted(range(n), key=lambda i: key[i])
                rank = [0] * n
                for r_, i in enumerate(order_key):
                    rank[i] = r_
            else:
                rank = list(range(n))
            inv = {}
            for i in range(n):
                inv[rank[i]] = i
            for i in range(n):
                if nd[i] == 0:
                    heapq.heappush(ready[ops[i].eng], rank[i])
            for e in ENGS:
                try_start(e, 0.0)
            while ev:
                t, kind, _, i = heapq.heappop(ev)
                if kind == 0:
                    e = ops[i].eng
                    busy[e] = False
                    try_start(e, t)
                else:
                    for s in succ[i]:
                        nd[s] -= 1
                        rdy_t[s] = t; blk[s] = ops[i]
                        if nd[s] == 0:
                            e = ops[s].eng
                            heapq.heappush(ready[e], rank[s])
                            try_start(e, t)
            assert len(order) == n, (len(order), n)
            self.sim_time = t if n else 0.0
            import os as _os
            if _os.environ.get("KSIM"):
                busy_t = {e: 0.0 for e in ENGS}
                for op in ops:
                    busy_t[op.eng] += op.cost
                print("PHASE %d: sim %.1f us, n=%d, busy(us): %s" % (self.phase, self.sim_time / 1e3, n,
                      " ".join("%s=%.0f" % (e, busy_t[e] / 1e3) for e in ENGS)))
                if self.want_tags and n:
                    last = max(ops, key=lambda o: o.t0 + o.lat)
                    acc = {}
                    o = last
                    steps = 0
                    while o is not None and steps < 200000:
                        steps += 1
                        if o.t0 > o.rdy + 1e-6 and o.prev is not None:
                            nxt = o.prev
                            key = ("ENG", o.eng, o.tag)
                            dt = o.t0 - nxt.t0
                        elif o.blk is not None:
                            nxt = o.blk
                            key = ("DEP", o.eng, o.tag)
                            dt = o.t0 - nxt.t0
                        else:
                            break
                        acc[key] = acc.get(key, 0.0) + dt
                        o = nxt
                    idle = {}
                    prev_end = 0.0
                    for o2 in sorted([o3 for o3 in ops if o3.eng == "pe"], key=lambda o3: o3.t0):
                        gap = o2.t0 - prev_end
                        if gap > 1.0:
                            bt = (o2.blk.eng, o2.blk.tag) if o2.blk is not None else None
                            k2 = (o2.tag, bt)
                            idle[k2] = idle.get(k2, 0.0) + gap
                        prev_end = o2.t0 + o2.cost
                    print("  PE idle total %.1f us; top:" % (sum(idle.values()) / 1e3))
                    for k2, v in sorted(idle.items(), key=lambda kv: -kv[1])[:22]:
                        print("   %8.1f us  pe-op %s  blocked-by %s" % (v / 1e3, k2[0], k2[1]))
                    import os as _os7
                    if _os7.environ.get("KWIN") and self.phase == 1:
                        w0, w1 = [float(x) * 1e3 for x in _os7.environ["KWIN"].split(",")]
                        for o2 in sorted([o3 for o3 in ops if w0 <= o3.t0 <= w1 and o3.eng in ("pe",)], key=lambda o3: o3.t0):
                            print("   t=%9.2f dur=%6.2f %s %s blk=%s" % (o2.t0 / 1e3, o2.cost / 1e3, o2.eng, o2.tag,
                                  (o2.blk.eng, o2.blk.tag) if o2.blk is not None else None))
                    tot = sum(acc.values())
                    print("  critical path total %.1f us; top contributors:" % (tot / 1e3))
                    for k, v in sorted(acc.items(), key=lambda kv: -kv[1])[:28]:
                        print("   %8.1f us  %s" % (v / 1e3, k))
        for i in order:
            op = ops[i]
            self.final[op.eng].append(op)
            self.gorder.append(op)

    def _pseudo(self, eng, deps):
        op = Op()
        op.eng = eng; op.fn = None; op.dma = False; op.sig = False; op.waits = []
        op.id = self.nid; self.nid += 1
        op.deps = list(deps); op.phase = self.phase
        self.final[eng].append(op)
        self.gorder.append(op)

    def _phase_sinks(self):
        lasts = []
        for e in COMPUTE:
            for op in reversed(self.final[e]):
                if op.fn is not None and not op.dma:
                    if op.phase == self.phase:
                        lasts.append(op)
                    break
        dmas = [op for e in ENGS for op in self.final[e] if op.dma and op.phase == self.phase]
        return lasts + dmas

    def barrier(self):
        self.schedule_phase()
        sinks = self._phase_sinks()
        for e in ENGS:
            self._pseudo(e, sinks)
        self.phase += 1

    def finish(self):
        self.schedule_phase()
        sinks = self._phase_sinks()
        for e in ENGS:
            self._pseudo(e, sinks)

    def lower(self):
        clock = {e: {} for e in ENGS}
        dknown = {e: set() for e in ENGS}
        NQ = 12
        dma_cum = [0] * (self.nds + NQ)
        dma_last = [None] * (self.nds + NQ)
        rr = 0
        rr2 = 0
        seqc = {e: 0 for e in ENGS}
        for op in self.gorder:
            eng = op.eng
            op.seq = seqc[eng]; seqc[eng] += 1
            deps = list(op.deps)
            if op.dma:
                if eng == "pool":
                    s = self.nds + rr2
                    rr2 = (rr2 + 1) % NQ
                else:
                    s = rr
                    rr = (rr + 1) % self.nds
                if dma_last[s] is not None:
                    deps.append(dma_last[s])
                dma_cum[s] += 16
                op.sem = s; op.semval = dma_cum[s]
                dma_last[s] = op
            clk = clock[eng]
            dk = dknown[eng]
            for p in deps:
                if p.dma:
                    if p.id in dk:
                        continue
                    op.waits.append(p); dk.add(p.id)
                else:
                    if p.fn is None:
                        continue
                    if p.eng == eng and eng == "pe":
                        continue
                    if clk.get(p.eng, -1) >= p.seq:
                        continue
                    op.waits.append(p); p.sig = True
                    clk[p.eng] = p.seq
                for k, v in p.clk.items():
                    if clk.get(k, -1) < v:
                        clk[k] = v
            op.clk = dict(clk)

    def emit(self):
        nc = self.nc
        self.lower()
        csem = {e: nc.alloc_semaphore(name="c_" + e) for e in COMPUTE}
        dsem = [nc.alloc_semaphore(name="d_%d" % i) for i in range(self.nds + 12)]
        for e in COMPUTE:
            c = 0
            for op in self.final[e]:
                if op.sig:
                    c += 1
                    op.semval = c

        def run(e, eng):
            for op in self.final[e]:
                for p in op.waits:
                    if p.dma:
                        eng.wait_ge(dsem[p.sem], p.semval)
                    else:
                        eng.wait_ge(csem[p.eng], p.semval)
                if op.fn is None:
                    continue
                ins = op.fn(eng)
                if op.dma:
                    ins.then_inc(dsem[op.sem], 16)
                elif op.sig:
                    ins.then_inc(csem[e], 1)

        with nc.Block() as block:
            @block.tensor
            def _(eng):
                run("pe", eng)

            @block.scalar
            def _(eng):
                run("act", eng)

            @block.vector
            def _(eng):
                run("dve", eng)

            @block.gpsimd
            def _(eng):
                run("pool", eng)

            @block.sync
            def _(eng):
                run("sp", eng)


class Tl:
    __slots__ = ("t", "b", "off")

    def __init__(self, t, name):
        self.t = t
        self.b = Buf(name)


class Ring:
    def __init__(self, tiles):
        self.tiles = tiles
        self.i = 0

    def next(self):
        t = self.tiles[self.i % len(self.tiles)]
        self.i += 1
        return t


SLOPES = [2.0 ** (-2.0 * (h + 1)) for h in range(4)]
QG = 256


def make_consts(T, PAST):
    bf = ml_dtypes.bfloat16
    NT = T // 128
    NKT = PAST // 128
    c = {}
    c["ident"] = np.eye(128, dtype=np.float32).astype(bf)
    s = np.arange(128)
    same = (s[:, None] // 64) == (s[None, :] // 64)
    c["triLE"] = (np.where(same & (s[:, None] <= s[None, :]), -1.0 / 16, 0.0)).astype(np.float32)
    c["triGT"] = (np.where(same & (s[:, None] > s[None, :]), -1.0 / 16, 0.0)).astype(np.float32)
    c["ind"] = (np.where((s[:, None] // 64) == np.arange(2)[None, :], -1.0 / 16, 0.0)).astype(np.float32)
    c["maskA"] = np.where(same & (s[:, None] <= s[None, :]), 1.0, 0.0).astype(np.float32)
    c["cm"] = np.where((s[:, None] // 64) == np.arange(2)[None, :], 1.0, 0.0).astype(np.float32)
    u = np.arange(64)
    same16 = (u[:, None] // 16) == (u[None, :] // 16)
    c["triLE_s"] = np.where(same16 & (u[:, None] <= u[None, :]), -1.0 / 16, 0.0).astype(np.float32)
    c["triGT_s"] = np.where(same16 & (u[:, None] > u[None, :]), -1.0 / 16, 0.0).astype(np.float32)
    c["ind_s"] = np.where((u[:, None] // 16) == np.arange(4)[None, :], -1.0 / 16, 0.0).astype(np.float32)
    c["rm_s"] = np.where((u[:, None] // 16) == np.arange(4)[None, :], 1.0, 0.0).astype(np.float32)
    c["maskA_s"] = np.where(same16 & (u[:, None] <= u[None, :]), 1.0, 0.0).astype(np.float32)
    corr = np.zeros((128, 4, 128), np.float32)
    k = s[:, None]; q = s[None, :]
    kc = k // 64; qc = q // 64
    for h in range(4):
        m = np.zeros((128, 128), np.float32)
        m = np.where((kc == qc) & (k > q), -2.0 * SLOPES[h] * (k - q), m)
        m = np.where(kc > qc, NEG, m)
        corr[:, h, :] = m
    c["corr"] = corr.reshape(128, 512).astype(bf)
    bt = np.zeros((128, 4, NT + 2), np.float32)
    for h in range(4):
        for idx in range(NT + 2):
            bt[:, h, idx] = SLOPES[h] * (s - 128.0 * (idx - 1))
    c["btab"] = bt.reshape(128, 4 * (NT + 2))
    sbt = np.zeros((128, 4, NKT), np.float32)
    for h in range(4):
        for kt in range(NKT):
            sbt[:, h, kt] = SLOPES[h] * (kt * 128 + s - PAST)
    c["sbias"] = sbt.reshape(128, 4 * NKT)
    nb = np.zeros((64, 4, 2, 64), np.float32)
    kb = u[:, None] // 16; kj = u[:, None] % 16
    qb = u[None, :] // 16; qi = u[None, :] % 16
    for h in range(4):
        m = np.where(kb == qb, -SLOPES[h] * np.abs(qi - kj) + SLOPES[h] * qi, NEG)
        nb[:, h, 0, :] = m
        nb[:, h, 1, :] = m
    c["nbias"] = nb.reshape(64, 512)
    return c


CONST_DT = {"ident": BF16, "corr": BF16}


class Arena:
    def __init__(self, nc, n):
        self.t = nc.alloc_sbuf_tensor("arena", [128, n], BF16)
        self.ap = self.t[:, :]
        self.n = n
        self.off = 0

    def alloc(self, name, shape, dt=BF16):
        free = 1
        for d in shape[1:]:
            free *= d
        nel = free if dt == BF16 else free * 2
        nal = (nel + 31) // 32 * 32
        assert self.off + nal <= self.n, ("arena overflow", name, self.off, nal, self.n)
        ap = self.ap[0:shape[0], self.off:self.off + nel]
        if dt == F32:
            ap = ap.bitcast(F32)
        if len(shape) > 2:
            names = "abcdef"[:len(shape) - 1]
            pat = "p (%s) -> p %s" % (" ".join(names), " ".join(names))
            kw = {names[i]: shape[1 + i] for i in range(len(shape) - 2)}
            ap = ap.rearrange(pat, **kw)
        tl = Tl(ap, name)
        tl.off = self.off
        self.off += nal
        return tl


def build_program(T, PAST, stop=None):
    NT = T // 128
    NG = T // QG
    NKT = PAST // 128
    assert NKT % 8 == 0 and T % QG == 0
    nc = bass.Bass("TRN2", target_bir_lowering=False)
    P = Prog(nc)
    if stop is not None and stop.startswith("n"):
        P.limit = int(stop[1:])

    def din(name, shape, dt=F32):
        return nc.dram_tensor(name, list(shape), dt, kind="ExternalInput").ap()

    def dout(name, shape, dt=F32):
        return nc.dram_tensor(name, list(shape), dt, kind="ExternalOutput").ap()

    xp = din("xp", [T, 1024]); xs = din("xs", [64, 1024])
    ck = din("ck", [4, PAST, 512]); cv = din("cv", [4, PAST, 512]); sg = din("sg", [4, 2, 128, 128])
    win_d = din("win", [128, 8 * 3088]); wout_d = din("wout", [128, 8 * 1024])
    wup_d = din("wup", [128, 8 * 4096]); wdn_d = din("wdn", [128, 32 * 1024])
    wgu_d = din("wgu", [17, 256]); gpre_d = din("gpre", [128, 8]); gpf_d = din("gpf", [128, 8])
    ggla_d = din("ggla", [128, 512]); gsub_d = din("gsub", [128, 128])
    gpm_d = din("gpm", [128, 1024]); gpo_d = din("gpo", [128, 1024]); lam_d = din("lam", [128, 256])
    cshape = {"ident": [128, 128], "triLE": [128, 128], "triGT": [128, 128], "ind": [128, 2],
              "maskA": [128, 128], "cm": [128, 2], "triLE_s": [64, 64], "triGT_s": [64, 64], "ind_s": [64, 4],
              "rm_s": [64, 4], "maskA_s": [64, 64], "corr": [128, 512], "btab": [128, 4 * (NT + 2)],
              "sbias": [128, 4 * NKT], "nbias": [64, 512]}
    cd = {k: din("c_" + k, v, CONST_DT.get(k, F32)) for k, v in cshape.items()}
    yp_d = dout("yp", [T, 1024]); ys_d = dout("ys", [64, 1024])
    kp_d = dout("kp", [T, 512]); vp_d = dout("vp", [T, 512]); gp_d = dout("gp", [2, 128, 128])
    ksm_d = dout("ksm", [64, 512]); vsm_d = dout("vsm", [64, 512]); gs_d = dout("gs", [4, 2, 128, 128])
    x1s = nc.dram_tensor("x1s", [T + 64, 1024], F32).ap()
    wupb = nc.dram_tensor("wupb", [128, 8 * 4096], BF16).ap()
    wdnb = nc.dram_tensor("wdnb", [128, 32 * 1024], BF16).ap()
    x1sb = [Buf("x1s%d" % i) for i in range(NT + 1)]

    def sb(name, shape, dt=F32):
        return Tl(nc.alloc_sbuf_tensor("s_" + name, list(shape), dt), name)

    C = {k: sb("k_" + k, v, CONST_DT.get(k, F32)) for k, v in cshape.items() if k != "nbias"}
    wg_aug = sb("wg_aug", [17, 256], BF16)
    gpre = sb("gpre", [128, 8]); gpf = sb("gpf", [128, 8])
    ggla = sb("ggla", [128, 512]); gsub8 = sb("gsub8", [128, 128])
    gpm = sb("gpm", [128, 1024])
    neg_lam = sb("neg_lam", [128, 1]); lam_s = sb("lam_s", [128, 4])
    C_eps = sb("c_eps", [128, 1]); C_one = sb("c_one", [128, 1])
    stat = Ring([sb("stat%d" % i, [128, 8]) for i in range(24)])
    agT = sb("agT", [17, 128], BF16)
    S32 = [sb("S32_%d" % p, [128, 128]) for p in range(2)]
    Sbf = [Ring([sb("Sbf%d_%d" % (p, v), [128, 128], BF16) for v in range(2)]) for p in range(2)]

    import os as _os3
    NA = (nc.sbuf_bytes_remaining - 1024) // 2 // 32 * 32 + int(_os3.environ.get("FAKE_NA", "0"))
    RD = {}
    for kv in _os3.environ.get("RINGS", "").split(","):
        if "=" in kv:
            RD[kv.split("=")[0]] = int(kv.split("=")[1])
    AR = Arena(nc, NA)
    W = {}

    RD_on = [False]

    def wring(name, n, shape, dt=BF16):
        if RD_on[0]:
            n = RD.get(name, n)
        W[name] = Ring([AR.alloc("%s%d" % (name, i), shape, dt) for i in range(n)])

    def wtile(name, shape, dt=BF16):
        W[name] = AR.alloc(name, shape, dt)

    def alloc_common(smp=False):
        wring("xin", 1, [128, 1024], F32)
        wring("b1k", 3, [128, 1024])
        wring("zqk", 1, [128, 512], F32); wring("vg", 2, [128, 512])
        wring("Eg", 1, [128, 512], F32); wring("rr", 2, [128, 512], F32)
        wring("qd_bf", 1, [128, 512]); wring("kd_f", 1, [128, 512], F32); wring("kd_bf", 1, [128, 512])
        wring("vd_f", 1, [128, 512], F32); wring("ag_bf", 1, [128, 16])
        wring("Lg", 1, [128, 256], F32)
        wring("ex", 2, [128, 256], F32)
        wring("ebL", 1, [128, 8], F32)
        wring("qt", 1, [128, 256]); wring("kt", 1, [128, 256]); wring("kh", 1, [128, 2, 256])
        wring("qz", 1, [128, 4, 128]); wring("kTg", 1, [128, 2, 128]); wring("ATm", 4, [128, 128])
        for t_ in W["qz"].tiles:
            MEMSET("pool", t_.t[:, :, :], 0.0, [t_])
        wring("ocat", 1 if smp else 3, [128, 1024])
        wring("t1", 1, [128, 2, 128], F32); wring("od", 1, [128, 2, 128], F32)
        wring("ytmp", 1, [128, 1024], F32); wring("xre", 1, [128, 1024], F32)

    banks = [Tl(nc.alloc_psum_tensor("bank%d" % i, [128, 512], F32), "bank%d" % i) for i in range(8)]
    for b_ in banks:
        b_.b.excl = True

    def bfview(tl):
        v = Tl(tl.t[:, :].bitcast(BF16).rearrange("p (k n) -> p k n", k=8), tl.b.name + "_bf")
        v.b = tl.b
        return v

    tp = bfview(banks[0])
    import os as _os
    NZP = int(_os.environ.get("NZP", "3"))
    zp = Ring([banks[1], banks[2]] + ([banks[7]] if NZP == 3 else []))
    scr = zp if NZP != 4 else Ring([banks[7]])
    if NZP == 9:
        class FakeRing:
            def __init__(self, tl, n, view=None):
                self.tiles = []
                for i in range(n):
                    t_ = Tl(tl.t, "fk")
                    t_.b.excl = True
                    self.tiles.append(t_)
                self.i = 0
            def next(self):
                t_ = self.tiles[self.i % len(self.tiles)]; self.i += 1
                return t_
        zp = FakeRing(banks[1], int(_os.environ.get("FZP", "6")))
        scr = FakeRing(banks[2], int(_os.environ.get("FSC", "4"))) if _os.environ.get("FSC", "4") != "0" else zp
    import os as _os5
    if True:
        if _os5.environ.get("FAKETP"):
            class TPProxy:
                def __init__(self, base, n):
                    self.t = base.t
                    self.bufs = [Buf("ftp%d" % i) for i in range(n)]
                    for b_ in self.bufs:
                        b_.excl = True
                    self.k = 0
                    self.cur = self.bufs[0]
                def rotate(self):
                    self.k += 1
                    self.cur = self.bufs[self.k % len(self.bufs)]
                @property
                def b(self):
                    return self.cur
            tp = TPProxy(tp, int(_os5.environ.get("FAKETP")))
    if NZP == 5:
        scr = Ring([Tl(banks[1].t, "fake_sc0"), Tl(banks[2].t, "fake_sc1")])
        for t_ in scr.tiles:
            t_.b.excl = True
        zp = Ring([banks[1], banks[2], banks[7]])
    gU = banks[3]; og = banks[4]
    accb = [banks[5], banks[6]]
    gAT = banks[7]
    tpB = bfview(banks[7])

    def bl(xs_):
        return [x.b if hasattr(x, "b") else x for x in xs_]

    def fsz(ap):
        n = 1
        for d in ap.shape[1:]:
            n *= d
        return n

    def ecost(eng, n):
        if eng == "act":
            return n / 1.2 + 200.0
        if eng == "dve":
            return n * 1.04 + 100.0
        return n * 3.0 + 300.0

    def MM(out, lhsT, rhs, start, stop, rd, wr, skip=True):
        c = max(fsz(rhs), 64) / 2.2 * (4.0 if rhs.dtype == F32 else 1.0) + 45.0
        return P.add("pe", lambda e: e.matmul(out, lhsT=lhsT, rhs=rhs, start=start, stop=stop,
                                              skip_group_check=skip), bl(rd), bl(wr), cost=c, lat=c + HOP_NS)

    def TR(out, in_, idn, rd, wr):
        return P.add("pe", lambda e: e.transpose(out, in_, idn), bl(rd), bl(wr), cost=120.0, lat=120.0 + HOP_NS)

    def ACT(out, in_, func, rd, wr, bias=None, scale=None, accum=None):
        kw = {}
        if bias is not None:
            kw["bias"] = bias
        if scale is not None:
            kw["scale"] = scale
        if accum is not None:
            kw["accum_out"] = accum
        c = ecost("act", fsz(in_)) + (60.0 if accum is not None else 0.0)
        return P.add("act", lambda e: e.activation(out=out, in_=in_, func=func, **kw), bl(rd), bl(wr), cost=c)

    def AMUL(out, in_, m, rd, wr):
        return P.add("act", lambda e: e.mul(out=out, in_=in_, mul=m), bl(rd), bl(wr), cost=ecost("act", fsz(in_)))

    def CP(eng, out, in_, rd, wr):
        c = ecost(eng, fsz(in_))
        if hasattr(tp, "rotate") and any(x is tp for x in rd):
            rd = [x.b if x is tp else x for x in rd]
            tp.rotate()
        if eng == "act":
            return P.add("act", lambda e: e.copy(out=out, in_=in_), bl(rd), bl(wr), cost=c)
        return P.add(eng, lambda e: e.tensor_copy(out, in_), bl(rd), bl(wr), cost=c)

    def TS(eng, out, in0, s1, s2, op0, op1, rd, wr):
        c = ecost(eng, fsz(in0))
        if eng == "act":
            assert op0 == ALU.mult and op1 is None
            return P.add("act", lambda e: e.activation(out=out, in_=in0, func=AF.Copy, scale=s1), bl(rd), bl(wr), cost=c)
        if op1 is None:
            return P.add(eng, lambda e: e.tensor_scalar(out, in0, s1, None, op0), bl(rd), bl(wr), cost=c)
        return P.add(eng, lambda e: e.tensor_scalar(out, in0, s1, s2, op0, op1), bl(rd), bl(wr), cost=c)

    def TT(eng, out, in0, in1, op, rd, wr):
        return P.add(eng, lambda e: e.tensor_tensor(out, in0, in1, op), bl(rd), bl(wr), cost=ecost(eng, fsz(in0)))

    def STT(eng, out, in0, scalar, in1, op0, op1, rd, wr):
        return P.add(eng, lambda e: e.scalar_tensor_tensor(out, in0, scalar, in1, op0, op1), bl(rd), bl(wr),
                     cost=ecost(eng, fsz(in0)))

    def RECIP(out, in_, rd, wr):
        return P.add("dve", lambda e: e.reciprocal(out, in_), bl(rd), bl(wr), cost=ecost("dve", fsz(in_)) * 2.0)

    def MEMSET(eng, ap, val, wr):
        return P.add(eng, lambda e: e.memset(ap, val), (), bl(wr), cost=ecost(eng, fsz(ap)) * 0.5)

    import os as _os9
    STORE_Q = _os9.environ.get("STOREQ", "sp")

    def DMA(q, out, in_, rd, wr):
        nbytes = fsz(out) * out.shape[0] * (4 if out.dtype == F32 else 2)
        is_store = "DRam" in type(out.tensor).__name__
        if is_store:
            q = STORE_Q
        return P.add(q, lambda e: e.dma_start(out=out, in_=in_), bl(rd), bl(wr), dma=True,
                     cost=(600.0 if q == "pool" else 120.0), lat=2000.0 + nbytes / 100.0)

    def SQ(dst_ap, dst_tl, in_, rd, sa, accum):
        ACT(dst_ap, in_, AF.Square, rd, [dst_tl, sa], accum=accum)

    ident = C["ident"]
    _alt = [0]

    def alt2():
        _alt[0] += 1
        return "act" if _alt[0] % 2 else "dve"

    def rstd_from_ss(ssap, n, nt, dst, tl):
        ACT(dst, ssap, AF.Ln, [tl, C_eps], [tl], bias=C_eps.t[0:nt, 0:1], scale=1.0 / n)
        ACT(dst, dst, AF.Exp, [tl], [tl], scale=-0.5)

    Win = AR.alloc("Win", [128, 8, 3088]); Wout = AR.alloc("Wout", [128, 8, 1024])
    A_FIX = AR.off
    kTs = [AR.alloc("kTs%d" % b, [128, PAST]) for b in range(4)]
    Vs = [AR.alloc("Vs%d" % b, [128, NKT, 129]) for b in range(4)]
    alloc_common(True)
    wring("cst", 3, [128, 8, 128], F32)
    wring("kcs", 2, [128, 8, 128])
    S32s = [[AR.alloc("S32s%d_%d" % (b, p), [128, 128], F32) for p in range(2)] for b in range(4)]
    Sbfs = [[AR.alloc("Sbfs%d_%d" % (b, p), [128, 128]) for p in range(2)] for b in range(4)]
    wtile("qm", [128, 4, 4, 64]); wtile("khm", [64, 4, 256])
    wtile("qTs", [128, 2, 4, 64]); wtile("kTn", [128, 4, 64]); wtile("Vsn", [64, 4, 129])
    wring("PTs", 2, [128, 4, 2, 64])
    wtile("PTn", [64, 128]); wtile("sctmp", [64, 128], F32)
    wtile("lam_in", [128, 256], F32); wtile("lam_tmp", [128, 128], F32)
    wtile("nbias", [64, 512], F32)
    C["nbias"] = W["nbias"]

    MEMSET("dve", C_eps.t[:, :], EPS, [C_eps]); MEMSET("dve", C_one.t[:, :], 1.0, [C_one])
    for k in C:
        DMA("sp", C[k].t[:, :], cd[k], [], [C[k]])
    DMA("sp", gpre.t[:, :], gpre_d, [], [gpre]); DMA("sp", gpf.t[:, :], gpf_d, [], [gpf])
    DMA("sp", ggla.t[:, :], ggla_d, [], [ggla]); DMA("sp", gpm.t[:, :], gpm_d, [], [gpm])
    lam_in = W["lam_in"]; lam_tmp = W["lam_tmp"]
    DMA("sp", lam_in.t[:, :], lam_d, [], [lam_in])

    def stage_f32(src_ap, ncols):
        c_ = W["cst"].next()
        ap = c_.t[:, :, :].rearrange("p a b -> p (a b)")[:, 0:ncols]
        DMA("sp", ap, src_ap, [], [c_])
        return c_, ap

    c_ = W["cst"].next()
    ap17 = c_.t[0:17, 0:2, :].rearrange("p a b -> p (a b)")
    DMA("sp", ap17, wgu_d, [], [c_])
    CP("dve", wg_aug.t[:, :], ap17, [c_], [wg_aug])
    c_, ap = stage_f32(gsub_d, 128)
    TS("dve", gsub8.t[:, :], ap, 1.0 - LAM_INIT, None, ALU.mult, None, [c_], [gsub8])
    TT("dve", lam_tmp.t[:, 0:64], lam_in.t[:, 0:64], lam_in.t[:, 64:128], ALU.mult, [lam_in], [lam_tmp])
    TT("dve", lam_tmp.t[:, 64:128], lam_in.t[:, 128:192], lam_in.t[:, 192:256], ALU.mult, [lam_in], [lam_tmp])
    P.add("dve", lambda e: e.reduce_sum(lam_s.t[:, 0:2], lam_tmp.t[:, :].rearrange("p (a b) -> p a b", a=2), AX.X),
          bl([lam_tmp]), bl([lam_s]))
    ACT(lam_s.t[:, 2:4], lam_s.t[:, 0:2], AF.Exp, [lam_s], [lam_s])
    TT("dve", neg_lam.t[:, :], lam_s.t[:, 3:4], lam_s.t[:, 2:3], ALU.subtract, [lam_s], [neg_lam])
    TS("dve", neg_lam.t[:, :], neg_lam.t[:, :], -LAM_INIT, None, ALU.add, None, [neg_lam], [neg_lam])
    k_ = 0
    for kc in range(8):
        for c0 in range(0, 3088, 1024):
            c1 = min(3088, c0 + 1024)
            c_, ap = stage_f32(win_d[:, kc * 3088 + c0:kc * 3088 + c1], c1 - c0)
            e_ = ("dve", "act")[k_ % 2]; k_ += 1
            TS(e_, Win.t[:, kc, c0:c1], ap, gpre.t[:, kc:kc + 1], None, ALU.mult, None, [c_, gpre], [Win])
    for kc in range(8):
        c_, ap = stage_f32(wout_d[:, kc * 1024:(kc + 1) * 1024], 1024)
        CP(("act", "dve")[kc % 2], Wout.t[:, kc, :], ap, [c_], [Wout])
    MEMSET("pool", agT.t[:, :], 1.0, [agT])
    for p in range(2):
        MEMSET("pool", S32[p].t[:, :], 0.0, [S32[p]])
        for v in range(2):
            MEMSET("pool", Sbf[p].tiles[v].t[:, :], 0.0, [Sbf[p].tiles[v]])
    MEMSET("pool", W["qm"].t[:, :, :, :], 0.0, [W["qm"]])
    MEMSET("pool", W["qTs"].t[:, :, :, :], 0.0, [W["qTs"]])
    for v in range(2):
        MEMSET("pool", W["PTs"].tiles[v].t[:, :, :, :], 0.0, [W["PTs"].tiles[v]])

    def early():
        P.finish()
        P.emit()
        return nc

    if stop == "prep":
        return early()

    def stage_AD(x_rows, nt, st):
        xi = W["xin"].next()
        DMA("sp", xi.t[0:nt, :], x_rows, [], [xi])
        sa = stat.next()
        xn_ = W["b1k"].next()
        SQ(xn_.t[0:nt, :], xn_, xi.t[0:nt, :], [xi], sa, sa.t[0:nt, 0:1])
        rstd_from_ss(sa.t[0:nt, 0:1], 1024.0, nt, sa.t[0:nt, 1:2], sa)
        TS("dve", xn_.t[0:nt, :], xi.t[0:nt, :], sa.t[0:nt, 1:2], None, ALU.mult, None, [xi, sa], [xn_])
        for kc in range(8):
            TR(tp.t[:, kc, 0:nt], xn_.t[0:nt, kc * 128:(kc + 1) * 128], ident.t[0:nt, 0:nt], [xn_, ident], [tp])
        h_ = W["b1k"].next()
        h3 = h_.t[:, :].rearrange("p (k n) -> p k n", k=8)
        CP("dve", h3[:, :, 0:nt], tp.t[:, :, 0:nt], [tp], [h_])

        def proj(c0, c1):
            z = zp.next()
            for kc in range(8):
                MM(z.t[0:nt, 0:c1 - c0], h3[:, kc, 0:nt], Win.t[:, kc, c0:c1], kc == 0, kc == 7, [h_, Win], [z])
            return z

        z = proj(0, 16)
        ag = W["ag_bf"].next()
        CP("dve", ag.t[0:nt, :], z.t[0:nt, 0:16], [z], [ag])
        TR(tp.t[0:16, 0, 0:nt], ag.t[0:nt, 0:16], ident.t[0:nt, 0:nt], [ag, ident], [tp])
        CP("act", agT.t[0:16, 0:nt], tp.t[0:16, 0, 0:nt], [tp], [agT])
        z = zp.next()
        MM(z.t[0:nt, 0:256], agT.t[0:17, 0:nt], wg_aug.t[0:17, 0:256], True, True, [agT, wg_aug], [z])
        L = W["Lg"].next()
        ACT(L.t[0:nt, :], z.t[0:nt, 0:256], AF.Exp, [z], [L], scale=-1.0)
        ACT(L.t[0:nt, :], L.t[0:nt, :], AF.Ln, [L, C_one], [L], bias=C_one.t[0:nt, 0:1])
        st["L"] = L
        z = proj(16, 528)
        zq = W["zqk"].next()
        CP("dve", zq.t[0:nt, :], z.t[0:nt, :], [z], [zq])
        st["zqk"] = zq
        z = proj(528, 1040)
        v_ = W["vg"].next()
        CP("dve", v_.t[0:nt, :], z.t[0:nt, :], [z], [v_])
        st["vg"] = v_
        z = proj(1040, 1552)
        E_ = W["Eg"].next(); r_ = W["rr"].next()
        ACT(E_.t[0:nt, :], z.t[0:nt, :], AF.Exp, [z], [E_], scale=-1.0)
        TT("dve", r_.t[0:nt, :], z.t[0:nt, :], ggla.t[0:nt, :], ALU.mult, [z, ggla], [r_])
        TS("dve", E_.t[0:nt, :], E_.t[0:nt, :], 1.0, None, ALU.add, None, [E_], [E_])
        RECIP(E_.t[0:nt, :], E_.t[0:nt, :], [E_], [E_])
        TT("dve", r_.t[0:nt, :], r_.t[0:nt, :], E_.t[0:nt, :], ALU.mult, [r_, E_], [r_])
        st["G2"] = r_
        z = proj(1552, 2064)
        qd = W["qd_bf"].next()
        TS("dve", qd.t[0:nt, :], z.t[0:nt, :], 0.125, None, ALU.mult, None, [z], [qd])
        z = proj(2064, 2576)
        kf = W["kd_f"].next(); kb = W["kd_bf"].next()
        CP("act", kf.t[0:nt, :], z.t[0:nt, :], [z], [kf])
        CP("dve", kb.t[0:nt, :], kf.t[0:nt, :], [kf], [kb])
        z = proj(2576, 3088)
        vf = W["vd_f"].next()
        CP("act", vf.t[0:nt, :], z.t[0:nt, :], [z], [vf])
        st["qd"] = qd; st["kf"] = kf; st["kb"] = kb; st["vf"] = vf

    def stage_gla_gates(nt, smp, st):
        L = st["L"]; zq = st["zqk"]
        z = zp.next()
        tle = C["triLE_s"] if smp else C["triLE"]; tgt = C["triGT_s"] if smp else C["triGT"]
        ind = C["ind_s"] if smp else C["ind"]
        ni = 4 if smp else 2
        gbL = zp.next()
        MM(z.t[0:nt, 0:256], tle.t[0:nt, 0:nt], L.t[0:nt, :], True, True, [tle, L], [z])
        MM(z.t[0:nt, 256:512], tgt.t[0:nt, 0:nt], L.t[0:nt, :], True, True, [tgt, L], [z])
        for p in range(2):
            MM(gbL.t[:, p * ni:(p + 1) * ni], L.t[0:nt, p * 128:(p + 1) * 128], ind.t[0:nt, 0:ni], True, True,
               [L, ind], [gbL])
        ebL_ = W["ebL"].next()
        q_ = W["qt"].next(); k_ = W["kt"].next(); kh = W["kh"].next()
        eb_ = W["ex"].next()
        ACT(eb_.t[0:nt, :], z.t[0:nt, 0:256], AF.Exp, [z], [eb_])
        STT("dve", q_.t[0:nt, :], zq.t[0:nt, 0:256], 0.125, eb_.t[0:nt, :], ALU.mult, ALU.mult, [zq, eb_], [q_])
        enb_ = W["ex"].next()
        ACT(enb_.t[0:nt, :], z.t[0:nt, 0:256], AF.Exp, [z], [enb_], scale=-1.0)
        TT("dve", k_.t[0:nt, :], zq.t[0:nt, 256:512], enb_.t[0:nt, :], ALU.mult, [zq, enb_], [k_])
        ec_ = W["ex"].next()
        ACT(ec_.t[0:nt, :], z.t[0:nt, 256:512], AF.Exp, [z], [ec_])
        ACT(ebL_.t[:, 0:2 * ni], gbL.t[:, 0:2 * ni], AF.Exp, [gbL], [ebL_])
        if smp:
            TT("pool", kh.t[0:nt, 0, :], zq.t[0:nt, 256:512], ec_.t[0:nt, :], ALU.mult, [zq, ec_], [kh])
        else:
            for c in range(2):
                STT("dve", kh.t[:, c, :], zq.t[:, 256:512], C["cm"].t[:, c:c + 1], ec_.t[:, :],
                    ALU.mult, ALU.mult, [zq, ec_, C["cm"]], [kh])
        for p in range(2):
            TR(tp.t[:, p, 0:nt], q_.t[0:nt, p * 128:(p + 1) * 128], ident.t[0:nt, 0:nt], [q_, ident], [tp])
            TR(tp.t[:, 2 + p, 0:nt], k_.t[0:nt, p * 128:(p + 1) * 128], ident.t[0:nt, 0:nt], [k_, ident], [tp])
        qz = W["qz"].next(); kTg = W["kTg"].next()
        qz4 = qz.t[:, :, :].rearrange("p (a b) t -> p a b t", b=2)
        for hp in range(2):
            CP("act", qz4[64 * hp:64 * hp + 64, :, hp, 0:nt], tp.t[64 * hp:64 * hp + 64, 0:2, 0:nt], [tp], [qz])
        CP("dve", kTg.t[:, :, 0:nt], tp.t[:, 2:4, 0:nt], [tp], [kTg])
        st["qz"] = qz; st["kTg"] = kTg; st["kh"] = kh; st["ebL"] = ebL_

    def gla_intra(nt, st, maskA):
        qz = st["qz"]; kTg = st["kTg"]; v_ = st["vg"]
        gAT_ = zp.next() if NZP == 3 else gAT
        for h in range(4):
            MM(gAT_.t[0:nt, h * 128:h * 128 + nt], kTg.t[:, h // 2, 0:nt], qz.t[:, h, 0:nt],
               True, True, [qz, kTg], [gAT_])
        ats = []
        for h in range(4):
            a_ = W["ATm"].next()
            TT("dve", a_.t[0:nt, 0:nt], gAT_.t[0:nt, h * 128:h * 128 + nt], maskA.t[0:nt, 0:nt], ALU.mult,
               [gAT_, maskA], [a_])
            ats.append(a_)
        for h in range(4):
            MM(og.t[0:nt, h * 128:(h + 1) * 128], ats[h].t[0:nt, 0:nt], v_.t[0:nt, h * 128:(h + 1) * 128],
               h == 0, False, [ats[h], v_], [og])

    def gla_out_norm(nt, st, oc):
        sa = stat.next()
        for h in range(4):
            SQ(oc.t[0:nt, h * 128:(h + 1) * 128], oc, og.t[0:nt, h * 128:(h + 1) * 128], [og], sa, sa.t[0:nt, h:h + 1])
        rstd_from_ss(sa.t[0:nt, 0:4], 128.0, nt, sa.t[0:nt, 4:8], sa)
        G2 = st["G2"]
        for h in range(4):
            STT("dve", oc.t[0:nt, h * 128:(h + 1) * 128], og.t[0:nt, h * 128:(h + 1) * 128], sa.t[0:nt, 4 + h:5 + h],
                G2.t[0:nt, h * 128:(h + 1) * 128], ALU.mult, ALU.mult, [og, sa, G2], [oc])

    def stage_gla_prompt(st, oc):
        nt = 128
        qz = st["qz"]; kh = st["kh"]; v_ = st["vg"]; ebL_ = st["ebL"]
        for c in range(2):
            for h in range(4):
                p, r0 = h // 2, 64 * (h % 2)
                MM(gU.t[r0:r0 + 64, (p * 2 + c) * 128:(p * 2 + c + 1) * 128],
                   kh.t[:, c, h * 64:(h + 1) * 64], v_.t[:, h * 128:(h + 1) * 128],
                   True, True, [kh, v_], [gU])
        gla_intra(nt, st, C["maskA"])
        for c in range(2):
            cur = [Sbf[p].tiles[Sbf[p].i % 2] for p in range(2)]
            for h in range(4):
                p, r0 = h // 2, 64 * (h % 2)
                MM(og.t[64 * c:64 * c + 64, h * 128:(h + 1) * 128], qz.t[:, h, 64 * c:64 * c + 64],
                   cur[p].t[:, :], False, (c == 1), [qz, cur[p]], [og])
            for p in range(2):
                Sbf[p].i += 1
                nxt = Sbf[p].tiles[Sbf[p].i % 2]
                STT("dve", S32[p].t[:, :], S32[p].t[:, :], ebL_.t[:, 2 * p + c:2 * p + c + 1],
                    gU.t[:, (p * 2 + c) * 128:(p * 2 + c + 1) * 128], ALU.mult, ALU.add, [S32[p], ebL_, gU], [S32[p]])
                CP("dve", nxt.t[:, :], S32[p].t[:, :], [S32[p]], [nxt])
        gla_out_norm(nt, st, oc)

    def attn_epilogue(nt, bank, oc, hh, t_ap, o_ap, t_tl, o_tl):
        a1 = bank.t[0:nt, 0:129]; a2 = bank.t[0:nt, 129:258]
        sums = bank.t[0:nt, 0:258].rearrange("p (s n) -> p s n", s=2)[:, :, 128:129]
        sa = stat.next()
        P.add("dve", lambda e: e.reciprocal(sa.t[0:nt, 0:2].rearrange("p (s n) -> p s n", s=2), sums), bl([bank]), bl([sa]),
              cost=150.0)
        TT("dve", sa.t[0:nt, 2:3], sa.t[0:nt, 1:2], neg_lam.t[0:nt, 0:1], ALU.mult, [sa, neg_lam], [sa])
        TS("dve", t_ap, a1[:, 0:128], sa.t[0:nt, 0:1], None, ALU.mult, None, [bank, sa], [t_tl])
        STT("dve", o_ap, a2[:, 0:128], sa.t[0:nt, 2:3], t_ap, ALU.mult, ALU.add, [bank, sa, t_tl], [o_tl])
        SQ(oc.t[0:nt, 512 + hh * 128:512 + (hh + 1) * 128], oc, o_ap, [o_tl], sa, sa.t[0:nt, 3:4])
        rstd_from_ss(sa.t[0:nt, 3:4], 128.0, nt, sa.t[0:nt, 4:5], sa)
        STT("dve", oc.t[0:nt, 512 + hh * 128:512 + (hh + 1) * 128], o_ap, sa.t[0:nt, 4:5],
            gsub8.t[0:nt, :], ALU.mult, ALU.mult, [o_tl, sa, gsub8], [oc])

    def stage_wout(nt, oc, x_rows, srow, sbuf_i):
        for kc in range(8):
            TR(tp.t[:, kc, 0:nt], oc.t[0:nt, kc * 128:(kc + 1) * 128], ident.t[0:nt, 0:nt], [oc, ident], [tp])
        o_ = W["b1k"].next()
        o3 = o_.t[:, :].rearrange("p (k n) -> p k n", k=8)
        CP("dve", o3[:, :, 0:nt], tp.t[:, :, 0:nt], [tp], [o_])
        xr = W["xre"].next()
        DMA("sp", xr.t[0:nt, :], x_rows, [], [xr])
        ys_ = []
        for n in range(2):
            z = zp.next()
            for kc in range(8):
                MM(z.t[0:nt, :], o3[:, kc, 0:nt], Wout.t[:, kc, n * 512:(n + 1) * 512], kc == 0, kc == 7, [o_, Wout], [z])
            ys_.append(z)
        sa = stat.next()
        yt = W["ytmp"].next()
        for n in range(2):
            SQ(yt.t[0:nt, n * 512:(n + 1) * 512], yt, ys_[n].t[0:nt, :], [ys_[n]], sa, sa.t[0:nt, n:n + 1])
        TT("dve", sa.t[0:nt, 2:3], sa.t[0:nt, 0:1], sa.t[0:nt, 1:2], ALU.add, [sa], [sa])
        rstd_from_ss(sa.t[0:nt, 2:3], 1024.0, nt, sa.t[0:nt, 3:4], sa)
        for n in range(2):
            STT("dve", yt.t[0:nt, n * 512:(n + 1) * 512], ys_[n].t[0:nt, :], sa.t[0:nt, 3:4],
                gpm.t[0:nt, n * 512:(n + 1) * 512], ALU.mult, ALU.mult, [ys_[n], sa, gpm], [yt])
        TT("dve", yt.t[0:nt, :], yt.t[0:nt, :], xr.t[0:nt, :], ALU.add, [yt, xr], [yt])
        DMA("sp", x1s[srow:srow + nt, :], yt.t[0:nt, :], [yt], [x1sb[sbuf_i]])

    def sample_phase():
        nt = 64
        st = {}
        stage_AD(xs, nt, st)
        qd = st["qd"]; kb = st["kb"]; kf = st["kf"]; vf = st["vf"]
        qTs = W["qTs"]; kTn = W["kTn"]; Vsn = W["Vsn"]; qm = W["qm"]; khm = W["khm"]
        DMA("sp", ksm_d, kf.t[0:nt, :], [kf], [])
        DMA("sp", vsm_d, vf.t[0:nt, :], [vf], [])
        if stop == "s1":
            return
        for hh in range(4):
            TR(tp.t[:, hh, 0:nt], qd.t[0:nt, hh * 128:(hh + 1) * 128], ident.t[0:nt, 0:nt], [qd, ident], [tp])
            TR(tp.t[:, 4 + hh, 0:nt], kb.t[0:nt, hh * 128:(hh + 1) * 128], ident.t[0:nt, 0:nt], [kb, ident], [tp])
        for s_ in range(2):
            CP("act", qTs.t[64 * s_:64 * s_ + 64, s_, :, :], tp.t[64 * s_:64 * s_ + 64, 0:4, 0:nt], [tp], [qTs])
        CP("dve", kTn.t[:, :, :], tp.t[:, 4:8, 0:nt], [tp], [kTn])
        MEMSET("pool", Vsn.t[:, :, 128:129], 1.0, [Vsn])
        CP("pool", Vsn.t[:, :, 0:128], vf.t[0:nt, :].rearrange("p (h n) -> p h n", h=4), [vf], [Vsn])
        if stop == "s1b":
            return
        stage_gla_gates(nt, True, st)
        if stop == "s2":
            return
        qz = st["qz"]; kh = st["kh"]; v_ = st["vg"]; ebL_ = st["ebL"]
        for b in range(4):
            for p in range(2):
                DMA("sp", S32s[b][p].t[:, :], sg[b, p], [], [S32s[b][p]])
                CP("pool", Sbfs[b][p].t[:, :], S32s[b][p].t[:, :], [S32s[b][p]], [Sbfs[b][p]])
        for h in range(4):
            for b in range(4):
                CP("pool", qm.t[:, h, b, 16 * b:16 * b + 16], qz.t[:, h, 16 * b:16 * b + 16], [qz], [qm])
        for b in range(4):
            TS("dve", khm.t[0:nt, b, :], kh.t[0:nt, 0, :], C["rm_s"].t[0:nt, b:b + 1], None, ALU.mult, None,
               [kh, C["rm_s"]], [khm])
        gla_intra(nt, st, C["maskA_s"])
        for h in range(4):
            p, r0 = h // 2, 64 * (h % 2)
            for b in range(4):
                MM(og.t[0:nt, h * 128:(h + 1) * 128], qm.t[:, h, b, :], Sbfs[b][p].t[:, :],
                   False, b == 3, [qm, Sbfs[b][p]], [og])
        oc = W["ocat"].next()
        gla_out_norm(nt, st, oc)
        if stop == "s3":
            return
        for b in range(4):
            for p in range(2):
                for hp in range(2):
                    h = 2 * p + hp
                    r0 = 64 * hp
                    MM(gU.t[r0:r0 + 64, p * 128:(p + 1) * 128], khm.t[0:nt, b, h * 64:(h + 1) * 64],
                       v_.t[0:nt, h * 128:(h + 1) * 128], True, True, [khm, v_], [gU])
            for p in range(2):
                STT("dve", S32s[b][p].t[:, :], S32s[b][p].t[:, :], ebL_.t[:, 4 * p + b:4 * p + b + 1],
                    gU.t[:, p * 128:(p + 1) * 128], ALU.mult, ALU.add, [S32s[b][p], ebL_, gU], [S32s[b][p]])
                DMA("sp", gs_d[b, p], S32s[b][p].t[:, :], [S32s[b][p]], [])
        if stop == "s4":
            return
        for hh in range(4):
            if stop == "s5" and hh == 1:
                return
            for b in range(4):
                for ch in range(NKT // 8):
                    c_ = W["cst"].next()
                    DMA("sp", c_.t[:, :, :], ck[b, ch * 1024:(ch + 1) * 1024, hh * 128:(hh + 1) * 128]
                        .rearrange("(k p) n -> p k n", p=128), [], [c_])
                    kc_ = W["kcs"].next()
                    CP("act", kc_.t[:, :, :], c_.t[:, :, :], [c_], [kc_])
                    tb = tp if ch % 2 == 0 else tpB
                    for k8 in range(8):
                        TR(tb.t[:, k8, :], kc_.t[:, k8, :], ident.t[:, :], [kc_, ident], [tb])
                    CP("dve", kTs[b].t[:, ch * 1024:(ch + 1) * 1024].rearrange("p (k n) -> p k n", k=8),
                       tb.t[:, :, :], [tb], [kTs[b]])
                    c_ = W["cst"].next()
                    DMA("sp", c_.t[:, :, :], cv[b, ch * 1024:(ch + 1) * 1024, hh * 128:(hh + 1) * 128]
                        .rearrange("(k p) n -> p k n", p=128), [], [c_])
                    CP("dve", Vs[b].t[:, ch * 8:(ch + 1) * 8, 0:128], c_.t[:, :, :], [c_], [Vs[b]])
                if hh == 0:
                    MEMSET("pool", Vs[b].t[:, :, 128:129], 1.0, [Vs[b]])
            acc = accb[hh % 2]
            first = True
            for kt in range(NKT):
                sc = zp.next()
                for b in range(4):
                    for s_ in range(2):
                        MM(sc.t[:, (b * 2 + s_) * 16:(b * 2 + s_ + 1) * 16],
                           kTs[b].t[:, kt * 128:(kt + 1) * 128],
                           qTs.t[:, s_, hh, 16 * b:16 * b + 16], True, True, [kTs[b], qTs], [sc])
                pt = W["PTs"].next()
                o_ap = bass.AP(AR.t, pt.off, [[NA, 128], [144, 4], [64, 2], [1, 16]])
                i_ap = sc.t[:, 0:128].rearrange("p (b s q) -> p b s q", b=4, s=2)
                ACT(o_ap, i_ap, AF.Exp, [sc, C["sbias"]], [pt],
                    bias=C["sbias"].t[:, hh * NKT + kt:hh * NKT + kt + 1])
                for b in range(4):
                    for s_ in range(2):
                        MM(acc.t[0:nt, s_ * 129:(s_ + 1) * 129], pt.t[:, b, s_, :], Vs[b].t[:, kt, :],
                           first, False, [pt, Vs[b]], [acc])
                        first = False
            sc = zp.next()
            for s_ in range(2):
                MM(sc.t[0:nt, s_ * 64:(s_ + 1) * 64], kTn.t[:, hh, :],
                   qTs.t[:, s_, hh, :], True, True, [kTn, qTs], [sc])
            sctmp = W["sctmp"]; PTn = W["PTn"]
            TT("dve", sctmp.t[:, :], sc.t[0:nt, 0:128], C["nbias"].t[:, hh * 128:(hh + 1) * 128], ALU.add,
               [sc, C["nbias"]], [sctmp])
            ACT(PTn.t[:, :], sctmp.t[:, :], AF.Exp, [sctmp], [PTn])
            for s_ in range(2):
                MM(acc.t[0:nt, s_ * 129:(s_ + 1) * 129], PTn.t[:, s_ * 64:(s_ + 1) * 64], Vsn.t[:, hh, :],
                   False, True, [PTn, Vsn], [acc])
            t_ = W["t1"].next(); o_ = W["od"].next()
            attn_epilogue(nt, acc, oc, hh, t_.t[0:nt, 0, :], o_.t[0:nt, 0, :], t_, o_)
        stage_wout(nt, oc, xs, T, NT)

    sample_phase()
    P.barrier()
    if stop in ("sample", "s1", "s1b", "s2", "s3", "s4", "s5"):
        return early()

    AR.off = A_FIX
    W.clear()
    kT2 = AR.alloc("kT2", [128, 4, T]).t
    Vaug = AR.alloc("Vaug", [128, NT, 4, 129]).t
    kTb = [Buf("kT%d" % g) for g in range(NG)]
    Vb = [Buf("V%d" % g) for g in range(NG)]
    RD_on[0] = True
    alloc_common()
    RD_on[0] = False
    wring("cvs", 1, [128, 512], F32); wring("cvo", 1, [128, 512])
    wring("qT2", 2, [128, 2, 4, QG])
    for t_ in W["qT2"].tiles:
        MEMSET("pool", t_.t[:, :, :, :], 0.0, [t_])
    wring("PT", 3, [128, 2, QG])
    import os as _os2
    if _os2.environ.get("KSIM"):
        print("phase P arena spare elems:", AR.n - AR.off)
    for ti_ in range(NT):
        MEMSET("pool", Vaug[:, ti_, :, 128:129], 1.0, [Vb[ti_ * 128 // QG]])

    def kv_store_prompt(g, ti, i, st, qa):
        qd = st["qd"]; kb = st["kb"]; kf = st["kf"]; vf = st["vf"]
        for hh in range(4):
            TR(tp.t[:, hh, :], qd.t[:, hh * 128:(hh + 1) * 128], ident.t[:, :], [qd, ident], [tp])
            TR(tp.t[:, 4 + hh, :], kb.t[:, hh * 128:(hh + 1) * 128], ident.t[:, :], [kb, ident], [tp])
        for s_ in range(2):
            CP("act", qa.t[64 * s_:64 * s_ + 64, s_, :, ti * 128:(ti + 1) * 128], tp.t[64 * s_:64 * s_ + 64, 0:4, :],
               [tp], [qa])
        CP("dve", kT2[:, :, i * 128:(i + 1) * 128], tp.t[:, 4:8, :], [tp], [kTb[g]])
        CP("dve", Vaug[:, i, :, 0:128], vf.t[:, :].rearrange("p (h n) -> p h n", h=4), [vf], [Vb[g]])
        DMA("sp", kp_d[i * 128:(i + 1) * 128, :], kf.t[:, :], [kf], [])
        DMA("sp", vp_d[i * 128:(i + 1) * 128, :], vf.t[:, :], [vf], [])

    NQT = QG // 128

    def attention_prompt(g, qa, ocs):
        nkt = NQT * (g + 1)
        btab = C["btab"]
        for hh in range(4):
            started = set()
            pend = []

            def pv(j, q0, pt):
                for s_ in range(2):
                    for qt in range(q0 // 128, NQT):
                        bk, sl = qt, s_
                        first = bk not in started
                        started.add(bk)
                        MM(accb[bk].t[:, sl * 129:(sl + 1) * 129], pt.t[:, s_, qt * 128:(qt + 1) * 128],
                           Vaug[:, j, hh, :], first, j == nkt - 1, [pt, Vb[j * 128 // QG]], [accb[bk]])

            for j in range(nkt):
                jj = j - NQT * g
                q0 = 128 * jj if jj >= 0 else 0
                sc = scr.next()
                sc3 = sc.t[:, :].rearrange("p (s q) -> p s q", s=2)
                for s_ in range(2):
                    MM(sc3[:, s_, q0:QG], kT2[:, hh, j * 128:(j + 1) * 128],
                       qa.t[:, s_, hh, q0:QG], True, jj < 0, [kTb[j * 128 // QG], qa], [sc])
                    if jj >= 0:
                        MM(sc3[:, s_, q0:q0 + 128], ident.t[:, :], C["corr"].t[:, hh * 128:(hh + 1) * 128],
                           False, True, [ident, C["corr"]], [sc])
                pt = W["PT"].next()
                bidx = NQT * g - j + 1
                ACT(pt.t[:, :, q0:QG], sc3[:, :, q0:QG], AF.Exp, [sc, btab], [pt],
                    bias=btab.t[:, hh * (NT + 2) + bidx:hh * (NT + 2) + bidx + 1])
                for a in pend:
                    pv(*a)
                pend = [(j, q0, pt)]
            for a in pend:
                pv(*a)
            t_ = W["t1"].next(); o_ = W["od"].next()
            for qt in range(NQT):
                attn_epilogue(128, accb[qt], ocs[qt], hh, t_.t[:, qt, :], o_.t[:, qt, :], t_, o_)

    conv_jobs = []
    for kc in range(8):
        for q8 in range(8):
            conv_jobs.append(("up", kc, kc * 4096 + q8 * 512))
    for c in range(32):
        for hf in range(2):
            conv_jobs.append(("dn", c, c * 1024 + hf * 512))
    conv_i = [0]
    wupbB = [Buf("wupbB%d" % kc) for kc in range(8)]
    NPRE = 6

    def conv_some(n):
        for _ in range(n):
            if conv_i[0] >= len(conv_jobs):
                return
            kind, kc, off = conv_jobs[conv_i[0]]
            e_ = ("dve", "act")[conv_i[0] % 2]
            conv_i[0] += 1
            s_ = W["cvs"].next(); o_ = W["cvo"].next()
            if kind == "up":
                DMA("sp", s_.t[:, :], wup_d[:, off:off + 512], [], [s_])
                TS(e_, o_.t[:, :], s_.t[:, :], gpf.t[:, kc:kc + 1], None, ALU.mult, None, [s_, gpf], [o_])
                DMA("sp", wupb[:, off:off + 512], o_.t[:, :], [o_], [wupbB[kc]])
            else:
                DMA("sp", s_.t[:, :], wdn_d[:, off:off + 512], [], [s_])
                CP(e_, o_.t[:, :], s_.t[:, :], [s_], [o_])
                DMA("sp", wdnb[:, off:off + 512], o_.t[:, :], [o_], [])

    per_group = (len(conv_jobs) + max(NG - 1, 1) - 1) // max(NG - 1, 1)
    for g in range(NG):
        qa = W["qT2"].next()
        ocs = []
        for ti in range(NQT):
            i = g * NQT + ti
            st = {}
            stage_AD(xp[i * 128:(i + 1) * 128, :], 128, st)
            kv_store_prompt(g, ti, i, st, qa)
            stage_gla_gates(128, False, st)
            oc = W["ocat"].next()
            stage_gla_prompt(st, oc)
            ocs.append(oc)
        attention_prompt(g, qa, ocs)
        for ti in range(NQT):
            i = g * NQT + ti
            stage_wout(128, ocs[ti], xp[i * 128:(i + 1) * 128, :], i * 128, i)
        if g < NG - 1 or NG == 1:
            conv_some(per_group)
    conv_some(len(conv_jobs))
    WupP = AR.ap[:, 0:32768].rearrange("p (k n) -> p k n", k=8)
    for kc in range(NPRE):
        for cb in range(4):
            DMA("sp", WupP[:, kc, cb * 1024:(cb + 1) * 1024],
                wupb[:, kc * 4096 + cb * 1024:kc * 4096 + (cb + 1) * 1024], [wupbB[kc]], [Win])
    for p in range(2):
        DMA("sp", gp_d[p], S32[p].t[:, :], [S32[p]], [])
    P.barrier()
    if stop == "pass1":
        return early()

    AR.off = 0
    W.clear()
    Wup = AR.alloc("Wup", [128, 8, 4096]); Wdn = AR.alloc("Wdn", [128, 32, 1024])
    WupB = [Buf("WupB%d" % i) for i in range(4)]
    WdnB = [Buf("WdnB%d" % i) for i in range(4)]
    wring("xn", 1, [128, 1024])
    wring("fT", 1, [128, 8, 512]); wring("upT", 1, [128, 32, 512])
    wring("xres", 2, [128, 1024], F32)
    wring("rstg", 1, [128, 512], F32)
    wring("x1r", 2, [128, 1024], F32); wring("ytmp", 1, [128, 1024], F32)
    gpo = Tl(gpm.t, "gpo")
    gpo.b = gpm.b
    DMA("sp", gpo.t[:, :], gpo_d, [], [gpo])
    for cb in range(4):
        for kc in range(NPRE, 8):
            DMA("sp", Wup.t[:, kc, cb * 1024:(cb + 1) * 1024], wupb[:, kc * 4096 + cb * 1024:kc * 4096 + (cb + 1) * 1024],
                [], [WupB[cb]])
    for cb in range(4):
        for c in range(cb * 8, cb * 8 + 8):
            DMA("sp", Wdn.t[:, c, :], wdnb[:, c * 1024:(c + 1) * 1024], [], [WdnB[cb]])
    upp = Ring([banks[i] for i in (3, 4, 5, 6)])

    def ffn_group(tiles):
        f_ = W["fT"].next(); u_ = W["upT"].next()
        col = 0
        cols = []
        for (nt, srow, out_ap, bi) in tiles:
            x1 = W["x1r"].next()
            DMA("sp", x1.t[0:nt, :], x1s[srow:srow + nt, :], [x1sb[bi]], [x1])
            sa = stat.next()
            xn_ = W["xn"].next()
            SQ(xn_.t[0:nt, :], xn_, x1.t[0:nt, :], [x1], sa, sa.t[0:nt, 0:1])
            rstd_from_ss(sa.t[0:nt, 0:1], 1024.0, nt, sa.t[0:nt, 1:2], sa)
            TS("dve", xn_.t[0:nt, :], x1.t[0:nt, :], sa.t[0:nt, 1:2], None, ALU.mult, None, [x1, sa], [xn_])
            for kc in range(8):
                TR(tp.t[:, kc, 0:nt], xn_.t[0:nt, kc * 128:(kc + 1) * 128], ident.t[0:nt, 0:nt], [xn_, ident], [tp])
            CP(alt2(), f_.t[:, :, col:col + nt], tp.t[:, :, 0:nt], [tp], [f_])
            cols.append(col)
            col += nt
        ntok = col
        for c in range(32):
            up = upp.next()
            for kc in range(8):
                MM(up.t[:, 0:ntok], Wup.t[:, kc, c * 128:(c + 1) * 128], f_.t[:, kc, 0:ntok], kc == 0, kc == 7,
                   [WupB[c // 8], f_], [up])
            r_ = W["rstg"].next()
            ACT(r_.t[:, 0:ntok], up.t[:, 0:ntok], AF.Relu, [up], [r_])
            TT("dve", u_.t[:, c, 0:ntok], r_.t[:, 0:ntok], r_.t[:, 0:ntok], ALU.mult, [r_], [u_])
        for k_, (nt, srow, out_ap, bi) in enumerate(tiles):
            c0 = cols[k_]
            x1 = W["xres"].next()
            DMA("sp", x1.t[0:nt, :], x1s[srow:srow + nt, :], [x1sb[bi]], [x1])
            ys_ = []
            for n in range(2):
                z = zp.next()
                for c in range(32):
                    MM(z.t[0:nt, :], u_.t[:, c, c0:c0 + nt], Wdn.t[:, c, n * 512:(n + 1) * 512], c == 0, c == 31,
                       [u_, WdnB[c // 8]], [z])
                ys_.append(z)
            sa = stat.next()
            yt = W["ytmp"].next()
            for n in range(2):
                SQ(yt.t[0:nt, n * 512:(n + 1) * 512], yt, ys_[n].t[0:nt, :], [ys_[n]], sa, sa.t[0:nt, n:n + 1])
            TT("dve", sa.t[0:nt, 2:3], sa.t[0:nt, 0:1], sa.t[0:nt, 1:2], ALU.add, [sa], [sa])
            rstd_from_ss(sa.t[0:nt, 2:3], 1024.0, nt, sa.t[0:nt, 3:4], sa)
            for n in range(2):
                STT("dve", yt.t[0:nt, n * 512:(n + 1) * 512], ys_[n].t[0:nt, :], sa.t[0:nt, 3:4],
                    gpo.t[0:nt, n * 512:(n + 1) * 512], ALU.mult, ALU.mult, [ys_[n], sa, gpo], [yt])
            TT("dve", yt.t[0:nt, :], yt.t[0:nt, :], x1.t[0:nt, :], ALU.add, [yt, x1], [yt])
            DMA("sp", out_ap, yt.t[0:nt, :], [yt], [])

    for i in range(0, NT, 4):
        ffn_group([(128, (i + k) * 128, yp_d[(i + k) * 128:(i + k + 1) * 128, :], i + k) for k in range(4)])
    ffn_group([(64, T, ys_d, NT)])
    P.finish()
    P.emit()
    return nc


def _core_inputs(c, T, PAST, I, consts):
    f = np.float32
    w_in = np.asarray(I["w_in"][0], f)
    perm = np.concatenate([w_in[:, 1536:1552], w_in[:, 0:1536], w_in[:, 1552:]], axis=1)
    m = {}
    m["xp"] = np.ascontiguousarray(I["x_prompt"][c], f)
    m["xs"] = np.ascontiguousarray(I["x_sample"][4 * c:4 * c + 4], f).reshape(64, 1024)
    m["ck"] = np.ascontiguousarray(I["cache_k"][0, 4 * c:4 * c + 4], f).reshape(4, PAST, 512)
    m["cv"] = np.ascontiguousarray(I["cache_v"][0, 4 * c:4 * c + 4], f).reshape(4, PAST, 512)
    m["sg"] = np.ascontiguousarray(I["state_gla"][0, 4 * c:4 * c + 4], f).reshape(4, 2, 128, 128)
    m["win"] = np.ascontiguousarray(perm.reshape(8, 128, 3088).transpose(1, 0, 2)).reshape(128, 8 * 3088)
    m["wout"] = np.ascontiguousarray(np.asarray(I["w_out"][0], f).reshape(8, 128, 1024).transpose(1, 0, 2)).reshape(128, 8192)
    m["wup"] = np.ascontiguousarray(np.asarray(I["w_ff_up"][0], f).reshape(8, 128, 4096).transpose(1, 0, 2)).reshape(128, 8 * 4096)
    m["wdn"] = np.ascontiguousarray(np.asarray(I["w_ff_down"][0], f).reshape(32, 128, 1024).transpose(1, 0, 2)).reshape(128, 32 * 1024)
    m["wgu"] = np.ascontiguousarray(np.concatenate([np.asarray(I["w_gate_up"][0], f), np.asarray(I["b_gate"][0], f)[None]], 0))
    m["gpre"] = np.ascontiguousarray(np.asarray(I["g_pre_mix"][0], f).reshape(8, 128).T)
    m["gpf"] = np.ascontiguousarray(np.asarray(I["g_pre_ffn"][0], f).reshape(8, 128).T)
    m["ggla"] = np.ascontiguousarray(np.broadcast_to(np.tile(np.asarray(I["g_gla_out"][0], f), 4)[None], (128, 512)))
    m["gsub"] = np.ascontiguousarray(np.broadcast_to(np.asarray(I["g_subln"][0], f)[None], (128, 128)))
    m["gpm"] = np.ascontiguousarray(np.broadcast_to(np.asarray(I["g_post_mix"][0], f)[None], (128, 1024)))
    m["gpo"] = np.ascontiguousarray(np.broadcast_to(np.asarray(I["g_post_ffn"][0], f)[None], (128, 1024)))
    lam = np.concatenate([np.asarray(I[k][0], f) for k in ("lam_q1", "lam_k1", "lam_q2", "lam_k2")])
    m["lam"] = np.ascontiguousarray(np.broadcast_to(lam[None], (128, 256)))
    for k, v in consts.items():
        m["c_" + k] = v
    return m


_CACHE = {}


def run_cores(I, T, PAST, ncores):
    key = (T, PAST)
    if key not in _CACHE:
        _CACHE[key] = (build_program(T, PAST), make_consts(T, PAST))
    nc, consts = _CACHE[key]
    in_maps = [_core_inputs(c, T, PAST, I, consts) for c in range(ncores)]
    res = run_bass_kernel_spmd(nc, in_maps, core_ids=list(range(ncores)))
    return res.results


def kernel(**inputs):
    T, PAST, NCORES = 4096, 2048, 8
    R = run_cores(inputs, T, PAST, NCORES)
    f = np.float32
    yp = np.stack([np.asarray(r["yp"], f) for r in R], 0)
    ys = np.concatenate([np.asarray(r["ys"], f).reshape(4, 16, 1024) for r in R], 0)
    kp = np.stack([np.asarray(r["kp"], f).reshape(T, 4, 128) for r in R], 0)[None]
    vp = np.stack([np.asarray(r["vp"], f).reshape(T, 4, 128) for r in R], 0)[None]
    gp = np.stack([np.asarray(r["gp"], f).reshape(4, 64, 128) for r in R], 0)[None]
    ksm = np.concatenate([np.asarray(r["ksm"], f).reshape(4, 16, 4, 128) for r in R], 0)[None]
    vsm = np.concatenate([np.asarray(r["vsm"], f).reshape(4, 16, 4, 128) for r in R], 0)[None]
    gs = np.concatenate([np.asarray(r["gs"], f).reshape(4, 4, 64, 128) for r in R], 0)[None]
    return (yp, ys, kp, vp, gp, ksm, vsm, gs)
```

```python
import math
import numpy as np
import ml_dtypes
import concourse.bass as bass
import concourse.mybir as mybir
from concourse.bass_utils import run_bass_kernel_spmd

F32 = mybir.dt.float32
BF16 = mybir.dt.bfloat16
AF = mybir.ActivationFunctionType
ALU = mybir.AluOpType
AX = mybir.AxisListType

EPS = 1e-6
LAM_INIT = 0.8 - 0.6 * math.exp(-0.3 * 0)
NEG = -30000.0
import os as _os_hop
HOP_NS = float(_os_hop.environ.get("KHOP", "250"))
ENGS = ("pe", "act", "dve", "pool", "sp")
COMPUTE = ("pe", "act", "dve", "pool")


class Buf:
    __slots__ = ("name", "w", "rs", "rd", "excl")

    def __init__(self, name):
        self.name = name
        self.excl = False
        self.w = None
        self.rs = []
        self.rd = []


class Op:
    __slots__ = ("eng", "seq", "fn", "dma", "sig", "waits", "sem", "semval", "clk", "id", "deps", "cost", "lat", "phase", "ridx", "tag", "t0", "rdy", "blk", "prev")


class Prog:
    limit = None
    want_tags = False

    def __init__(self, nc, n_dma_sems=28):
        self.nc = nc
        self.pending = []
        self.final = {e: [] for e in ENGS}
        self.gorder = []
        self.nds = n_dma_sems
        self.nid = 0
        self.phase = 0
        self.sched = True

    def add(self, eng, fn, reads=(), writes=(), dma=False, extra_deps=(), cost=100.0, lat=None):
        if self.limit is not None and fn is not None and self.nid >= self.limit:
            return None
        op = Op()
        op.eng = eng; op.fn = fn; op.dma = dma; op.sig = False; op.waits = []
        op.id = self.nid; self.nid += 1
        op.cost = cost; op.lat = (cost + HOP_NS) if lat is None else lat
        op.phase = self.phase
        if self.want_tags:
            import sys as _s
            f = _s._getframe(2)
            op.tag = (f.f_lineno, f.f_back.f_lineno if f.f_back is not None else 0)
        deps = {}
        for b in reads:
            if b.w is not None:
                deps[b.w.id] = b.w
            if b.excl:
                for r in b.rs:
                    if r.eng != eng:
                        deps[r.id] = r
        for b in writes:
            if b.w is not None:
                deps[b.w.id] = b.w
            for r in b.rs:
                deps[r.id] = r
            for r in b.rd:
                deps[r.id] = r
        for p in extra_deps:
            deps[p.id] = p
        deps.pop(op.id, None)
        op.deps = list(deps.values())
        self.pending.append(op)
        for b in reads:
            if dma:
                b.rd.append(op)
            else:
                b.rs.append(op)
        for b in writes:
            b.w = op; b.rs = []; b.rd = []
        return op

    def schedule_phase(self):
        import heapq
        ops = self.pending
        self.pending = []
        n = len(ops)
        for i, op in enumerate(ops):
            op.ridx = i
        succ = [[] for _ in range(n)]
        nd = [0] * n
        for i, op in enumerate(ops):
            for p in op.deps:
                if p.phase == self.phase:
                    succ[p.ridx].append(i)
                    nd[i] += 1
        order = []
        if not self.sched:
            order = list(range(n))
        else:
            ready = {e: [] for e in ENGS}
            busy = {e: False for e in ENGS}
            ev = []
            cnt = [0]

            lastop = {e: None for e in ENGS}
            rdy_t = [0.0] * n
            blk = [None] * n

            def try_start(e, t):
                if busy[e] or not ready[e]:
                    return
                i = inv[heapq.heappop(ready[e])]
                op = ops[i]
                busy[e] = True
                op.t0 = t; op.rdy = rdy_t[i]; op.blk = blk[i]; op.prev = lastop[e]; lastop[e] = op
                order.append(i)
                cnt[0] += 1
                heapq.heappush(ev, (t + op.cost, 0, cnt[0], i))
                heapq.heappush(ev, (t + op.lat, 1, cnt[0], i))

            import os as _os8
            PRI = _os8.environ.get("KPRI", "idx")
            if PRI != "idx":
                bl_ = [0.0] * n
                for i in range(n - 1, -1, -1):
                    m = 0.0
                    for s in succ[i]:
                        if bl_[s] > m:
                            m = bl_[s]
                    bl_[i] = m + ops[i].lat
                W_ = float(PRI)
                key = [(-(bl_[i]) + W_ * i * 100.0) for i in range(n)]
                order_key = sorted(range(n), key=lambda i: key[i])
                rank = [0] * n
                for r_, i in enumerate(order_key):
                    rank[i] = r_
            else:
                rank = list(range(n))
            inv = {}
            for i in range(n):
                inv[rank[i]] = i
            for i in range(n):
                if nd[i] == 0:
                    heapq.heappush(ready[ops[i].eng], rank[i])
            for e in ENGS:
                try_start(e, 0.0)
            while ev:
                t, kind, _, i = heapq.heappop(ev)
                if kind == 0:
                    e = ops[i].eng
                    busy[e] = False
                    try_start(e, t)
                else:
                    for s in succ[i]:
                        nd[s] -= 1
                        rdy_t[s] = t; blk[s] = ops[i]
                        if nd[s] == 0:
                            e = ops[s].eng
                            heapq.heappush(ready[e], rank[s])
                            try_start(e, t)
            assert len(order) == n, (len(order), n)
            self.sim_time = t if n else 0.0
            import os as _os
            if _os.environ.get("KSIM"):
                busy_t = {e: 0.0 for e in ENGS}
                for op in ops:
                    busy_t[op.eng] += op.cost
                print("PHASE %d: sim %.1f us, n=%d, busy(us): %s" % (self.phase, self.sim_time / 1e3, n,
                      " ".join("%s=%.0f" % (e, busy_t[e] / 1e3) for e in ENGS)))
                if self.want_tags and n:
                    last = max(ops, key=lambda o: o.t0 + o.lat)
                    acc = {}
                    o = last
                    steps = 0
                    while o is not None and steps < 200000:
                        steps += 1
                        if o.t0 > o.rdy + 1e-6 and o.prev is not None:
                            nxt = o.prev
                            key = ("ENG", o.eng, o.tag)
                            dt = o.t0 - nxt.t0
                        elif o.blk is not None:
                            nxt = o.blk
                            key = ("DEP", o.eng, o.tag)
                            dt = o.t0 - nxt.t0
                        else:
                            break
                        acc[key] = acc.get(key, 0.0) + dt
                        o = nxt
                    idle = {}
                    prev_end = 0.0
                    for o2 in sorted([o3 for o3 in ops if o3.eng == "pe"], key=lambda o3: o3.t0):
                        gap = o2.t0 - prev_end
                        if gap > 1.0:
                            bt = (o2.blk.eng, o2.blk.tag) if o2.blk is not None else None
                            k2 = (o2.tag, bt)
                            idle[k2] = idle.get(k2, 0.0) + gap
                        prev_end = o2.t0 + o2.cost
                    print("  PE idle total %.1f us; top:" % (sum(idle.values()) / 1e3))
                    for k2, v in sorted(idle.items(), key=lambda kv: -kv[1])[:22]:
                        print("   %8.1f us  pe-op %s  blocked-by %s" % (v / 1e3, k2[0], k2[1]))
                    import os as _os7
                    if _os7.environ.get("KWIN") and self.phase == 1:
                        w0, w1 = [float(x) * 1e3 for x in _os7.environ["KWIN"].split(",")]
                        for o2 in sorted([o3 for o3 in ops if w0 <= o3.t0 <= w1 and o3.eng in ("pe",)], key=lambda o3: o3.t0):
                            print("   t=%9.2f dur=%6.2f %s %s blk=%s" % (o2.t0 / 1e3, o2.cost / 1e3, o2.eng, o2.tag,
                                  (o2.blk.eng, o2.blk.tag) if o2.blk is not None else None))
                    tot = sum(acc.values())
                    print("  critical path total %.1f us; top contributors:" % (tot / 1e3))
                    for k, v in sorted(acc.items(), key=lambda kv: -kv[1])[:28]:
                        print("   %8.1f us  %s" % (v / 1e3, k))
        for i in order:
            op = ops[i]
            self.final[op.eng].append(op)
            self.gorder.append(op)

    def _pseudo(self, eng, deps):
        op = Op()
        op.eng = eng; op.fn = None; op.dma = False; op.sig = False; op.waits = []
        op.id = self.nid; self.nid += 1
        op.deps = list(deps); op.phase = self.phase
        self.final[eng].append(op)
        self.gorder.append(op)

    def _phase_sinks(self):
        lasts = []
        for e in COMPUTE:
            for op in reversed(self.final[e]):
                if op.fn is not None and not op.dma:
                    if op.phase == self.phase:
                        lasts.append(op)
                    break
        dmas = [op for e in ENGS for op in self.final[e] if op.dma and op.phase == self.phase]
        return lasts + dmas

    def barrier(self):
        self.schedule_phase()
        sinks = self._phase_sinks()
        for e in ENGS:
            self._pseudo(e, sinks)
        self.phase += 1

    def finish(self):
        self.schedule_phase()
        sinks = self._phase_sinks()
        for e in ENGS:
            self._pseudo(e, sinks)

    def lower(self):
        clock = {e: {} for e in ENGS}
        dknown = {e: set() for e in ENGS}
        NQ = 12
        dma_cum = [0] * (self.nds + NQ)
        dma_last = [None] * (self.nds + NQ)
        rr = 0
        rr2 = 0
        seqc = {e: 0 for e in ENGS}
        for op in self.gorder:
            eng = op.eng
            op.seq = seqc[eng]; seqc[eng] += 1
            deps = list(op.deps)
            if op.dma:
                if eng == "pool":
                    s = self.nds + rr2
                    rr2 = (rr2 + 1) % NQ
                else:
                    s = rr
                    rr = (rr + 1) % self.nds
                if dma_last[s] is not None:
                    deps.append(dma_last[s])
                dma_cum[s] += 16
                op.sem = s; op.semval = dma_cum[s]
                dma_last[s] = op
            clk = clock[eng]
            dk = dknown[eng]
            for p in deps:
                if p.dma:
                    if p.id in dk:
                        continue
                    op.waits.append(p); dk.add(p.id)
                else:
                    if p.fn is None:
                        continue
                    if p.eng == eng and eng == "pe":
                        continue
                    if clk.get(p.eng, -1) >= p.seq:
                        continue
                    op.waits.append(p); p.sig = True
                    clk[p.eng] = p.seq
                for k, v in p.clk.items():
                    if clk.get(k, -1) < v:
                        clk[k] = v
            op.clk = dict(clk)

    def emit(self):
        nc = self.nc
        self.lower()
        csem = {e: nc.alloc_semaphore(name="c_" + e) for e in COMPUTE}
        dsem = [nc.alloc_semaphore(name="d_%d" % i) for i in range(self.nds + 12)]
        for e in COMPUTE:
            c = 0
            for op in self.final[e]:
                if op.sig:
                    c += 1
                    op.semval = c

        def run(e, eng):
            for op in self.final[e]:
                for p in op.waits:
                    if p.dma:
                        eng.wait_ge(dsem[p.sem], p.semval)
                    else:
                        eng.wait_ge(csem[p.eng], p.semval)
                if op.fn is None:
                    continue
                ins = op.fn(eng)
                if op.dma:
                    ins.then_inc(dsem[op.sem], 16)
                elif op.sig:
                    ins.then_inc(csem[e], 1)

        with nc.Block() as block:
            @block.tensor
            def _(eng):
                run("pe", eng)

            @block.scalar
            def _(eng):
                run("act", eng)

            @block.vector
            def _(eng):
                run("dve", eng)

            @block.gpsimd
            def _(eng):
                run("pool", eng)

            @block.sync
            def _(eng):
                run("sp", eng)


class Tl:
    __slots__ = ("t", "b", "off")

    def __init__(self, t, name):
        self.t = t
        self.b = Buf(name)


class Ring:
    def __init__(self, tiles):
        self.tiles = tiles
        self.i = 0

    def next(self):
        t = self.tiles[self.i % len(self.tiles)]
        self.i += 1
        return t


SLOPES = [2.0 ** (-2.0 * (h + 1)) for h in range(4)]
QG = 256


def make_consts(T, PAST):
    bf = ml_dtypes.bfloat16
    NT = T // 128
    NKT = PAST // 128
    c = {}
    c["ident"] = np.eye(128, dtype=np.float32).astype(bf)
    s = np.arange(128)
    same = (s[:, None] // 64) == (s[None, :] // 64)
    c["triLE"] = (np.where(same & (s[:, None] <= s[None, :]), -1.0 / 16, 0.0)).astype(np.float32)
    c["triGT"] = (np.where(same & (s[:, None] > s[None, :]), -1.0 / 16, 0.0)).astype(np.float32)
    c["ind"] = (np.where((s[:, None] // 64) == np.arange(2)[None, :], -1.0 / 16, 0.0)).astype(np.float32)
    c["maskA"] = np.where(same & (s[:, None] <= s[None, :]), 1.0, 0.0).astype(np.float32)
    c["cm"] = np.where((s[:, None] // 64) == np.arange(2)[None, :], 1.0, 0.0).astype(np.float32)
    u = np.arange(64)
    same16 = (u[:, None] // 16) == (u[None, :] // 16)
    c["triLE_s"] = np.where(same16 & (u[:, None] <= u[None, :]), -1.0 / 16, 0.0).astype(np.float32)
    c["triGT_s"] = np.where(same16 & (u[:, None] > u[None, :]), -1.0 / 16, 0.0).astype(np.float32)
    c["ind_s"] = np.where((u[:, None] // 16) == np.arange(4)[None, :], -1.0 / 16, 0.0).astype(np.float32)
    c["rm_s"] = np.where((u[:, None] // 16) == np.arange(4)[None, :], 1.0, 0.0).astype(np.float32)
    c["maskA_s"] = np.where(same16 & (u[:, None] <= u[None, :]), 1.0, 0.0).astype(np.float32)
    corr = np.zeros((128, 4, 128), np.float32)
    k = s[:, None]; q = s[None, :]
    kc = k // 64; qc = q // 64
    for h in range(4):
        m = np.zeros((128, 128), np.float32)
        m = np.where((kc == qc) & (k > q), -2.0 * SLOPES[h] * (k - q), m)
        m = np.where(kc > qc, NEG, m)
        corr[:, h, :] = m
    c["corr"] = corr.reshape(128, 512).astype(bf)
    bt = np.zeros((128, 4, NT + 2), np.float32)
    for h in range(4):
        for idx in range(NT + 2):
            bt[:, h, idx] = SLOPES[h] * (s - 128.0 * (idx - 1))
    c["btab"] = bt.reshape(128, 4 * (NT + 2))
    sbt = np.zeros((128, 4, NKT), np.float32)
    for h in range(4):
        for kt in range(NKT):
            sbt[:, h, kt] = SLOPES[h] * (kt * 128 + s - PAST)
    c["sbias"] = sbt.reshape(128, 4 * NKT)
    nb = np.zeros((64, 4, 2, 64), np.float32)
    kb = u[:, None] // 16; kj = u[:, None] % 16
    qb = u[None, :] // 16; qi = u[None, :] % 16
    for h in range(4):
        m = np.where(kb == qb, -SLOPES[h] * np.abs(qi - kj) + SLOPES[h] * qi, NEG)
        nb[:, h, 0, :] = m
        nb[:, h, 1, :] = m
    c["nbias"] = nb.reshape(64, 512)
    return c


CONST_DT = {"ident": BF16, "corr": BF16}


class Arena:
    def __init__(self, nc, n):
        self.t = nc.alloc_sbuf_tensor("arena", [128, n], BF16)
        self.ap = self.t[:, :]
        self.n = n
        self.off = 0

    def alloc(self, name, shape, dt=BF16):
        free = 1
        for d in shape[1:]:
            free *= d
        nel = free if dt == BF16 else free * 2
        nal = (nel + 31) // 32 * 32
        assert self.off + nal <= self.n, ("arena overflow", name, self.off, nal, self.n)
        ap = self.ap[0:shape[0], self.off:self.off + nel]
        if dt == F32:
            ap = ap.bitcast(F32)
        if len(shape) > 2:
            names = "abcdef"[:len(shape) - 1]
            pat = "p (%s) -> p %s" % (" ".join(names), " ".join(names))
            kw = {names[i]: shape[1 + i] for i in range(len(shape) - 2)}
            ap = ap.rearrange(pat, **kw)
        tl = Tl(ap, name)
        tl.off = self.off
        self.off += nal
        return tl


def build_program(T, PAST, stop=None):
    NT = T // 128
    NG = T // QG
    NKT = PAST // 128
    assert NKT % 8 == 0 and T % QG == 0
    nc = bass.Bass("TRN2", target_bir_lowering=False)
    P = Prog(nc)
    if stop is not None and stop.startswith("n"):
        P.limit = int(stop[1:])

    def din(name, shape, dt=F32):
        return nc.dram_tensor(name, list(shape), dt, kind="ExternalInput").ap()

    def dout(name, shape, dt=F32):
        return nc.dram_tensor(name, list(shape), dt, kind="ExternalOutput").ap()

    xp = din("xp", [T, 1024]); xs = din("xs", [64, 1024])
    ck = din("ck", [4, PAST, 512]); cv = din("cv", [4, PAST, 512]); sg = din("sg", [4, 2, 128, 128])
    win_d = din("win", [128, 8 * 3088]); wout_d = din("wout", [128, 8 * 1024])
    wup_d = din("wup", [128, 8 * 4096]); wdn_d = din("wdn", [128, 32 * 1024])
    wgu_d = din("wgu", [17, 256]); gpre_d = din("gpre", [128, 8]); gpf_d = din("gpf", [128, 8])
    ggla_d = din("ggla", [128, 512]); gsub_d = din("gsub", [128, 128])
    gpm_d = din("gpm", [128, 1024]); gpo_d = din("gpo", [128, 1024]); lam_d = din("lam", [128, 256])
    cshape = {"ident": [128, 128], "triLE": [128, 128], "triGT": [128, 128], "ind": [128, 2],
              "maskA": [128, 128], "cm": [128, 2], "triLE_s": [64, 64], "triGT_s": [64, 64], "ind_s": [64, 4],
              "rm_s": [64, 4], "maskA_s": [64, 64], "corr": [128, 512], "btab": [128, 4 * (NT + 2)],
              "sbias": [128, 4 * NKT], "nbias": [64, 512]}
    cd = {k: din("c_" + k, v, CONST_DT.get(k, F32)) for k, v in cshape.items()}
    yp_d = dout("yp", [T, 1024]); ys_d = dout("ys", [64, 1024])
    kp_d = dout("kp", [T, 512]); vp_d = dout("vp", [T, 512]); gp_d = dout("gp", [2, 128, 128])
    ksm_d = dout("ksm", [64, 512]); vsm_d = dout("vsm", [64, 512]); gs_d = dout("gs", [4, 2, 128, 128])
    x1s = nc.dram_tensor("x1s", [T + 64, 1024], F32).ap()
    wupb = nc.dram_tensor("wupb", [128, 8 * 4096], BF16).ap()
    wdnb = nc.dram_tensor("wdnb", [128, 32 * 1024], BF16).ap()
    x1sb = [Buf("x1s%d" % i) for i in range(NT + 1)]

    def sb(name, shape, dt=F32):
        return Tl(nc.alloc_sbuf_tensor("s_" + name, list(shape), dt), name)

    C = {k: sb("k_" + k, v, CONST_DT.get(k, F32)) for k, v in cshape.items() if k != "nbias"}
    wg_aug = sb("wg_aug", [17, 256], BF16)
    gpre = sb("gpre", [128, 8]); gpf = sb("gpf", [128, 8])
    ggla = sb("ggla", [128, 512]); gsub8 = sb("gsub8", [128, 128])
    gpm = sb("gpm", [128, 1024])
    neg_lam = sb("neg_lam", [128, 1]); lam_s = sb("lam_s", [128, 4])
    C_eps = sb("c_eps", [128, 1]); C_one = sb("c_one", [128, 1])
    stat = Ring([sb("stat%d" % i, [128, 8]) for i in range(24)])
    agT = sb("agT", [17, 128], BF16)
    S32 = [sb("S32_%d" % p, [128, 128]) for p in range(2)]
    Sbf = [Ring([sb("Sbf%d_%d" % (p, v), [128, 128], BF16) for v in range(2)]) for p in range(2)]

    import os as _os3
    NA = (nc.sbuf_bytes_remaining - 1024) // 2 // 32 * 32 + int(_os3.environ.get("FAKE_NA", "0"))
    RD = {}
    for kv in _os3.environ.get("RINGS", "").split(","):
        if "=" in kv:
            RD[kv.split("=")[0]] = int(kv.split("=")[1])
    AR = Arena(nc, NA)
    W = {}

    RD_on = [False]

    def wring(name, n, shape, dt=BF16):
        if RD_on[0]:
            n = RD.get(name, n)
        W[name] = Ring([AR.alloc("%s%d" % (name, i), shape, dt) for i in range(n)])

    def wtile(name, shape, dt=BF16):
        W[name] = AR.alloc(name, shape, dt)

    def alloc_common(smp=False):
        wring("xin", 1, [128, 1024], F32)
        wring("b1k", 3, [128, 1024])
        wring("zqk", 1, [128, 512], F32); wring("vg", 2, [128, 512])
        wring("Eg", 1, [128, 512], F32); wring("rr", 2, [128, 512], F32)
        wring("qd_bf", 1, [128, 512]); wring("kd_f", 1, [128, 512], F32); wring("kd_bf", 1, [128, 512])
        wring("vd_f", 1, [128, 512], F32); wring("ag_bf", 1, [128, 16])
        wring("Lg", 1, [128, 256], F32)
        wring("ex", 2, [128, 256], F32)
        wring("ebL", 1, [128, 8], F32)
        wring("qt", 1, [128, 256]); wring("kt", 1, [128, 256]); wring("kh", 1, [128, 2, 256])
        wring("qz", 1, [128, 4, 128]); wring("kTg", 1, [128, 2, 128]); wring("ATm", 4, [128, 128])
        for t_ in W["qz"].tiles:
            MEMSET("pool", t_.t[:, :, :], 0.0, [t_])
        wring("ocat", 1 if smp else 3, [128, 1024])
        wring("t1", 1, [128, 2, 128], F32); wring("od", 1, [128, 2, 128], F32)
        wring("ytmp", 1, [128, 1024], F32); wring("xre", 1, [128, 1024], F32)

    banks = [Tl(nc.alloc_psum_tensor("bank%d" % i, [128, 512], F32), "bank%d" % i) for i in range(8)]
    for b_ in banks:
        b_.b.excl = True

    def bfview(tl):
        v = Tl(tl.t[:, :].bitcast(BF16).rearrange("p (k n) -> p k n", k=8), tl.b.name + "_bf")
        v.b = tl.b
        return v

    tp = bfview(banks[0])
    import os as _os
    NZP = int(_os.environ.get("NZP", "3"))
    zp = Ring([banks[1], banks[2]] + ([banks[7]] if NZP == 3 else []))
    scr = zp if NZP != 4 else Ring([banks[7]])
    if NZP == 9:
        class FakeRing:
            def __init__(self, tl, n, view=None):
                self.tiles = []
                for i in range(n):
                    t_ = Tl(tl.t, "fk")
                    t_.b.excl = True
                    self.tiles.append(t_)
                self.i = 0
            def next(self):
                t_ = self.tiles[self.i % len(self.tiles)]; self.i += 1
                return t_
        zp = FakeRing(banks[1], int(_os.environ.get("FZP", "6")))
        scr = FakeRing(banks[2], int(_os.environ.get("FSC", "4"))) if _os.environ.get("FSC", "4") != "0" else zp
    import os as _os5
    if True:
        if _os5.environ.get("FAKETP"):
            class TPProxy:
                def __init__(self, base, n):
                    self.t = base.t
                    self.bufs = [Buf("ftp%d" % i) for i in range(n)]
                    for b_ in self.bufs:
                        b_.excl = True
                    self.k = 0
                    self.cur = self.bufs[0]
                def rotate(self):
                    self.k += 1
                    self.cur = self.bufs[self.k % len(self.bufs)]
                @property
                def b(self):
                    return self.cur
            tp = TPProxy(tp, int(_os5.environ.get("FAKETP")))
    if NZP == 5:
        scr = Ring([Tl(banks[1].t, "fake_sc0"), Tl(banks[2].t, "fake_sc1")])
        for t_ in scr.tiles:
            t_.b.excl = True
        zp = Ring([banks[1], banks[2], banks[7]])
    gU = banks[3]; og = banks[4]
    accb = [banks[5], banks[6]]
    gAT = banks[7]
    tpB = bfview(banks[7])

    def bl(xs_):
        return [x.b if hasattr(x, "b") else x for x in xs_]

    def fsz(ap):
        n = 1
        for d in ap.shape[1:]:
            n *= d
        return n

    def ecost(eng, n):
        if eng == "act":
            return n / 1.2 + 200.0
        if eng == "dve":
            return n * 1.04 + 100.0
        return n * 3.0 + 300.0

    def MM(out, lhsT, rhs, start, stop, rd, wr, skip=True):
        c = max(fsz(rhs), 64) / 2.2 * (4.0 if rhs.dtype == F32 else 1.0) + 45.0
        return P.add("pe", lambda e: e.matmul(out, lhsT=lhsT, rhs=rhs, start=start, stop=stop,
                                              skip_group_check=skip), bl(rd), bl(wr), cost=c, lat=c + HOP_NS)

    def TR(out, in_, idn, rd, wr):
        return P.add("pe", lambda e: e.transpose(out, in_, idn), bl(rd), bl(wr), cost=120.0, lat=120.0 + HOP_NS)

    def ACT(out, in_, func, rd, wr, bias=None, scale=None, accum=None):
        kw = {}
        if bias is not None:
            kw["bias"] = bias
        if scale is not None:
            kw["scale"] = scale
        if accum is not None:
            kw["accum_out"] = accum
        c = ecost("act", fsz(in_)) + (60.0 if accum is not None else 0.0)
        return P.add("act", lambda e: e.activation(out=out, in_=in_, func=func, **kw), bl(rd), bl(wr), cost=c)

    def AMUL(out, in_, m, rd, wr):
        return P.add("act", lambda e: e.mul(out=out, in_=in_, mul=m), bl(rd), bl(wr), cost=ecost("act", fsz(in_)))

    def CP(eng, out, in_, rd, wr):
        c = ecost(eng, fsz(in_))
        if hasattr(tp, "rotate") and any(x is tp for x in rd):
            rd = [x.b if x is tp else x for x in rd]
            tp.rotate()
        if eng == "act":
            return P.add("act", lambda e: e.copy(out=out, in_=in_), bl(rd), bl(wr), cost=c)
        return P.add(eng, lambda e: e.tensor_copy(out, in_), bl(rd), bl(wr), cost=c)

    def TS(eng, out, in0, s1, s2, op0, op1, rd, wr):
        c = ecost(eng, fsz(in0))
        if eng == "act":
            assert op0 == ALU.mult and op1 is None
            return P.add("act", lambda e: e.activation(out=out, in_=in0, func=AF.Copy, scale=s1), bl(rd), bl(wr), cost=c)
        if op1 is None:
            return P.add(eng, lambda e: e.tensor_scalar(out, in0, s1, None, op0), bl(rd), bl(wr), cost=c)
        return P.add(eng, lambda e: e.tensor_scalar(out, in0, s1, s2, op0, op1), bl(rd), bl(wr), cost=c)

    def TT(eng, out, in0, in1, op, rd, wr):
        return P.add(eng, lambda e: e.tensor_tensor(out, in0, in1, op), bl(rd), bl(wr), cost=ecost(eng, fsz(in0)))

    def STT(eng, out, in0, scalar, in1, op0, op1, rd, wr):
        return P.add(eng, lambda e: e.scalar_tensor_tensor(out, in0, scalar, in1, op0, op1), bl(rd), bl(wr),
                     cost=ecost(eng, fsz(in0)))

    def RECIP(out, in_, rd, wr):
        return P.add("dve", lambda e: e.reciprocal(out, in_), bl(rd), bl(wr), cost=ecost("dve", fsz(in_)) * 2.0)

    def MEMSET(eng, ap, val, wr):
        return P.add(eng, lambda e: e.memset(ap, val), (), bl(wr), cost=ecost(eng, fsz(ap)) * 0.5)

    import os as _os9
    STORE_Q = _os9.environ.get("STOREQ", "sp")

    def DMA(q, out, in_, rd, wr):
        nbytes = fsz(out) * out.shape[0] * (4 if out.dtype == F32 else 2)
        is_store = "DRam" in type(out.tensor).__name__
        if is_store:
            q = STORE_Q
        return P.add(q, lambda e: e.dma_start(out=out, in_=in_), bl(rd), bl(wr), dma=True,
                     cost=(600.0 if q == "pool" else 120.0), lat=2000.0 + nbytes / 100.0)

    def SQ(dst_ap, dst_tl, in_, rd, sa, accum):
        ACT(dst_ap, in_, AF.Square, rd, [dst_tl, sa], accum=accum)

    ident = C["ident"]
    _alt = [0]

    def alt2():
        _alt[0] += 1
        return "act" if _alt[0] % 2 else "dve"

    def rstd_from_ss(ssap, n, nt, dst, tl):
        ACT(dst, ssap, AF.Ln, [tl, C_eps], [tl], bias=C_eps.t[0:nt, 0:1], scale=1.0 / n)
        ACT(dst, dst, AF.Exp, [tl], [tl], scale=-0.5)

    Win = AR.alloc("Win", [128, 8, 3088]); Wout = AR.alloc("Wout", [128, 8, 1024])
    A_FIX = AR.off
    kTs = [AR.alloc("kTs%d" % b, [128, PAST]) for b in range(4)]
    Vs = [AR.alloc("Vs%d" % b, [128, NKT, 129]) for b in range(4)]
    alloc_common(True)
    wring("cst", 5, [128, 8, 128], F32)
    wring("kcs", 3, [128, 8, 128])
    S32s = [[AR.alloc("S32s%d_%d" % (b, p), [128, 128], F32) for p in range(2)] for b in range(4)]
    Sbfs = [[AR.alloc("Sbfs%d_%d" % (b, p), [128, 128]) for p in range(2)] for b in range(4)]
    wtile("qm", [128, 4, 4, 64]); wtile("khm", [64, 4, 256])
    wtile("qTs", [128, 2, 4, 64]); wtile("kTn", [128, 4, 64]); wtile("Vsn", [64, 4, 129])
    wring("PTs", 2, [128, 4, 2, 64])
    wtile("PTn", [64, 128]); wtile("sctmp", [64, 128], F32)
    wtile("lam_in", [128, 256], F32); wtile("lam_tmp", [128, 128], F32)
    wtile("nbias", [64, 512], F32)
    C["nbias"] = W["nbias"]

    MEMSET("dve", C_eps.t[:, :], EPS, [C_eps]); MEMSET("dve", C_one.t[:, :], 1.0, [C_one])
    for k in C:
        DMA("sp", C[k].t[:, :], cd[k], [], [C[k]])
    DMA("sp", gpre.t[:, :], gpre_d, [], [gpre]); DMA("sp", gpf.t[:, :], gpf_d, [], [gpf])
    DMA("sp", ggla.t[:, :], ggla_d, [], [ggla]); DMA("sp", gpm.t[:, :], gpm_d, [], [gpm])
    lam_in = W["lam_in"]; lam_tmp = W["lam_tmp"]
    DMA("sp", lam_in.t[:, :], lam_d, [], [lam_in])

    def stage_f32(src_ap, ncols):
        c_ = W["cst"].next()
        ap = c_.t[:, :, :].rearrange("p a b -> p (a b)")[:, 0:ncols]
        DMA("sp", ap, src_ap, [], [c_])
        return c_, ap

    c_ = W["cst"].next()
    ap17 = c_.t[0:17, 0:2, :].rearrange("p a b -> p (a b)")
    DMA("sp", ap17, wgu_d, [], [c_])
    CP("dve", wg_aug.t[:, :], ap17, [c_], [wg_aug])
    c_, ap = stage_f32(gsub_d, 128)
    TS("dve", gsub8.t[:, :], ap, 1.0 - LAM_INIT, None, ALU.mult, None, [c_], [gsub8])
    TT("dve", lam_tmp.t[:, 0:64], lam_in.t[:, 0:64], lam_in.t[:, 64:128], ALU.mult, [lam_in], [lam_tmp])
    TT("dve", lam_tmp.t[:, 64:128], lam_in.t[:, 128:192], lam_in.t[:, 192:256], ALU.mult, [lam_in], [lam_tmp])
    P.add("dve", lambda e: e.reduce_sum(lam_s.t[:, 0:2], lam_tmp.t[:, :].rearrange("p (a b) -> p a b", a=2), AX.X),
          bl([lam_tmp]), bl([lam_s]))
    ACT(lam_s.t[:, 2:4], lam_s.t[:, 0:2], AF.Exp, [lam_s], [lam_s])
    TT("dve", neg_lam.t[:, :], lam_s.t[:, 3:4], lam_s.t[:, 2:3], ALU.subtract, [lam_s], [neg_lam])
    TS("dve", neg_lam.t[:, :], neg_lam.t[:, :], -LAM_INIT, None, ALU.add, None, [neg_lam], [neg_lam])
    k_ = 0
    for kc in range(8):
        for c0 in range(0, 3088, 1024):
            c1 = min(3088, c0 + 1024)
            c_, ap = stage_f32(win_d[:, kc * 3088 + c0:kc * 3088 + c1], c1 - c0)
            e_ = ("dve", "act")[k_ % 2]; k_ += 1
            TS(e_, Win.t[:, kc, c0:c1], ap, gpre.t[:, kc:kc + 1], None, ALU.mult, None, [c_, gpre], [Win])
    for kc in range(8):
        c_, ap = stage_f32(wout_d[:, kc * 1024:(kc + 1) * 1024], 1024)
        CP(("act", "dve")[kc % 2], Wout.t[:, kc, :], ap, [c_], [Wout])
    MEMSET("pool", agT.t[:, :], 1.0, [agT])
    for p in range(2):
        MEMSET("pool", S32[p].t[:, :], 0.0, [S32[p]])
        for v in range(2):
            MEMSET("pool", Sbf[p].tiles[v].t[:, :], 0.0, [Sbf[p].tiles[v]])
    MEMSET("pool", W["qm"].t[:, :, :, :], 0.0, [W["qm"]])
    MEMSET("pool", W["qTs"].t[:, :, :, :], 0.0, [W["qTs"]])
    for v in range(2):
        MEMSET("pool", W["PTs"].tiles[v].t[:, :, :, :], 0.0, [W["PTs"].tiles[v]])

    def early():
        P.finish()
        P.emit()
        return nc

    if stop == "prep":
        return early()

    def stage_AD(x_rows, nt, st):
        xi = W["xin"].next()
        DMA("sp", xi.t[0:nt, :], x_rows, [], [xi])
        sa = stat.next()
        xn_ = W["b1k"].next()
        SQ(xn_.t[0:nt, :], xn_, xi.t[0:nt, :], [xi], sa, sa.t[0:nt, 0:1])
        rstd_from_ss(sa.t[0:nt, 0:1], 1024.0, nt, sa.t[0:nt, 1:2], sa)
        TS("dve", xn_.t[0:nt, :], xi.t[0:nt, :], sa.t[0:nt, 1:2], None, ALU.mult, None, [xi, sa], [xn_])
        for kc in range(8):
            TR(tp.t[:, kc, 0:nt], xn_.t[0:nt, kc * 128:(kc + 1) * 128], ident.t[0:nt, 0:nt], [xn_, ident], [tp])
        h_ = W["b1k"].next()
        h3 = h_.t[:, :].rearrange("p (k n) -> p k n", k=8)
        CP("dve", h3[:, :, 0:nt], tp.t[:, :, 0:nt], [tp], [h_])

        def proj(c0, c1):
            z = zp.next()
            for kc in range(8):
                MM(z.t[0:nt, 0:c1 - c0], h3[:, kc, 0:nt], Win.t[:, kc, c0:c1], kc == 0, kc == 7, [h_, Win], [z])
            return z

        z = proj(0, 16)
        ag = W["ag_bf"].next()
        CP("dve", ag.t[0:nt, :], z.t[0:nt, 0:16], [z], [ag])
        TR(tp.t[0:16, 0, 0:nt], ag.t[0:nt, 0:16], ident.t[0:nt, 0:nt], [ag, ident], [tp])
        CP("act", agT.t[0:16, 0:nt], tp.t[0:16, 0, 0:nt], [tp], [agT])
        z = zp.next()
        MM(z.t[0:nt, 0:256], agT.t[0:17, 0:nt], wg_aug.t[0:17, 0:256], True, True, [agT, wg_aug], [z])
        L = W["Lg"].next()
        ACT(L.t[0:nt, :], z.t[0:nt, 0:256], AF.Exp, [z], [L], scale=-1.0)
        ACT(L.t[0:nt, :], L.t[0:nt, :], AF.Ln, [L, C_one], [L], bias=C_one.t[0:nt, 0:1])
        st["L"] = L
        z = proj(16, 528)
        zq = W["zqk"].next()
        CP("dve", zq.t[0:nt, :], z.t[0:nt, :], [z], [zq])
        st["zqk"] = zq
        z = proj(528, 1040)
        v_ = W["vg"].next()
        CP("dve", v_.t[0:nt, :], z.t[0:nt, :], [z], [v_])
        st["vg"] = v_
        z = proj(1040, 1552)
        E_ = W["Eg"].next(); r_ = W["rr"].next()
        ACT(E_.t[0:nt, :], z.t[0:nt, :], AF.Exp, [z], [E_], scale=-1.0)
        TT("dve", r_.t[0:nt, :], z.t[0:nt, :], ggla.t[0:nt, :], ALU.mult, [z, ggla], [r_])
        TS("dve", E_.t[0:nt, :], E_.t[0:nt, :], 1.0, None, ALU.add, None, [E_], [E_])
        RECIP(E_.t[0:nt, :], E_.t[0:nt, :], [E_], [E_])
        TT("dve", r_.t[0:nt, :], r_.t[0:nt, :], E_.t[0:nt, :], ALU.mult, [r_, E_], [r_])
        st["G2"] = r_
        z = proj(1552, 2064)
        qd = W["qd_bf"].next()
        TS("dve", qd.t[0:nt, :], z.t[0:nt, :], 0.125, None, ALU.mult, None, [z], [qd])
        z = proj(2064, 2576)
        kf = W["kd_f"].next(); kb = W["kd_bf"].next()
        CP("act", kf.t[0:nt, :], z.t[0:nt, :], [z], [kf])
        CP("dve", kb.t[0:nt, :], kf.t[0:nt, :], [kf], [kb])
        z = proj(2576, 3088)
        vf = W["vd_f"].next()
        CP("act", vf.t[0:nt, :], z.t[0:nt, :], [z], [vf])
        st["qd"] = qd; st["kf"] = kf; st["kb"] = kb; st["vf"] = vf

    def stage_gla_gates(nt, smp, st):
        L = st["L"]; zq = st["zqk"]
        z = zp.next()
        tle = C["triLE_s"] if smp else C["triLE"]; tgt = C["triGT_s"] if smp else C["triGT"]
        ind = C["ind_s"] if smp else C["ind"]
        ni = 4 if smp else 2
        gbL = zp.next()
        MM(z.t[0:nt, 0:256], tle.t[0:nt, 0:nt], L.t[0:nt, :], True, True, [tle, L], [z])
        MM(z.t[0:nt, 256:512], tgt.t[0:nt, 0:nt], L.t[0:nt, :], True, True, [tgt, L], [z])
        for p in range(2):
            MM(gbL.t[:, p * ni:(p + 1) * ni], L.t[0:nt, p * 128:(p + 1) * 128], ind.t[0:nt, 0:ni], True, True,
               [L, ind], [gbL])
        ebL_ = W["ebL"].next()
        q_ = W["qt"].next(); k_ = W["kt"].next(); kh = W["kh"].next()
        eb_ = W["ex"].next()
        ACT(eb_.t[0:nt, :], z.t[0:nt, 0:256], AF.Exp, [z], [eb_])
        STT("dve", q_.t[0:nt, :], zq.t[0:nt, 0:256], 0.125, eb_.t[0:nt, :], ALU.mult, ALU.mult, [zq, eb_], [q_])
        enb_ = W["ex"].next()
        ACT(enb_.t[0:nt, :], z.t[0:nt, 0:256], AF.Exp, [z], [enb_], scale=-1.0)
        TT("dve", k_.t[0:nt, :], zq.t[0:nt, 256:512], enb_.t[0:nt, :], ALU.mult, [zq, enb_], [k_])
        ec_ = W["ex"].next()
        ACT(ec_.t[0:nt, :], z.t[0:nt, 256:512], AF.Exp, [z], [ec_])
        ACT(ebL_.t[:, 0:2 * ni], gbL.t[:, 0:2 * ni], AF.Exp, [gbL], [ebL_])
        if smp:
            TT("pool", kh.t[0:nt, 0, :], zq.t[0:nt, 256:512], ec_.t[0:nt, :], ALU.mult, [zq, ec_], [kh])
        else:
            for c in range(2):
                STT("dve", kh.t[:, c, :], zq.t[:, 256:512], C["cm"].t[:, c:c + 1], ec_.t[:, :],
                    ALU.mult, ALU.mult, [zq, ec_, C["cm"]], [kh])
        for p in range(2):
            TR(tp.t[:, p, 0:nt], q_.t[0:nt, p * 128:(p + 1) * 128], ident.t[0:nt, 0:nt], [q_, ident], [tp])
            TR(tp.t[:, 2 + p, 0:nt], k_.t[0:nt, p * 128:(p + 1) * 128], ident.t[0:nt, 0:nt], [k_, ident], [tp])
        qz = W["qz"].next(); kTg = W["kTg"].next()
        qz4 = qz.t[:, :, :].rearrange("p (a b) t -> p a b t", b=2)
        for hp in range(2):
            CP("act", qz4[64 * hp:64 * hp + 64, :, hp, 0:nt], tp.t[64 * hp:64 * hp + 64, 0:2, 0:nt], [tp], [qz])
        CP("dve", kTg.t[:, :, 0:nt], tp.t[:, 2:4, 0:nt], [tp], [kTg])
        st["qz"] = qz; st["kTg"] = kTg; st["kh"] = kh; st["ebL"] = ebL_

    def gla_intra(nt, st, maskA):
        qz = st["qz"]; kTg = st["kTg"]; v_ = st["vg"]
        gAT_ = zp.next() if NZP == 3 else gAT
        for h in range(4):
            MM(gAT_.t[0:nt, h * 128:h * 128 + nt], kTg.t[:, h // 2, 0:nt], qz.t[:, h, 0:nt],
               True, True, [qz, kTg], [gAT_])
        ats = []
        for h in range(4):
            a_ = W["ATm"].next()
            TT("dve", a_.t[0:nt, 0:nt], gAT_.t[0:nt, h * 128:h * 128 + nt], maskA.t[0:nt, 0:nt], ALU.mult,
               [gAT_, maskA], [a_])
            ats.append(a_)
        for h in range(4):
            MM(og.t[0:nt, h * 128:(h + 1) * 128], ats[h].t[0:nt, 0:nt], v_.t[0:nt, h * 128:(h + 1) * 128],
               h == 0, False, [ats[h], v_], [og])

    def gla_out_norm(nt, st, oc):
        sa = stat.next()
        for h in range(4):
            SQ(oc.t[0:nt, h * 128:(h + 1) * 128], oc, og.t[0:nt, h * 128:(h + 1) * 128], [og], sa, sa.t[0:nt, h:h + 1])
        rstd_from_ss(sa.t[0:nt, 0:4], 128.0, nt, sa.t[0:nt, 4:8], sa)
        G2 = st["G2"]
        for h in range(4):
            STT("dve", oc.t[0:nt, h * 128:(h + 1) * 128], og.t[0:nt, h * 128:(h + 1) * 128], sa.t[0:nt, 4 + h:5 + h],
                G2.t[0:nt, h * 128:(h + 1) * 128], ALU.mult, ALU.mult, [og, sa, G2], [oc])

    def stage_gla_prompt(st, oc):
        nt = 128
        qz = st["qz"]; kh = st["kh"]; v_ = st["vg"]; ebL_ = st["ebL"]
        for c in range(2):
            for h in range(4):
                p, r0 = h // 2, 64 * (h % 2)
                MM(gU.t[r0:r0 + 64, (p * 2 + c) * 128:(p * 2 + c + 1) * 128],
                   kh.t[:, c, h * 64:(h + 1) * 64], v_.t[:, h * 128:(h + 1) * 128],
                   True, True, [kh, v_], [gU])
        gla_intra(nt, st, C["maskA"])
        for c in range(2):
            cur = [Sbf[p].tiles[Sbf[p].i % 2] for p in range(2)]
            for h in range(4):
                p, r0 = h // 2, 64 * (h % 2)
                MM(og.t[64 * c:64 * c + 64, h * 128:(h + 1) * 128], qz.t[:, h, 64 * c:64 * c + 64],
                   cur[p].t[:, :], False, (c == 1), [qz, cur[p]], [og])
            for p in range(2):
                Sbf[p].i += 1
                nxt = Sbf[p].tiles[Sbf[p].i % 2]
                STT("dve", S32[p].t[:, :], S32[p].t[:, :], ebL_.t[:, 2 * p + c:2 * p + c + 1],
                    gU.t[:, (p * 2 + c) * 128:(p * 2 + c + 1) * 128], ALU.mult, ALU.add, [S32[p], ebL_, gU], [S32[p]])
                CP("dve", nxt.t[:, :], S32[p].t[:, :], [S32[p]], [nxt])
        gla_out_norm(nt, st, oc)

    def attn_epilogue(nt, bank, oc, hh, t_ap, o_ap, t_tl, o_tl):
        a1 = bank.t[0:nt, 0:129]; a2 = bank.t[0:nt, 129:258]
        sums = bank.t[0:nt, 0:258].rearrange("p (s n) -> p s n", s=2)[:, :, 128:129]
        sa = stat.next()
        P.add("dve", lambda e: e.reciprocal(sa.t[0:nt, 0:2].rearrange("p (s n) -> p s n", s=2), sums), bl([bank]), bl([sa]),
              cost=150.0)
        TT("dve", sa.t[0:nt, 2:3], sa.t[0:nt, 1:2], neg_lam.t[0:nt, 0:1], ALU.mult, [sa, neg_lam], [sa])
        TS("dve", t_ap, a1[:, 0:128], sa.t[0:nt, 0:1], None, ALU.mult, None, [bank, sa], [t_tl])
        STT("dve", o_ap, a2[:, 0:128], sa.t[0:nt, 2:3], t_ap, ALU.mult, ALU.add, [bank, sa, t_tl], [o_tl])
        SQ(oc.t[0:nt, 512 + hh * 128:512 + (hh + 1) * 128], oc, o_ap, [o_tl], sa, sa.t[0:nt, 3:4])
        rstd_from_ss(sa.t[0:nt, 3:4], 128.0, nt, sa.t[0:nt, 4:5], sa)
        STT("dve", oc.t[0:nt, 512 + hh * 128:512 + (hh + 1) * 128], o_ap, sa.t[0:nt, 4:5],
            gsub8.t[0:nt, :], ALU.mult, ALU.mult, [o_tl, sa, gsub8], [oc])

    def stage_wout(nt, oc, x_rows, srow, sbuf_i):
        for kc in range(8):
            TR(tp.t[:, kc, 0:nt], oc.t[0:nt, kc * 128:(kc + 1) * 128], ident.t[0:nt, 0:nt], [oc, ident], [tp])
        o_ = W["b1k"].next()
        o3 = o_.t[:, :].rearrange("p (k n) -> p k n", k=8)
        CP("dve", o3[:, :, 0:nt], tp.t[:, :, 0:nt], [tp], [o_])
        xr = W["xre"].next()
        DMA("sp", xr.t[0:nt, :], x_rows, [], [xr])
        ys_ = []
        for n in range(2):
            z = zp.next()
            for kc in range(8):
                MM(z.t[0:nt, :], o3[:, kc, 0:nt], Wout.t[:, kc, n * 512:(n + 1) * 512], kc == 0, kc == 7, [o_, Wout], [z])
            ys_.append(z)
        sa = stat.next()
        yt = W["ytmp"].next()
        for n in range(2):
            SQ(yt.t[0:nt, n * 512:(n + 1) * 512], yt, ys_[n].t[0:nt, :], [ys_[n]], sa, sa.t[0:nt, n:n + 1])
        TT("dve", sa.t[0:nt, 2:3], sa.t[0:nt, 0:1], sa.t[0:nt, 1:2], ALU.add, [sa], [sa])
        rstd_from_ss(sa.t[0:nt, 2:3], 1024.0, nt, sa.t[0:nt, 3:4], sa)
        for n in range(2):
            STT("dve", yt.t[0:nt, n * 512:(n + 1) * 512], ys_[n].t[0:nt, :], sa.t[0:nt, 3:4],
                gpm.t[0:nt, n * 512:(n + 1) * 512], ALU.mult, ALU.mult, [ys_[n], sa, gpm], [yt])
        TT("dve", yt.t[0:nt, :], yt.t[0:nt, :], xr.t[0:nt, :], ALU.add, [yt, xr], [yt])
        DMA("sp", x1s[srow:srow + nt, :], yt.t[0:nt, :], [yt], [x1sb[sbuf_i]])

    def sample_phase():
        nt = 64
        st = {}
        stage_AD(xs, nt, st)
        qd = st["qd"]; kb = st["kb"]; kf = st["kf"]; vf = st["vf"]
        qTs = W["qTs"]; kTn = W["kTn"]; Vsn = W["Vsn"]; qm = W["qm"]; khm = W["khm"]
        DMA("sp", ksm_d, kf.t[0:nt, :], [kf], [])
        DMA("sp", vsm_d, vf.t[0:nt, :], [vf], [])
        if stop == "s1":
            return
        for hh in range(4):
            TR(tp.t[:, hh, 0:nt], qd.t[0:nt, hh * 128:(hh + 1) * 128], ident.t[0:nt, 0:nt], [qd, ident], [tp])
            TR(tp.t[:, 4 + hh, 0:nt], kb.t[0:nt, hh * 128:(hh + 1) * 128], ident.t[0:nt, 0:nt], [kb, ident], [tp])
        for s_ in range(2):
            CP("act", qTs.t[64 * s_:64 * s_ + 64, s_, :, :], tp.t[64 * s_:64 * s_ + 64, 0:4, 0:nt], [tp], [qTs])
        CP("dve", kTn.t[:, :, :], tp.t[:, 4:8, 0:nt], [tp], [kTn])
        MEMSET("pool", Vsn.t[:, :, 128:129], 1.0, [Vsn])
        CP("pool", Vsn.t[:, :, 0:128], vf.t[0:nt, :].rearrange("p (h n) -> p h n", h=4), [vf], [Vsn])
        if stop == "s1b":
            return
        stage_gla_gates(nt, True, st)
        if stop == "s2":
            return
        qz = st["qz"]; kh = st["kh"]; v_ = st["vg"]; ebL_ = st["ebL"]
        for b in range(4):
            for p in range(2):
                DMA("sp", S32s[b][p].t[:, :], sg[b, p], [], [S32s[b][p]])
                CP("pool", Sbfs[b][p].t[:, :], S32s[b][p].t[:, :], [S32s[b][p]], [Sbfs[b][p]])
        for h in range(4):
            for b in range(4):
                CP("pool", qm.t[:, h, b, 16 * b:16 * b + 16], qz.t[:, h, 16 * b:16 * b + 16], [qz], [qm])
        for b in range(4):
            TS("dve", khm.t[0:nt, b, :], kh.t[0:nt, 0, :], C["rm_s"].t[0:nt, b:b + 1], None, ALU.mult, None,
               [kh, C["rm_s"]], [khm])
        gla_intra(nt, st, C["maskA_s"])
        for h in range(4):
            p, r0 = h // 2, 64 * (h % 2)
            for b in range(4):
                MM(og.t[0:nt, h * 128:(h + 1) * 128], qm.t[:, h, b, :], Sbfs[b][p].t[:, :],
                   False, b == 3, [qm, Sbfs[b][p]], [og])
        oc = W["ocat"].next()
        gla_out_norm(nt, st, oc)
        if stop == "s3":
            return
        for b in range(4):
            for p in range(2):
                for hp in range(2):
                    h = 2 * p + hp
                    r0 = 64 * hp
                    MM(gU.t[r0:r0 + 64, p * 128:(p + 1) * 128], khm.t[0:nt, b, h * 64:(h + 1) * 64],
                       v_.t[0:nt, h * 128:(h + 1) * 128], True, True, [khm, v_], [gU])
            for p in range(2):
                STT("dve", S32s[b][p].t[:, :], S32s[b][p].t[:, :], ebL_.t[:, 4 * p + b:4 * p + b + 1],
                    gU.t[:, p * 128:(p + 1) * 128], ALU.mult, ALU.add, [S32s[b][p], ebL_, gU], [S32s[b][p]])
                DMA("sp", gs_d[b, p], S32s[b][p].t[:, :], [S32s[b][p]], [])
        if stop == "s4":
            return
        for hh in range(4):
            if stop == "s5" and hh == 1:
                return
            for b in range(4):
                for ch in range(NKT // 8):
                    c_ = W["cst"].next()
                    DMA("sp", c_.t[:, :, :], ck[b, ch * 1024:(ch + 1) * 1024, hh * 128:(hh + 1) * 128]
                        .rearrange("(k p) n -> p k n", p=128), [], [c_])
                    kc_ = W["kcs"].next()
                    CP("act", kc_.t[:, :, :], c_.t[:, :, :], [c_], [kc_])
                    tb = tp if ch % 2 == 0 else tpB
                    for k8 in range(8):
                        TR(tb.t[:, k8, :], kc_.t[:, k8, :], ident.t[:, :], [kc_, ident], [tb])
                    CP("dve", kTs[b].t[:, ch * 1024:(ch + 1) * 1024].rearrange("p (k n) -> p k n", k=8),
                       tb.t[:, :, :], [tb], [kTs[b]])
                    c_ = W["cst"].next()
                    DMA("sp", c_.t[:, :, :], cv[b, ch * 1024:(ch + 1) * 1024, hh * 128:(hh + 1) * 128]
                        .rearrange("(k p) n -> p k n", p=128), [], [c_])
                    CP("dve", Vs[b].t[:, ch * 8:(ch + 1) * 8, 0:128], c_.t[:, :, :], [c_], [Vs[b]])
                if hh == 0:
                    MEMSET("pool", Vs[b].t[:, :, 128:129], 1.0, [Vs[b]])
            acc = accb[hh % 2]
            first = True
            for kt in range(NKT):
                sc = zp.next()
                for b in range(4):
                    for s_ in range(2):
                        MM(sc.t[:, (b * 2 + s_) * 16:(b * 2 + s_ + 1) * 16],
                           kTs[b].t[:, kt * 128:(kt + 1) * 128],
                           qTs.t[:, s_, hh, 16 * b:16 * b + 16], True, True, [kTs[b], qTs], [sc])
                pt = W["PTs"].next()
                o_ap = bass.AP(AR.t, pt.off, [[NA, 128], [144, 4], [64, 2], [1, 16]])
                i_ap = sc.t[:, 0:128].rearrange("p (b s q) -> p b s q", b=4, s=2)
                ACT(o_ap, i_ap, AF.Exp, [sc, C["sbias"]], [pt],
                    bias=C["sbias"].t[:, hh * NKT + kt:hh * NKT + kt + 1])
                for b in range(4):
                    for s_ in range(2):
                        MM(acc.t[0:nt, s_ * 129:(s_ + 1) * 129], pt.t[:, b, s_, :], Vs[b].t[:, kt, :],
                           first, False, [pt, Vs[b]], [acc])
                        first = False
            sc = zp.next()
            for s_ in range(2):
                MM(sc.t[0:nt, s_ * 64:(s_ + 1) * 64], kTn.t[:, hh, :],
                   qTs.t[:, s_, hh, :], True, True, [kTn, qTs], [sc])
            sctmp = W["sctmp"]; PTn = W["PTn"]
            TT("dve", sctmp.t[:, :], sc.t[0:nt, 0:128], C["nbias"].t[:, hh * 128:(hh + 1) * 128], ALU.add,
               [sc, C["nbias"]], [sctmp])
            ACT(PTn.t[:, :], sctmp.t[:, :], AF.Exp, [sctmp], [PTn])
            for s_ in range(2):
                MM(acc.t[0:nt, s_ * 129:(s_ + 1) * 129], PTn.t[:, s_ * 64:(s_ + 1) * 64], Vsn.t[:, hh, :],
                   False, True, [PTn, Vsn], [acc])
            t_ = W["t1"].next(); o_ = W["od"].next()
            attn_epilogue(nt, acc, oc, hh, t_.t[0:nt, 0, :], o_.t[0:nt, 0, :], t_, o_)
        stage_wout(nt, oc, xs, T, NT)

    sample_phase()
    P.barrier()
    if stop in ("sample", "s1", "s1b", "s2", "s3", "s4", "s5"):
        return early()

    AR.off = A_FIX
    W.clear()
    kT2 = AR.alloc("kT2", [128, 4, T]).t
    Vaug = AR.alloc("Vaug", [128, NT, 4, 129]).t
    kTb = [Buf("kT%d" % g) for g in range(NG)]
    Vb = [Buf("V%d" % g) for g in range(NG)]
    RD_on[0] = True
    alloc_common()
    RD_on[0] = False
    wring("cvs", 1, [128, 512], F32); wring("cvo", 1, [128, 512])
    wring("qT2", 2, [128, 2, 4, QG])
    for t_ in W["qT2"].tiles:
        MEMSET("pool", t_.t[:, :, :, :], 0.0, [t_])
    wring("PT", 3, [128, 2, QG])
    import os as _os2
    if _os2.environ.get("KSIM"):
        print("phase P arena spare elems:", AR.n - AR.off)
    for ti_ in range(NT):
        MEMSET("pool", Vaug[:, ti_, :, 128:129], 1.0, [Vb[ti_ * 128 // QG]])

    def kv_store_prompt(g, ti, i, st, qa):
        qd = st["qd"]; kb = st["kb"]; kf = st["kf"]; vf = st["vf"]
        for hh in range(4):
            TR(tp.t[:, hh, :], qd.t[:, hh * 128:(hh + 1) * 128], ident.t[:, :], [qd, ident], [tp])
            TR(tp.t[:, 4 + hh, :], kb.t[:, hh * 128:(hh + 1) * 128], ident.t[:, :], [kb, ident], [tp])
        for s_ in range(2):
            CP("act", qa.t[64 * s_:64 * s_ + 64, s_, :, ti * 128:(ti + 1) * 128], tp.t[64 * s_:64 * s_ + 64, 0:4, :],
               [tp], [qa])
        CP("dve", kT2[:, :, i * 128:(i + 1) * 128], tp.t[:, 4:8, :], [tp], [kTb[g]])
        CP("dve", Vaug[:, i, :, 0:128], vf.t[:, :].rearrange("p (h n) -> p h n", h=4), [vf], [Vb[g]])
        DMA("sp", kp_d[i * 128:(i + 1) * 128, :], kf.t[:, :], [kf], [])
        DMA("sp", vp_d[i * 128:(i + 1) * 128, :], vf.t[:, :], [vf], [])

    NQT = QG // 128

    def attention_prompt(g, qa, ocs):
        nkt = NQT * (g + 1)
        btab = C["btab"]
        for hh in range(4):
            started = set()
            pend = []

            def pv(j, q0, pt):
                for s_ in range(2):
                    for qt in range(q0 // 128, NQT):
                        bk, sl = qt, s_
                        first = bk not in started
                        started.add(bk)
                        MM(accb[bk].t[:, sl * 129:(sl + 1) * 129], pt.t[:, s_, qt * 128:(qt + 1) * 128],
                           Vaug[:, j, hh, :], first, j == nkt - 1, [pt, Vb[j * 128 // QG]], [accb[bk]])

            for j in range(nkt):
                jj = j - NQT * g
                q0 = 128 * jj if jj >= 0 else 0
                sc = scr.next()
                sc3 = sc.t[:, :].rearrange("p (s q) -> p s q", s=2)
                for s_ in range(2):
                    MM(sc3[:, s_, q0:QG], kT2[:, hh, j * 128:(j + 1) * 128],
                       qa.t[:, s_, hh, q0:QG], True, jj < 0, [kTb[j * 128 // QG], qa], [sc])
                    if jj >= 0:
                        MM(sc3[:, s_, q0:q0 + 128], ident.t[:, :], C["corr"].t[:, hh * 128:(hh + 1) * 128],
                           False, True, [ident, C["corr"]], [sc])
                pt = W["PT"].next()
                bidx = NQT * g - j + 1
                ACT(pt.t[:, :, q0:QG], sc3[:, :, q0:QG], AF.Exp, [sc, btab], [pt],
                    bias=btab.t[:, hh * (NT + 2) + bidx:hh * (NT + 2) + bidx + 1])
                for a in pend:
                    pv(*a)
                pend = [(j, q0, pt)]
            for a in pend:
                pv(*a)
            t_ = W["t1"].next(); o_ = W["od"].next()
            for qt in range(NQT):
                attn_epilogue(128, accb[qt], ocs[qt], hh, t_.t[:, qt, :], o_.t[:, qt, :], t_, o_)

    conv_jobs = []
    for kc in range(8):
        for q8 in range(8):
            conv_jobs.append(("up", kc, kc * 4096 + q8 * 512))
    for c in range(32):
        for hf in range(2):
            conv_jobs.append(("dn", c, c * 1024 + hf * 512))
    conv_i = [0]
    wupbB = [Buf("wupbB%d" % kc) for kc in range(8)]
    NPRE = 6

    def conv_some(n):
        for _ in range(n):
            if conv_i[0] >= len(conv_jobs):
                return
            kind, kc, off = conv_jobs[conv_i[0]]
            e_ = ("dve", "act")[conv_i[0] % 2]
            conv_i[0] += 1
            s_ = W["cvs"].next(); o_ = W["cvo"].next()
            if kind == "up":
                DMA("sp", s_.t[:, :], wup_d[:, off:off + 512], [], [s_])
                TS(e_, o_.t[:, :], s_.t[:, :], gpf.t[:, kc:kc + 1], None, ALU.mult, None, [s_, gpf], [o_])
                DMA("sp", wupb[:, off:off + 512], o_.t[:, :], [o_], [wupbB[kc]])
            else:
                DMA("sp", s_.t[:, :], wdn_d[:, off:off + 512], [], [s_])
                CP(e_, o_.t[:, :], s_.t[:, :], [s_], [o_])
                DMA("sp", wdnb[:, off:off + 512], o_.t[:, :], [o_], [])

    per_group = (len(conv_jobs) + max(NG - 1, 1) - 1) // max(NG - 1, 1)
    for g in range(NG):
        qa = W["qT2"].next()
        ocs = []
        for ti in range(NQT):
            i = g * NQT + ti
            st = {}
            stage_AD(xp[i * 128:(i + 1) * 128, :], 128, st)
            kv_store_prompt(g, ti, i, st, qa)
            stage_gla_gates(128, False, st)
            oc = W["ocat"].next()
            stage_gla_prompt(st, oc)
            ocs.append(oc)
        attention_prompt(g, qa, ocs)
        for ti in range(NQT):
            i = g * NQT + ti
            stage_wout(128, ocs[ti], xp[i * 128:(i + 1) * 128, :], i * 128, i)
        if g < NG - 1 or NG == 1:
            conv_some(per_group)
    conv_some(len(conv_jobs))
    WupP = AR.ap[:, 0:32768].rearrange("p (k n) -> p k n", k=8)
    for kc in range(NPRE):
        for cb in range(4):
            DMA("sp", WupP[:, kc, cb * 1024:(cb + 1) * 1024],
                wupb[:, kc * 4096 + cb * 1024:kc * 4096 + (cb + 1) * 1024], [wupbB[kc]], [Win])
    for p in range(2):
        DMA("sp", gp_d[p], S32[p].t[:, :], [S32[p]], [])
    P.barrier()
    if stop == "pass1":
        return early()

    AR.off = 0
    W.clear()
    Wup = AR.alloc("Wup", [128, 8, 4096]); Wdn = AR.alloc("Wdn", [128, 32, 1024])
    WupB = [Buf("WupB%d" % i) for i in range(4)]
    WdnB = [Buf("WdnB%d" % i) for i in range(4)]
    wring("xn", 1, [128, 1024])
    wring("fT", 1, [128, 8, 512]); wring("upT", 1, [128, 32, 512])
    wring("xres", 2, [128, 1024], F32)
    wring("rstg", 1, [128, 512], F32)
    wring("x1r", 2, [128, 1024], F32); wring("ytmp", 1, [128, 1024], F32)
    gpo = Tl(gpm.t, "gpo")
    gpo.b = gpm.b
    DMA("sp", gpo.t[:, :], gpo_d, [], [gpo])
    for cb in range(4):
        for kc in range(NPRE, 8):
            DMA("sp", Wup.t[:, kc, cb * 1024:(cb + 1) * 1024], wupb[:, kc * 4096 + cb * 1024:kc * 4096 + (cb + 1) * 1024],
                [], [WupB[cb]])
    for cb in range(4):
        for c in range(cb * 8, cb * 8 + 8):
            DMA("sp", Wdn.t[:, c, :], wdnb[:, c * 1024:(c + 1) * 1024], [], [WdnB[cb]])
    upp = Ring([banks[i] for i in (3, 4, 5, 6)])

    def ffn_group(tiles):
        f_ = W["fT"].next(); u_ = W["upT"].next()
        col = 0
        cols = []
        for (nt, srow, out_ap, bi) in tiles:
            x1 = W["x1r"].next()
            DMA("sp", x1.t[0:nt, :], x1s[srow:srow + nt, :], [x1sb[bi]], [x1])
            sa = stat.next()
            xn_ = W["xn"].next()
            SQ(xn_.t[0:nt, :], xn_, x1.t[0:nt, :], [x1], sa, sa.t[0:nt, 0:1])
            rstd_from_ss(sa.t[0:nt, 0:1], 1024.0, nt, sa.t[0:nt, 1:2], sa)
            TS("dve", xn_.t[0:nt, :], x1.t[0:nt, :], sa.t[0:nt, 1:2], None, ALU.mult, None, [x1, sa], [xn_])
            for kc in range(8):
                TR(tp.t[:, kc, 0:nt], xn_.t[0:nt, kc * 128:(kc + 1) * 128], ident.t[0:nt, 0:nt], [xn_, ident], [tp])
            CP(alt2(), f_.t[:, :, col:col + nt], tp.t[:, :, 0:nt], [tp], [f_])
            cols.append(col)
            col += nt
        ntok = col
        for c in range(32):
            up = upp.next()
            for kc in range(8):
                MM(up.t[:, 0:ntok], Wup.t[:, kc, c * 128:(c + 1) * 128], f_.t[:, kc, 0:ntok], kc == 0, kc == 7,
                   [WupB[c // 8], f_], [up])
            r_ = W["rstg"].next()
            ACT(r_.t[:, 0:ntok], up.t[:, 0:ntok], AF.Relu, [up], [r_])
            TT("dve", u_.t[:, c, 0:ntok], r_.t[:, 0:ntok], r_.t[:, 0:ntok], ALU.mult, [r_], [u_])
        for k_, (nt, srow, out_ap, bi) in enumerate(tiles):
            c0 = cols[k_]
            x1 = W["xres"].next()
            DMA("sp", x1.t[0:nt, :], x1s[srow:srow + nt, :], [x1sb[bi]], [x1])
            ys_ = []
            for n in range(2):
                z = zp.next()
                for c in range(32):
                    MM(z.t[0:nt, :], u_.t[:, c, c0:c0 + nt], Wdn.t[:, c, n * 512:(n + 1) * 512], c == 0, c == 31,
                       [u_, WdnB[c // 8]], [z])
                ys_.append(z)
            sa = stat.next()
            yt = W["ytmp"].next()
            for n in range(2):
                SQ(yt.t[0:nt, n * 512:(n + 1) * 512], yt, ys_[n].t[0:nt, :], [ys_[n]], sa, sa.t[0:nt, n:n + 1])
            TT("dve", sa.t[0:nt, 2:3], sa.t[0:nt, 0:1], sa.t[0:nt, 1:2], ALU.add, [sa], [sa])
            rstd_from_ss(sa.t[0:nt, 2:3], 1024.0, nt, sa.t[0:nt, 3:4], sa)
            for n in range(2):
                STT("dve", yt.t[0:nt, n * 512:(n + 1) * 512], ys_[n].t[0:nt, :], sa.t[0:nt, 3:4],
                    gpo.t[0:nt, n * 512:(n + 1) * 512], ALU.mult, ALU.mult, [ys_[n], sa, gpo], [yt])
            TT("dve", yt.t[0:nt, :], yt.t[0:nt, :], x1.t[0:nt, :], ALU.add, [yt, x1], [yt])
            DMA("sp", out_ap, yt.t[0:nt, :], [yt], [])

    for i in range(0, NT, 4):
        ffn_group([(128, (i + k) * 128, yp_d[(i + k) * 128:(i + k + 1) * 128, :], i + k) for k in range(4)])
    ffn_group([(64, T, ys_d, NT)])
    P.finish()
    P.emit()
    return nc


def _core_inputs(c, T, PAST, I, consts):
    f = np.float32
    w_in = np.asarray(I["w_in"][0], f)
    perm = np.concatenate([w_in[:, 1536:1552], w_in[:, 0:1536], w_in[:, 1552:]], axis=1)
    m = {}
    m["xp"] = np.ascontiguousarray(I["x_prompt"][c], f)
    m["xs"] = np.ascontiguousarray(I["x_sample"][4 * c:4 * c + 4], f).reshape(64, 1024)
    m["ck"] = np.ascontiguousarray(I["cache_k"][0, 4 * c:4 * c + 4], f).reshape(4, PAST, 512)
    m["cv"] = np.ascontiguousarray(I["cache_v"][0, 4 * c:4 * c + 4], f).reshape(4, PAST, 512)
    m["sg"] = np.ascontiguousarray(I["state_gla"][0, 4 * c:4 * c + 4], f).reshape(4, 2, 128, 128)
    m["win"] = np.ascontiguousarray(perm.reshape(8, 128, 3088).transpose(1, 0, 2)).reshape(128, 8 * 3088)
    m["wout"] = np.ascontiguousarray(np.asarray(I["w_out"][0], f).reshape(8, 128, 1024).transpose(1, 0, 2)).reshape(128, 8192)
    m["wup"] = np.ascontiguousarray(np.asarray(I["w_ff_up"][0], f).reshape(8, 128, 4096).transpose(1, 0, 2)).reshape(128, 8 * 4096)
    m["wdn"] = np.ascontiguousarray(np.asarray(I["w_ff_down"][0], f).reshape(32, 128, 1024).transpose(1, 0, 2)).reshape(128, 32 * 1024)
    m["wgu"] = np.ascontiguousarray(np.concatenate([np.asarray(I["w_gate_up"][0], f), np.asarray(I["b_gate"][0], f)[None]], 0))
    m["gpre"] = np.ascontiguousarray(np.asarray(I["g_pre_mix"][0], f).reshape(8, 128).T)
    m["gpf"] = np.ascontiguousarray(np.asarray(I["g_pre_ffn"][0], f).reshape(8, 128).T)
    m["ggla"] = np.ascontiguousarray(np.broadcast_to(np.tile(np.asarray(I["g_gla_out"][0], f), 4)[None], (128, 512)))
    m["gsub"] = np.ascontiguousarray(np.broadcast_to(np.asarray(I["g_subln"][0], f)[None], (128, 128)))
    m["gpm"] = np.ascontiguousarray(np.broadcast_to(np.asarray(I["g_post_mix"][0], f)[None], (128, 1024)))
    m["gpo"] = np.ascontiguousarray(np.broadcast_to(np.asarray(I["g_post_ffn"][0], f)[None], (128, 1024)))
    lam = np.concatenate([np.asarray(I[k][0], f) for k in ("lam_q1", "lam_k1", "lam_q2", "lam_k2")])
    m["lam"] = np.ascontiguousarray(np.broadcast_to(lam[None], (128, 256)))
    for k, v in consts.items():
        m["c_" + k] = v
    return m


_CACHE = {}


def run_cores(I, T, PAST, ncores):
    key = (T, PAST)
    if key not in _CACHE:
        _CACHE[key] = (build_program(T, PAST), make_consts(T, PAST))
    nc, consts = _CACHE[key]
    in_maps = [_core_inputs(c, T, PAST, I, consts) for c in range(ncores)]
    res = run_bass_kernel_spmd(nc, in_maps, core_ids=list(range(ncores)))
    return res.results


def kernel(**inputs):
    T, PAST, NCORES = 4096, 2048, 8
    R = run_cores(inputs, T, PAST, NCORES)
    f = np.float32
    yp = np.stack([np.asarray(r["yp"], f) for r in R], 0)
    ys = np.concatenate([np.asarray(r["ys"], f).reshape(4, 16, 1024) for r in R], 0)
    kp = np.stack([np.asarray(r["kp"], f).reshape(T, 4, 128) for r in R], 0)[None]
    vp = np.stack([np.asarray(r["vp"], f).reshape(T, 4, 128) for r in R], 0)[None]
    gp = np.stack([np.asarray(r["gp"], f).reshape(4, 64, 128) for r in R], 0)[None]
    ksm = np.concatenate([np.asarray(r["ksm"], f).reshape(4, 16, 4, 128) for r in R], 0)[None]
    vsm = np.concatenate([np.asarray(r["vsm"], f).reshape(4, 16, 4, 128) for r in R], 0)[None]
    gs = np.concatenate([np.asarray(r["gs"], f).reshape(4, 4, 64, 128) for r in R], 0)[None]
    return (yp, ys, kp, vp, gp, ksm, vsm, gs)
```

```python
import math
import numpy as np
import ml_dtypes
import concourse.bass as bass
import concourse.mybir as mybir
from concourse.bass_utils import run_bass_kernel_spmd

F32 = mybir.dt.float32
BF16 = mybir.dt.bfloat16
AF = mybir.ActivationFunctionType
ALU = mybir.AluOpType
AX = mybir.AxisListType

EPS = 1e-6
LAM_INIT = 0.8 - 0.6 * math.exp(-0.3 * 0)
NEG = -30000.0
import os as _os_hop
HOP_NS = float(_os_hop.environ.get("KHOP", "250"))
ENGS = ("pe", "act", "dve", "pool", "sp")
COMPUTE = ("pe", "act", "dve", "pool")


class Buf:
    __slots__ = ("name", "w", "rs", "rd", "excl")

    def __init__(self, name):
        self.name = name
        self.excl = False
        self.w = None
        self.rs = []
        self.rd = []


class Op:
    __slots__ = ("eng", "seq", "fn", "dma", "sig", "waits", "sem", "semval", "clk", "id", "deps", "cost", "lat", "phase", "ridx", "tag", "t0", "rdy", "blk", "prev")


class Prog:
    limit = None
    want_tags = False

    def __init__(self, nc, n_dma_sems=28):
        self.nc = nc
        self.pending = []
        self.final = {e: [] for e in ENGS}
        self.gorder = []
        self.nds = n_dma_sems
        self.nid = 0
        self.phase = 0
        self.sched = True

    def add(self, eng, fn, reads=(), writes=(), dma=False, extra_deps=(), cost=100.0, lat=None):
        if self.limit is not None and fn is not None and self.nid >= self.limit:
            return None
        op = Op()
        op.eng = eng; op.fn = fn; op.dma = dma; op.sig = False; op.waits = []
        op.id = self.nid; self.nid += 1
        op.cost = cost; op.lat = (cost + HOP_NS) if lat is None else lat
        op.phase = self.phase
        if self.want_tags:
            import sys as _s
            f = _s._getframe(2)
            op.tag = (f.f_lineno, f.f_back.f_lineno if f.f_back is not None else 0)
        deps = {}
        for b in reads:
            if b.w is not None:
                deps[b.w.id] = b.w
            if b.excl:
                for r in b.rs:
                    if r.eng != eng:
                        deps[r.id] = r
        for b in writes:
            if b.w is not None:
                deps[b.w.id] = b.w
            for r in b.rs:
                deps[r.id] = r
            for r in b.rd:
                deps[r.id] = r
        for p in extra_deps:
            deps[p.id] = p
        deps.pop(op.id, None)
        op.deps = list(deps.values())
        self.pending.append(op)
        for b in reads:
            if dma:
                b.rd.append(op)
            else:
                b.rs.append(op)
        for b in writes:
            b.w = op; b.rs = []; b.rd = []
        return op

    def schedule_phase(self):
        import heapq
        ops = self.pending
        self.pending = []
        n = len(ops)
        for i, op in enumerate(ops):
            op.ridx = i
        succ = [[] for _ in range(n)]
        nd = [0] * n
        for i, op in enumerate(ops):
            for p in op.deps:
                if p.phase == self.phase:
                    succ[p.ridx].append(i)
                    nd[i] += 1
        order = []
        if not self.sched:
            order = list(range(n))
        else:
            ready = {e: [] for e in ENGS}
            busy = {e: False for e in ENGS}
            ev = []
            cnt = [0]

            lastop = {e: None for e in ENGS}
            rdy_t = [0.0] * n
            blk = [None] * n

            def try_start(e, t):
                if busy[e] or not ready[e]:
                    return
                i = inv[heapq.heappop(ready[e])]
                op = ops[i]
                busy[e] = True
                op.t0 = t; op.rdy = rdy_t[i]; op.blk = blk[i]; op.prev = lastop[e]; lastop[e] = op
                order.append(i)
                cnt[0] += 1
                heapq.heappush(ev, (t + op.cost, 0, cnt[0], i))
                heapq.heappush(ev, (t + op.lat, 1, cnt[0], i))

            import os as _os8
            PRI = _os8.environ.get("KPRI", "idx")
            if PRI != "idx":
                bl_ = [0.0] * n
                for i in range(n - 1, -1, -1):
                    m = 0.0
                    for s in succ[i]:
                        if bl_[s] > m:
                            m = bl_[s]
                    bl_[i] = m + ops[i].lat
                W_ = float(PRI)
                key = [(-(bl_[i]) + W_ * i * 100.0) for i in range(n)]
                order_key = sorted(range(n), key=lambda i: key[i])
                rank = [0] * n
                for r_, i in enumerate(order_key):
                    rank[i] = r_
            else:
                rank = list(range(n))
            inv = {}
            for i in range(n):
                inv[rank[i]] = i
            for i in range(n):
                if nd[i] == 0:
                    heapq.heappush(ready[ops[i].eng], rank[i])
            for e in ENGS:
                try_start(e, 0.0)
            while ev:
                t, kind, _, i = heapq.heappop(ev)
                if kind == 0:
                    e = ops[i].eng
                    busy[e] = False
                    try_start(e, t)
                else:
                    for s in succ[i]:
                        nd[s] -= 1
                        rdy_t[s] = t; blk[s] = ops[i]
                        if nd[s] == 0:
                            e = ops[s].eng
                            heapq.heappush(ready[e], rank[s])
                            try_start(e, t)
            assert len(order) == n, (len(order), n)
            self.sim_time = t if n else 0.0
            import os as _os
            if _os.environ.get("KSIM"):
                busy_t = {e: 0.0 for e in ENGS}
                for op in ops:
                    busy_t[op.eng] += op.cost
                print("PHASE %d: sim %.1f us, n=%d, busy(us): %s" % (self.phase, self.sim_time / 1e3, n,
                      " ".join("%s=%.0f" % (e, busy_t[e] / 1e3) for e in ENGS)))
                if self.want_tags and n:
                    last = max(ops, key=lambda o: o.t0 + o.lat)
                    acc = {}
                    o = last
                    steps = 0
                    while o is not None and steps < 200000:
                        steps += 1
                        if o.t0 > o.rdy + 1e-6 and o.prev is not None:
                            nxt = o.prev
                            key = ("ENG", o.eng, o.tag)
                            dt = o.t0 - nxt.t0
                        elif o.blk is not None:
                            nxt = o.blk
                            key = ("DEP", o.eng, o.tag)
                            dt = o.t0 - nxt.t0
                        else:
                            break
                        acc[key] = acc.get(key, 0.0) + dt
                        o = nxt
                    idle = {}
                    prev_end = 0.0
                    for o2 in sorted([o3 for o3 in ops if o3.eng == "pe"], key=lambda o3: o3.t0):
                        gap = o2.t0 - prev_end
                        if gap > 1.0:
                            bt = (o2.blk.eng, o2.blk.tag) if o2.blk is not None else None
                            k2 = (o2.tag, bt)
                            idle[k2] = idle.get(k2, 0.0) + gap
                        prev_end = o2.t0 + o2.cost
                    print("  PE idle total %.1f us; top:" % (sum(idle.values()) / 1e3))
                    for k2, v in sorted(idle.items(), key=lambda kv: -kv[1])[:22]:
                        print("   %8.1f us  pe-op %s  blocked-by %s" % (v / 1e3, k2[0], k2[1]))
                    import os as _os7
                    if _os7.environ.get("KWIN") and self.phase == 1:
                        w0, w1 = [float(x) * 1e3 for x in _os7.environ["KWIN"].split(",")]
                        for o2 in sorted([o3 for o3 in ops if w0 <= o3.t0 <= w1 and o3.eng in ("pe",)], key=lambda o3: o3.t0):
                            print("   t=%9.2f dur=%6.2f %s %s blk=%s" % (o2.t0 / 1e3, o2.cost / 1e3, o2.eng, o2.tag,
                                  (o2.blk.eng, o2.blk.tag) if o2.blk is not None else None))
                    tot = sum(acc.values())
                    print("  critical path total %.1f us; top contributors:" % (tot / 1e3))
                    for k, v in sorted(acc.items(), key=lambda kv: -kv[1])[:28]:
                        print("   %8.1f us  %s" % (v / 1e3, k))
        for i in order:
            op = ops[i]
            self.final[op.eng].append(op)
            self.gorder.append(op)

    def _pseudo(self, eng, deps):
        op = Op()
        op.eng = eng; op.fn = None; op.dma = False; op.sig = False; op.waits = []
        op.id = self.nid; self.nid += 1
        op.deps = list(deps); op.phase = self.phase
        self.final[eng].append(op)
        self.gorder.append(op)

    def _phase_sinks(self):
        lasts = []
        for e in COMPUTE:
            for op in reversed(self.final[e]):
                if op.fn is not None and not op.dma:
                    if op.phase == self.phase:
                        lasts.append(op)
                    break
        dmas = [op for e in ENGS for op in self.final[e] if op.dma and op.phase == self.phase]
        return lasts + dmas

    def barrier(self):
        self.schedule_phase()
        sinks = self._phase_sinks()
        for e in ENGS:
            self._pseudo(e, sinks)
        self.phase += 1

    def finish(self):
        self.schedule_phase()
        sinks = self._phase_sinks()
        for e in ENGS:
            self._pseudo(e, sinks)

    def lower(self):
        clock = {e: {} for e in ENGS}
        dknown = {e: set() for e in ENGS}
        NQ = 12
        dma_cum = [0] * (self.nds + NQ)
        dma_last = [None] * (self.nds + NQ)
        rr = 0
        rr2 = 0
        seqc = {e: 0 for e in ENGS}
        for op in self.gorder:
            eng = op.eng
            op.seq = seqc[eng]; seqc[eng] += 1
            deps = list(op.deps)
            if op.dma:
                if eng == "pool":
                    s = self.nds + rr2
                    rr2 = (rr2 + 1) % NQ
                else:
                    s = rr
                    rr = (rr + 1) % self.nds
                if dma_last[s] is not None:
                    deps.append(dma_last[s])
                dma_cum[s] += 16
                op.sem = s; op.semval = dma_cum[s]
                dma_last[s] = op
            clk = clock[eng]
            dk = dknown[eng]
            for p in deps:
                if p.dma:
                    if p.id in dk:
                        continue
                    op.waits.append(p); dk.add(p.id)
                else:
                    if p.fn is None:
                        continue
                    if p.eng == eng and eng == "pe":
                        continue
                    if clk.get(p.eng, -1) >= p.seq:
                        continue
                    op.waits.append(p); p.sig = True
                    clk[p.eng] = p.seq
                for k, v in p.clk.items():
                    if clk.get(k, -1) < v:
                        clk[k] = v
            op.clk = dict(clk)

    def emit(self):
        nc = self.nc
        self.lower()
        csem = {e: nc.alloc_semaphore(name="c_" + e) for e in COMPUTE}
        dsem = [nc.alloc_semaphore(name="d_%d" % i) for i in range(self.nds + 12)]
        for e in COMPUTE:
            c = 0
            for op in self.final[e]:
                if op.sig:
                    c += 1
                    op.semval = c

        def run(e, eng):
            for op in self.final[e]:
                for p in op.waits:
                    if p.dma:
                        eng.wait_ge(dsem[p.sem], p.semval)
                    else:
                        eng.wait_ge(csem[p.eng], p.semval)
                if op.fn is None:
                    continue
                ins = op.fn(eng)
                if op.dma:
                    ins.then_inc(dsem[op.sem], 16)
                elif op.sig:
                    ins.then_inc(csem[e], 1)

        with nc.Block() as block:
            @block.tensor
            def _(eng):
                run("pe", eng)

            @block.scalar
            def _(eng):
                run("act", eng)

            @block.vector
            def _(eng):
                run("dve", eng)

            @block.gpsimd
            def _(eng):
                run("pool", eng)

            @block.sync
            def _(eng):
                run("sp", eng)


class Tl:
    __slots__ = ("t", "b", "off")

    def __init__(self, t, name):
        self.t = t
        self.b = Buf(name)


class Ring:
    def __init__(self, tiles):
        self.tiles = tiles
        self.i = 0

    def next(self):
        t = self.tiles[self.i % len(self.tiles)]
        self.i += 1
        return t


SLOPES = [2.0 ** (-2.0 * (h + 1)) for h in range(4)]
QG = 256


def make_consts(T, PAST):
    bf = ml_dtypes.bfloat16
    NT = T // 128
    NKT = PAST // 128
    c = {}
    c["ident"] = np.eye(128, dtype=np.float32).astype(bf)
    s = np.arange(128)
    same = (s[:, None] // 64) == (s[None, :] // 64)
    c["triLE"] = (np.where(same & (s[:, None] <= s[None, :]), -1.0 / 16, 0.0)).astype(np.float32)
    c["triGT"] = (np.where(same & (s[:, None] > s[None, :]), -1.0 / 16, 0.0)).astype(np.float32)
    c["ind"] = (np.where((s[:, None] // 64) == np.arange(2)[None, :], -1.0 / 16, 0.0)).astype(np.float32)
    c["maskA"] = np.where(same & (s[:, None] <= s[None, :]), 1.0, 0.0).astype(np.float32)
    c["cm"] = np.where((s[:, None] // 64) == np.arange(2)[None, :], 1.0, 0.0).astype(np.float32)
    u = np.arange(64)
    same16 = (u[:, None] // 16) == (u[None, :] // 16)
    c["triLE_s"] = np.where(same16 & (u[:, None] <= u[None, :]), -1.0 / 16, 0.0).astype(np.float32)
    c["triGT_s"] = np.where(same16 & (u[:, None] > u[None, :]), -1.0 / 16, 0.0).astype(np.float32)
    c["ind_s"] = np.where((u[:, None] // 16) == np.arange(4)[None, :], -1.0 / 16, 0.0).astype(np.float32)
    c["rm_s"] = np.where((u[:, None] // 16) == np.arange(4)[None, :], 1.0, 0.0).astype(np.float32)
    c["maskA_s"] = np.where(same16 & (u[:, None] <= u[None, :]), 1.0, 0.0).astype(np.float32)
    corr = np.zeros((128, 4, 128), np.float32)
    k = s[:, None]; q = s[None, :]
    kc = k // 64; qc = q // 64
    for h in range(4):
        m = np.zeros((128, 128), np.float32)
        m = np.where((kc == qc) & (k > q), -2.0 * SLOPES[h] * (k - q), m)
        m = np.where(kc > qc, NEG, m)
        corr[:, h, :] = m
    c["corr"] = corr.reshape(128, 512).astype(bf)
    bt = np.zeros((128, 4, NT + 2), np.float32)
    for h in range(4):
        for idx in range(NT + 2):
            bt[:, h, idx] = SLOPES[h] * (s - 128.0 * (idx - 1))
    c["btab"] = bt.reshape(128, 4 * (NT + 2))
    sbt = np.zeros((128, 4, NKT), np.float32)
    for h in range(4):
        for kt in range(NKT):
            sbt[:, h, kt] = SLOPES[h] * (kt * 128 + s - PAST)
    c["sbias"] = sbt.reshape(128, 4 * NKT)
    nb = np.zeros((64, 4, 2, 64), np.float32)
    kb = u[:, None] // 16; kj = u[:, None] % 16
    qb = u[None, :] // 16; qi = u[None, :] % 16
    for h in range(4):
        m = np.where(kb == qb, -SLOPES[h] * np.abs(qi - kj) + SLOPES[h] * qi, NEG)
        nb[:, h, 0, :] = m
        nb[:, h, 1, :] = m
    c["nbias"] = nb.reshape(64, 512)
    return c


CONST_DT = {"ident": BF16, "corr": BF16}


class Arena:
    def __init__(self, nc, n):
        self.t = nc.alloc_sbuf_tensor("arena", [128, n], BF16)
        self.ap = self.t[:, :]
        self.n = n
        self.off = 0

    def alloc(self, name, shape, dt=BF16):
        free = 1
        for d in shape[1:]:
            free *= d
        nel = free if dt == BF16 else free * 2
        nal = (nel + 31) // 32 * 32
        assert self.off + nal <= self.n, ("arena overflow", name, self.off, nal, self.n)
        ap = self.ap[0:shape[0], self.off:self.off + nel]
        if dt == F32:
            ap = ap.bitcast(F32)
        if len(shape) > 2:
            names = "abcdef"[:len(shape) - 1]
            pat = "p (%s) -> p %s" % (" ".join(names), " ".join(names))
            kw = {names[i]: shape[1 + i] for i in range(len(shape) - 2)}
            ap = ap.rearrange(pat, **kw)
        tl = Tl(ap, name)
        tl.off = self.off
        self.off += nal
        return tl


def build_program(T, PAST, stop=None):
    NT = T // 128
    NG = T // QG
    NKT = PAST // 128
    assert NKT % 8 == 0 and T % QG == 0
    nc = bass.Bass("TRN2", target_bir_lowering=False)
    P = Prog(nc)
    if stop is not None and stop.startswith("n"):
        P.limit = int(stop[1:])

    def din(name, shape, dt=F32):
        return nc.dram_tensor(name, list(shape), dt, kind="ExternalInput").ap()

    def dout(name, shape, dt=F32):
        return nc.dram_tensor(name, list(shape), dt, kind="ExternalOutput").ap()

    xp = din("xp", [T, 1024]); xs = din("xs", [64, 1024])
    ck = din("ck", [4, PAST, 512]); cv = din("cv", [4, PAST, 512]); sg = din("sg", [4, 2, 128, 128])
    win_d = din("win", [128, 8 * 3088]); wout_d = din("wout", [128, 8 * 1024])
    wup_d = din("wup", [128, 8 * 4096]); wdn_d = din("wdn", [128, 32 * 1024])
    wgu_d = din("wgu", [17, 256]); gpre_d = din("gpre", [128, 8]); gpf_d = din("gpf", [128, 8])
    ggla_d = din("ggla", [128, 512]); gsub_d = din("gsub", [128, 128])
    gpm_d = din("gpm", [128, 1024]); gpo_d = din("gpo", [128, 1024]); lam_d = din("lam", [128, 256])
    cshape = {"ident": [128, 128], "triLE": [128, 128], "triGT": [128, 128], "ind": [128, 2],
              "maskA": [128, 128], "cm": [128, 2], "triLE_s": [64, 64], "triGT_s": [64, 64], "ind_s": [64, 4],
              "rm_s": [64, 4], "maskA_s": [64, 64], "corr": [128, 512], "btab": [128, 4 * (NT + 2)],
              "sbias": [128, 4 * NKT], "nbias": [64, 512]}
    cd = {k: din("c_" + k, v, CONST_DT.get(k, F32)) for k, v in cshape.items()}
    yp_d = dout("yp", [T, 1024]); ys_d = dout("ys", [64, 1024])
    kp_d = dout("kp", [T, 512]); vp_d = dout("vp", [T, 512]); gp_d = dout("gp", [2, 128, 128])
    ksm_d = dout("ksm", [64, 512]); vsm_d = dout("vsm", [64, 512]); gs_d = dout("gs", [4, 2, 128, 128])
    x1s = nc.dram_tensor("x1s", [T + 64, 1024], F32).ap()
    wupb = nc.dram_tensor("wupb", [128, 8 * 4096], BF16).ap()
    wdnb = nc.dram_tensor("wdnb", [128, 32 * 1024], BF16).ap()
    x1sb = [Buf("x1s%d" % i) for i in range(NT + 1)]

    def sb(name, shape, dt=F32):
        return Tl(nc.alloc_sbuf_tensor("s_" + name, list(shape), dt), name)

    C = {k: sb("k_" + k, v, CONST_DT.get(k, F32)) for k, v in cshape.items() if k != "nbias"}
    wg_aug = sb("wg_aug", [17, 256], BF16)
    gpre = sb("gpre", [128, 8]); gpf = sb("gpf", [128, 8])
    ggla = sb("ggla", [128, 512]); gsub8 = sb("gsub8", [128, 128])
    gpm = sb("gpm", [128, 1024])
    neg_lam = sb("neg_lam", [128, 1]); lam_s = sb("lam_s", [128, 4])
    C_eps = sb("c_eps", [128, 1]); C_one = sb("c_one", [128, 1])
    stat = Ring([sb("stat%d" % i, [128, 8]) for i in range(24)])
    agT = sb("agT", [17, 128], BF16)
    S32 = [sb("S32_%d" % p, [128, 128]) for p in range(2)]
    Sbf = [Ring([sb("Sbf%d_%d" % (p, v), [128, 128], BF16) for v in range(2)]) for p in range(2)]

    import os as _os3
    NA = (nc.sbuf_bytes_remaining - 1024) // 2 // 32 * 32 + int(_os3.environ.get("FAKE_NA", "0"))
    RD = {}
    for kv in _os3.environ.get("RINGS", "").split(","):
        if "=" in kv:
            RD[kv.split("=")[0]] = int(kv.split("=")[1])
    AR = Arena(nc, NA)
    W = {}

    RD_on = [False]

    def wring(name, n, shape, dt=BF16):
        if RD_on[0]:
            n = RD.get(name, n)
        W[name] = Ring([AR.alloc("%s%d" % (name, i), shape, dt) for i in range(n)])

    def wtile(name, shape, dt=BF16):
        W[name] = AR.alloc(name, shape, dt)

    def alloc_common(smp=False):
        wring("xin", 1, [128, 1024], F32)
        wring("b1k", 3, [128, 1024])
        wring("zqk", 1, [128, 512], F32); wring("vg", 2, [128, 512])
        wring("Eg", 1, [128, 512], F32); wring("rr", 2, [128, 512], F32)
        wring("qd_bf", 1, [128, 512]); wring("kd_f", 1, [128, 512], F32); wring("kd_bf", 1, [128, 512])
        wring("vd_f", 1, [128, 512], F32); wring("ag_bf", 1, [128, 16])
        wring("Lg", 1, [128, 256], F32)
        wring("ex", 2, [128, 256], F32)
        wring("ebL", 1, [128, 8], F32)
        wring("qt", 1, [128, 256]); wring("kt", 1, [128, 256]); wring("kh", 1, [128, 2, 256])
        wring("qz", 1, [128, 4, 128]); wring("kTg", 1, [128, 2, 128]); wring("ATm", 4, [128, 128])
        for t_ in W["qz"].tiles:
            MEMSET("pool", t_.t[:, :, :], 0.0, [t_])
        wring("ocat", 1 if smp else 3, [128, 1024])
        wring("t1", 1, [128, 2, 128], F32); wring("od", 1, [128, 2, 128], F32)
        wring("ytmp", 1, [128, 1024], F32); wring("xre", 1, [128, 1024], F32)

    banks = [Tl(nc.alloc_psum_tensor("bank%d" % i, [128, 512], F32), "bank%d" % i) for i in range(8)]
    for b_ in banks:
        b_.b.excl = True

    def bfview(tl):
        v = Tl(tl.t[:, :].bitcast(BF16).rearrange("p (k n) -> p k n", k=8), tl.b.name + "_bf")
        v.b = tl.b
        return v

    tp = bfview(banks[0])
    import os as _os
    NZP = int(_os.environ.get("NZP", "3"))
    zp = Ring([banks[1], banks[2]] + ([banks[7]] if NZP == 3 else []))
    scr = zp if NZP != 4 else Ring([banks[7]])
    if NZP == 9:
        class FakeRing:
            def __init__(self, tl, n, view=None):
                self.tiles = []
                for i in range(n):
                    t_ = Tl(tl.t, "fk")
                    t_.b.excl = True
                    self.tiles.append(t_)
                self.i = 0
            def next(self):
                t_ = self.tiles[self.i % len(self.tiles)]; self.i += 1
                return t_
        zp = FakeRing(banks[1], int(_os.environ.get("FZP", "6")))
        scr = FakeRing(banks[2], int(_os.environ.get("FSC", "4"))) if _os.environ.get("FSC", "4") != "0" else zp
    import os as _os5
    if True:
        if _os5.environ.get("FAKETP"):
            class TPProxy:
                def __init__(self, base, n):
                    self.t = base.t
                    self.bufs = [Buf("ftp%d" % i) for i in range(n)]
                    for b_ in self.bufs:
                        b_.excl = True
                    self.k = 0
                    self.cur = self.bufs[0]
                def rotate(self):
                    self.k += 1
                    self.cur = self.bufs[self.k % len(self.bufs)]
                @property
                def b(self):
                    return self.cur
            tp = TPProxy(tp, int(_os5.environ.get("FAKETP")))
    if NZP == 5:
        scr = Ring([Tl(banks[1].t, "fake_sc0"), Tl(banks[2].t, "fake_sc1")])
        for t_ in scr.tiles:
            t_.b.excl = True
        zp = Ring([banks[1], banks[2], banks[7]])
    gU = banks[3]; og = banks[4]
    accb = [banks[5], banks[6]]
    gAT = banks[7]
    tpB = bfview(banks[7])

    def bl(xs_):
        return [x.b if hasattr(x, "b") else x for x in xs_]

    def fsz(ap):
        n = 1
        for d in ap.shape[1:]:
            n *= d
        return n

    def ecost(eng, n):
        if eng == "act":
            return n / 1.2 + 200.0
        if eng == "dve":
            return n * 1.04 + 100.0
        return n * 3.0 + 300.0

    def MM(out, lhsT, rhs, start, stop, rd, wr, skip=True):
        c = max(fsz(rhs), 64) / 2.2 * (4.0 if rhs.dtype == F32 else 1.0) + 45.0
        return P.add("pe", lambda e: e.matmul(out, lhsT=lhsT, rhs=rhs, start=start, stop=stop,
                                              skip_group_check=skip), bl(rd), bl(wr), cost=c, lat=c + HOP_NS)

    def TR(out, in_, idn, rd, wr):
        return P.add("pe", lambda e: e.transpose(out, in_, idn), bl(rd), bl(wr), cost=120.0, lat=120.0 + HOP_NS)

    def ACT(out, in_, func, rd, wr, bias=None, scale=None, accum=None):
        kw = {}
        if bias is not None:
            kw["bias"] = bias
        if scale is not None:
            kw["scale"] = scale
        if accum is not None:
            kw["accum_out"] = accum
        c = ecost("act", fsz(in_)) + (60.0 if accum is not None else 0.0)
        return P.add("act", lambda e: e.activation(out=out, in_=in_, func=func, **kw), bl(rd), bl(wr), cost=c)

    def AMUL(out, in_, m, rd, wr):
        return P.add("act", lambda e: e.mul(out=out, in_=in_, mul=m), bl(rd), bl(wr), cost=ecost("act", fsz(in_)))

    def CP(eng, out, in_, rd, wr):
        c = ecost(eng, fsz(in_))
        if hasattr(tp, "rotate") and any(x is tp for x in rd):
            rd = [x.b if x is tp else x for x in rd]
            tp.rotate()
        if eng == "act":
            return P.add("act", lambda e: e.copy(out=out, in_=in_), bl(rd), bl(wr), cost=c)
        return P.add(eng, lambda e: e.tensor_copy(out, in_), bl(rd), bl(wr), cost=c)

    def TS(eng, out, in0, s1, s2, op0, op1, rd, wr):
        c = ecost(eng, fsz(in0))
        if eng == "act":
            assert op0 == ALU.mult and op1 is None
            return P.add("act", lambda e: e.activation(out=out, in_=in0, func=AF.Copy, scale=s1), bl(rd), bl(wr), cost=c)
        if op1 is None:
            return P.add(eng, lambda e: e.tensor_scalar(out, in0, s1, None, op0), bl(rd), bl(wr), cost=c)
        return P.add(eng, lambda e: e.tensor_scalar(out, in0, s1, s2, op0, op1), bl(rd), bl(wr), cost=c)

    def TT(eng, out, in0, in1, op, rd, wr):
        return P.add(eng, lambda e: e.tensor_tensor(out, in0, in1, op), bl(rd), bl(wr), cost=ecost(eng, fsz(in0)))

    def STT(eng, out, in0, scalar, in1, op0, op1, rd, wr):
        return P.add(eng, lambda e: e.scalar_tensor_tensor(out, in0, scalar, in1, op0, op1), bl(rd), bl(wr),
                     cost=ecost(eng, fsz(in0)))

    def RECIP(out, in_, rd, wr):
        return P.add("dve", lambda e: e.reciprocal(out, in_), bl(rd), bl(wr), cost=ecost("dve", fsz(in_)) * 2.0)

    def MEMSET(eng, ap, val, wr):
        return P.add(eng, lambda e: e.memset(ap, val), (), bl(wr), cost=ecost(eng, fsz(ap)) * 0.5)

    import os as _os9
    STORE_Q = _os9.environ.get("STOREQ", "sp")

    def DMA(q, out, in_, rd, wr):
        nbytes = fsz(out) * out.shape[0] * (4 if out.dtype == F32 else 2)
        is_store = "DRam" in type(out.tensor).__name__
        if is_store:
            q = STORE_Q
        return P.add(q, lambda e: e.dma_start(out=out, in_=in_), bl(rd), bl(wr), dma=True,
                     cost=(600.0 if q == "pool" else 120.0), lat=2000.0 + nbytes / 100.0)

    def SQ(dst_ap, dst_tl, in_, rd, sa, accum):
        ACT(dst_ap, in_, AF.Square, rd, [dst_tl, sa], accum=accum)

    ident = C["ident"]
    _alt = [0]

    def alt2():
        _alt[0] += 1
        return "act" if _alt[0] % 2 else "dve"

    def rstd_from_ss(ssap, n, nt, dst, tl):
        ACT(dst, ssap, AF.Ln, [tl, C_eps], [tl], bias=C_eps.t[0:nt, 0:1], scale=1.0 / n)
        ACT(dst, dst, AF.Exp, [tl], [tl], scale=-0.5)

    Win = AR.alloc("Win", [128, 8, 3088]); Wout = AR.alloc("Wout", [128, 8, 1024])
    A_FIX = AR.off
    kTs = [AR.alloc("kTs%d" % b, [128, PAST]) for b in range(4)]
    Vs = [AR.alloc("Vs%d" % b, [128, NKT, 129]) for b in range(4)]
    alloc_common(True)
    wring("cst", 5, [128, 8, 128], F32)
    wring("kcs", 3, [128, 8, 128])
    S32s = [[AR.alloc("S32s%d_%d" % (b, p), [128, 128], F32) for p in range(2)] for b in range(4)]
    Sbfs = [[AR.alloc("Sbfs%d_%d" % (b, p), [128, 128]) for p in range(2)] for b in range(4)]
    wtile("qm", [128, 4, 4, 64]); wtile("khm", [64, 4, 256])
    wtile("qTs", [128, 2, 4, 64]); wtile("kTn", [128, 4, 64]); wtile("Vsn", [64, 4, 129])
    wring("PTs", 2, [128, 4, 2, 64])
    wtile("PTn", [64, 128]); wtile("sctmp", [64, 128], F32)
    wtile("lam_in", [128, 256], F32); wtile("lam_tmp", [128, 128], F32)
    wtile("nbias", [64, 512], F32)
    C["nbias"] = W["nbias"]

    MEMSET("dve", C_eps.t[:, :], EPS, [C_eps]); MEMSET("dve", C_one.t[:, :], 1.0, [C_one])
    for k in C:
        DMA("sp", C[k].t[:, :], cd[k], [], [C[k]])
    DMA("sp", gpre.t[:, :], gpre_d, [], [gpre]); DMA("sp", gpf.t[:, :], gpf_d, [], [gpf])
    DMA("sp", ggla.t[:, :], ggla_d, [], [ggla]); DMA("sp", gpm.t[:, :], gpm_d, [], [gpm])
    lam_in = W["lam_in"]; lam_tmp = W["lam_tmp"]
    DMA("sp", lam_in.t[:, :], lam_d, [], [lam_in])

    def stage_f32(src_ap, ncols):
        c_ = W["cst"].next()
        ap = c_.t[:, :, :].rearrange("p a b -> p (a b)")[:, 0:ncols]
        DMA("sp", ap, src_ap, [], [c_])
        return c_, ap

    c_ = W["cst"].next()
    ap17 = c_.t[0:17, 0:2, :].rearrange("p a b -> p (a b)")
    DMA("sp", ap17, wgu_d, [], [c_])
    CP("dve", wg_aug.t[:, :], ap17, [c_], [wg_aug])
    c_, ap = stage_f32(gsub_d, 128)
    TS("dve", gsub8.t[:, :], ap, 1.0 - LAM_INIT, None, ALU.mult, None, [c_], [gsub8])
    TT("dve", lam_tmp.t[:, 0:64], lam_in.t[:, 0:64], lam_in.t[:, 64:128], ALU.mult, [lam_in], [lam_tmp])
    TT("dve", lam_tmp.t[:, 64:128], lam_in.t[:, 128:192], lam_in.t[:, 192:256], ALU.mult, [lam_in], [lam_tmp])
    P.add("dve", lambda e: e.reduce_sum(lam_s.t[:, 0:2], lam_tmp.t[:, :].rearrange("p (a b) -> p a b", a=2), AX.X),
          bl([lam_tmp]), bl([lam_s]))
    ACT(lam_s.t[:, 2:4], lam_s.t[:, 0:2], AF.Exp, [lam_s], [lam_s])
    TT("dve", neg_lam.t[:, :], lam_s.t[:, 3:4], lam_s.t[:, 2:3], ALU.subtract, [lam_s], [neg_lam])
    TS("dve", neg_lam.t[:, :], neg_lam.t[:, :], -LAM_INIT, None, ALU.add, None, [neg_lam], [neg_lam])
    k_ = 0
    for kc in range(8):
        for c0 in range(0, 3088, 1024):
            c1 = min(3088, c0 + 1024)
            c_, ap = stage_f32(win_d[:, kc * 3088 + c0:kc * 3088 + c1], c1 - c0)
            e_ = ("dve", "act")[k_ % 2]; k_ += 1
            TS(e_, Win.t[:, kc, c0:c1], ap, gpre.t[:, kc:kc + 1], None, ALU.mult, None, [c_, gpre], [Win])
    for kc in range(8):
        c_, ap = stage_f32(wout_d[:, kc * 1024:(kc + 1) * 1024], 1024)
        CP(("act", "dve")[kc % 2], Wout.t[:, kc, :], ap, [c_], [Wout])
    MEMSET("pool", agT.t[:, :], 1.0, [agT])
    for p in range(2):
        MEMSET("pool", S32[p].t[:, :], 0.0, [S32[p]])
        for v in range(2):
            MEMSET("pool", Sbf[p].tiles[v].t[:, :], 0.0, [Sbf[p].tiles[v]])
    MEMSET("pool", W["qm"].t[:, :, :, :], 0.0, [W["qm"]])
    MEMSET("pool", W["qTs"].t[:, :, :, :], 0.0, [W["qTs"]])
    for v in range(2):
        MEMSET("pool", W["PTs"].tiles[v].t[:, :, :, :], 0.0, [W["PTs"].tiles[v]])

    def early():
        P.finish()
        P.emit()
        return nc

    if stop == "prep":
        return early()

    def stage_AD(x_rows, nt, st):
        xi = W["xin"].next()
        DMA("sp", xi.t[0:nt, :], x_rows, [], [xi])
        sa = stat.next()
        xn_ = W["b1k"].next()
        SQ(xn_.t[0:nt, :], xn_, xi.t[0:nt, :], [xi], sa, sa.t[0:nt, 0:1])
        rstd_from_ss(sa.t[0:nt, 0:1], 1024.0, nt, sa.t[0:nt, 1:2], sa)
        TS("dve", xn_.t[0:nt, :], xi.t[0:nt, :], sa.t[0:nt, 1:2], None, ALU.mult, None, [xi, sa], [xn_])
        for kc in range(8):
            TR(tp.t[:, kc, 0:nt], xn_.t[0:nt, kc * 128:(kc + 1) * 128], ident.t[0:nt, 0:nt], [xn_, ident], [tp])
        h_ = W["b1k"].next()
        h3 = h_.t[:, :].rearrange("p (k n) -> p k n", k=8)
        CP("dve", h3[:, :, 0:nt], tp.t[:, :, 0:nt], [tp], [h_])

        def proj(c0, c1):
            z = zp.next()
            for kc in range(8):
                MM(z.t[0:nt, 0:c1 - c0], h3[:, kc, 0:nt], Win.t[:, kc, c0:c1], kc == 0, kc == 7, [h_, Win], [z])
            return z

        z = proj(0, 16)
        ag = W["ag_bf"].next()
        CP("dve", ag.t[0:nt, :], z.t[0:nt, 0:16], [z], [ag])
        TR(tp.t[0:16, 0, 0:nt], ag.t[0:nt, 0:16], ident.t[0:nt, 0:nt], [ag, ident], [tp])
        CP("act", agT.t[0:16, 0:nt], tp.t[0:16, 0, 0:nt], [tp], [agT])
        z = zp.next()
        MM(z.t[0:nt, 0:256], agT.t[0:17, 0:nt], wg_aug.t[0:17, 0:256], True, True, [agT, wg_aug], [z])
        L = W["Lg"].next()
        ACT(L.t[0:nt, :], z.t[0:nt, 0:256], AF.Exp, [z], [L], scale=-1.0)
        ACT(L.t[0:nt, :], L.t[0:nt, :], AF.Ln, [L, C_one], [L], bias=C_one.t[0:nt, 0:1])
        st["L"] = L
        z = proj(16, 528)
        zq = W["zqk"].next()
        CP("dve", zq.t[0:nt, :], z.t[0:nt, :], [z], [zq])
        st["zqk"] = zq
        z = proj(528, 1040)
        v_ = W["vg"].next()
        CP("dve", v_.t[0:nt, :], z.t[0:nt, :], [z], [v_])
        st["vg"] = v_
        z = proj(1040, 1552)
        E_ = W["Eg"].next(); r_ = W["rr"].next()
        ACT(E_.t[0:nt, :], z.t[0:nt, :], AF.Exp, [z], [E_], scale=-1.0)
        TT("dve", r_.t[0:nt, :], z.t[0:nt, :], ggla.t[0:nt, :], ALU.mult, [z, ggla], [r_])
        TS("dve", E_.t[0:nt, :], E_.t[0:nt, :], 1.0, None, ALU.add, None, [E_], [E_])
        RECIP(E_.t[0:nt, :], E_.t[0:nt, :], [E_], [E_])
        TT("dve", r_.t[0:nt, :], r_.t[0:nt, :], E_.t[0:nt, :], ALU.mult, [r_, E_], [r_])
        st["G2"] = r_
        z = proj(1552, 2064)
        qd = W["qd_bf"].next()
        TS("dve", qd.t[0:nt, :], z.t[0:nt, :], 0.125, None, ALU.mult, None, [z], [qd])
        z = proj(2064, 2576)
        kf = W["kd_f"].next(); kb = W["kd_bf"].next()
        CP("act", kf.t[0:nt, :], z.t[0:nt, :], [z], [kf])
        CP("dve", kb.t[0:nt, :], kf.t[0:nt, :], [kf], [kb])
        z = proj(2576, 3088)
        vf = W["vd_f"].next()
        CP("act", vf.t[0:nt, :], z.t[0:nt, :], [z], [vf])
        st["qd"] = qd; st["kf"] = kf; st["kb"] = kb; st["vf"] = vf

    def stage_gla_gates(nt, smp, st):
        L = st["L"]; zq = st["zqk"]
        z = zp.next()
        tle = C["triLE_s"] if smp else C["triLE"]; tgt = C["triGT_s"] if smp else C["triGT"]
        ind = C["ind_s"] if smp else C["ind"]
        ni = 4 if smp else 2
        gbL = zp.next()
        MM(z.t[0:nt, 0:256], tle.t[0:nt, 0:nt], L.t[0:nt, :], True, True, [tle, L], [z])
        MM(z.t[0:nt, 256:512], tgt.t[0:nt, 0:nt], L.t[0:nt, :], True, True, [tgt, L], [z])
        for p in range(2):
            MM(gbL.t[:, p * ni:(p + 1) * ni], L.t[0:nt, p * 128:(p + 1) * 128], ind.t[0:nt, 0:ni], True, True,
               [L, ind], [gbL])
        ebL_ = W["ebL"].next()
        q_ = W["qt"].next(); k_ = W["kt"].next(); kh = W["kh"].next()
        eb_ = W["ex"].next()
        ACT(eb_.t[0:nt, :], z.t[0:nt, 0:256], AF.Exp, [z], [eb_])
        STT("dve", q_.t[0:nt, :], zq.t[0:nt, 0:256], 0.125, eb_.t[0:nt, :], ALU.mult, ALU.mult, [zq, eb_], [q_])
        enb_ = W["ex"].next()
        ACT(enb_.t[0:nt, :], z.t[0:nt, 0:256], AF.Exp, [z], [enb_], scale=-1.0)
        TT("dve", k_.t[0:nt, :], zq.t[0:nt, 256:512], enb_.t[0:nt, :], ALU.mult, [zq, enb_], [k_])
        ec_ = W["ex"].next()
        ACT(ec_.t[0:nt, :], z.t[0:nt, 256:512], AF.Exp, [z], [ec_])
        ACT(ebL_.t[:, 0:2 * ni], gbL.t[:, 0:2 * ni], AF.Exp, [gbL], [ebL_])
        if smp:
            TT("pool", kh.t[0:nt, 0, :], zq.t[0:nt, 256:512], ec_.t[0:nt, :], ALU.mult, [zq, ec_], [kh])
        else:
            for c in range(2):
                STT("dve", kh.t[:, c, :], zq.t[:, 256:512], C["cm"].t[:, c:c + 1], ec_.t[:, :],
                    ALU.mult, ALU.mult, [zq, ec_, C["cm"]], [kh])
        for p in range(2):
            TR(tp.t[:, p, 0:nt], q_.t[0:nt, p * 128:(p + 1) * 128], ident.t[0:nt, 0:nt], [q_, ident], [tp])
            TR(tp.t[:, 2 + p, 0:nt], k_.t[0:nt, p * 128:(p + 1) * 128], ident.t[0:nt, 0:nt], [k_, ident], [tp])
        qz = W["qz"].next(); kTg = W["kTg"].next()
        qz4 = qz.t[:, :, :].rearrange("p (a b) t -> p a b t", b=2)
        for hp in range(2):
            CP("act", qz4[64 * hp:64 * hp + 64, :, hp, 0:nt], tp.t[64 * hp:64 * hp + 64, 0:2, 0:nt], [tp], [qz])
        CP("dve", kTg.t[:, :, 0:nt], tp.t[:, 2:4, 0:nt], [tp], [kTg])
        st["qz"] = qz; st["kTg"] = kTg; st["kh"] = kh; st["ebL"] = ebL_

    def gla_intra(nt, st, maskA):
        qz = st["qz"]; kTg = st["kTg"]; v_ = st["vg"]
        gAT_ = zp.next() if NZP == 3 else gAT
        for h in range(4):
            MM(gAT_.t[0:nt, h * 128:h * 128 + nt], kTg.t[:, h // 2, 0:nt], qz.t[:, h, 0:nt],
               True, True, [qz, kTg], [gAT_])
        ats = []
        for h in range(4):
            a_ = W["ATm"].next()
            TT("dve", a_.t[0:nt, 0:nt], gAT_.t[0:nt, h * 128:h * 128 + nt], maskA.t[0:nt, 0:nt], ALU.mult,
               [gAT_, maskA], [a_])
            ats.append(a_)
        for h in range(4):
            MM(og.t[0:nt, h * 128:(h + 1) * 128], ats[h].t[0:nt, 0:nt], v_.t[0:nt, h * 128:(h + 1) * 128],
               h == 0, False, [ats[h], v_], [og])

    def gla_out_norm(nt, st, oc):
        sa = stat.next()
        for h in range(4):
            SQ(oc.t[0:nt, h * 128:(h + 1) * 128], oc, og.t[0:nt, h * 128:(h + 1) * 128], [og], sa, sa.t[0:nt, h:h + 1])
        rstd_from_ss(sa.t[0:nt, 0:4], 128.0, nt, sa.t[0:nt, 4:8], sa)
        G2 = st["G2"]
        for h in range(4):
            STT("dve", oc.t[0:nt, h * 128:(h + 1) * 128], og.t[0:nt, h * 128:(h + 1) * 128], sa.t[0:nt, 4 + h:5 + h],
                G2.t[0:nt, h * 128:(h + 1) * 128], ALU.mult, ALU.mult, [og, sa, G2], [oc])

    def stage_gla_prompt(st, oc):
        nt = 128
        qz = st["qz"]; kh = st["kh"]; v_ = st["vg"]; ebL_ = st["ebL"]
        for c in range(2):
            for h in range(4):
                p, r0 = h // 2, 64 * (h % 2)
                MM(gU.t[r0:r0 + 64, (p * 2 + c) * 128:(p * 2 + c + 1) * 128],
                   kh.t[:, c, h * 64:(h + 1) * 64], v_.t[:, h * 128:(h + 1) * 128],
                   True, True, [kh, v_], [gU])
        gla_intra(nt, st, C["maskA"])
        for c in range(2):
            cur = [Sbf[p].tiles[Sbf[p].i % 2] for p in range(2)]
            for h in range(4):
                p, r0 = h // 2, 64 * (h % 2)
                MM(og.t[64 * c:64 * c + 64, h * 128:(h + 1) * 128], qz.t[:, h, 64 * c:64 * c + 64],
                   cur[p].t[:, :], False, (c == 1), [qz, cur[p]], [og])
            for p in range(2):
                Sbf[p].i += 1
                nxt = Sbf[p].tiles[Sbf[p].i % 2]
                STT("dve", S32[p].t[:, :], S32[p].t[:, :], ebL_.t[:, 2 * p + c:2 * p + c + 1],
                    gU.t[:, (p * 2 + c) * 128:(p * 2 + c + 1) * 128], ALU.mult, ALU.add, [S32[p], ebL_, gU], [S32[p]])
                CP("dve", nxt.t[:, :], S32[p].t[:, :], [S32[p]], [nxt])
        gla_out_norm(nt, st, oc)

    def attn_epilogue(nt, bank, oc, hh, t_ap, o_ap, t_tl, o_tl):
        a1 = bank.t[0:nt, 0:129]; a2 = bank.t[0:nt, 129:258]
        sums = bank.t[0:nt, 0:258].rearrange("p (s n) -> p s n", s=2)[:, :, 128:129]
        sa = stat.next()
        P.add("dve", lambda e: e.reciprocal(sa.t[0:nt, 0:2].rearrange("p (s n) -> p s n", s=2), sums), bl([bank]), bl([sa]),
              cost=150.0)
        TT("dve", sa.t[0:nt, 2:3], sa.t[0:nt, 1:2], neg_lam.t[0:nt, 0:1], ALU.mult, [sa, neg_lam], [sa])
        TS("dve", t_ap, a1[:, 0:128], sa.t[0:nt, 0:1], None, ALU.mult, None, [bank, sa], [t_tl])
        STT("dve", o_ap, a2[:, 0:128], sa.t[0:nt, 2:3], t_ap, ALU.mult, ALU.add, [bank, sa, t_tl], [o_tl])
        SQ(oc.t[0:nt, 512 + hh * 128:512 + (hh + 1) * 128], oc, o_ap, [o_tl], sa, sa.t[0:nt, 3:4])
        rstd_from_ss(sa.t[0:nt, 3:4], 128.0, nt, sa.t[0:nt, 4:5], sa)
        STT("dve", oc.t[0:nt, 512 + hh * 128:512 + (hh + 1) * 128], o_ap, sa.t[0:nt, 4:5],
            gsub8.t[0:nt, :], ALU.mult, ALU.mult, [o_tl, sa, gsub8], [oc])

    def stage_wout(nt, oc, x_rows, srow, sbuf_i):
        for kc in range(8):
            TR(tp.t[:, kc, 0:nt], oc.t[0:nt, kc * 128:(kc + 1) * 128], ident.t[0:nt, 0:nt], [oc, ident], [tp])
        o_ = W["b1k"].next()
        o3 = o_.t[:, :].rearrange("p (k n) -> p k n", k=8)
        CP("dve", o3[:, :, 0:nt], tp.t[:, :, 0:nt], [tp], [o_])
        xr = W["xre"].next()
        DMA("sp", xr.t[0:nt, :], x_rows, [], [xr])
        ys_ = []
        for n in range(2):
            z = zp.next()
            for kc in range(8):
                MM(z.t[0:nt, :], o3[:, kc, 0:nt], Wout.t[:, kc, n * 512:(n + 1) * 512], kc == 0, kc == 7, [o_, Wout], [z])
            ys_.append(z)
        sa = stat.next()
        yt = W["ytmp"].next()
        for n in range(2):
            SQ(yt.t[0:nt, n * 512:(n + 1) * 512], yt, ys_[n].t[0:nt, :], [ys_[n]], sa, sa.t[0:nt, n:n + 1])
        TT("dve", sa.t[0:nt, 2:3], sa.t[0:nt, 0:1], sa.t[0:nt, 1:2], ALU.add, [sa], [sa])
        rstd_from_ss(sa.t[0:nt, 2:3], 1024.0, nt, sa.t[0:nt, 3:4], sa)
        for n in range(2):
            STT("dve", yt.t[0:nt, n * 512:(n + 1) * 512], ys_[n].t[0:nt, :], sa.t[0:nt, 3:4],
                gpm.t[0:nt, n * 512:(n + 1) * 512], ALU.mult, ALU.mult, [ys_[n], sa, gpm], [yt])
        TT("dve", yt.t[0:nt, :], yt.t[0:nt, :], xr.t[0:nt, :], ALU.add, [yt, xr], [yt])
        DMA("sp", x1s[srow:srow + nt, :], yt.t[0:nt, :], [yt], [x1sb[sbuf_i]])

    def sample_phase():
        nt = 64
        st = {}
        stage_AD(xs, nt, st)
        qd = st["qd"]; kb = st["kb"]; kf = st["kf"]; vf = st["vf"]
        qTs = W["qTs"]; kTn = W["kTn"]; Vsn = W["Vsn"]; qm = W["qm"]; khm = W["khm"]
        DMA("sp", ksm_d, kf.t[0:nt, :], [kf], [])
        DMA("sp", vsm_d, vf.t[0:nt, :], [vf], [])
        if stop == "s1":
            return
        for hh in range(4):
            TR(tp.t[:, hh, 0:nt], qd.t[0:nt, hh * 128:(hh + 1) * 128], ident.t[0:nt, 0:nt], [qd, ident], [tp])
            TR(tp.t[:, 4 + hh, 0:nt], kb.t[0:nt, hh * 128:(hh + 1) * 128], ident.t[0:nt, 0:nt], [kb, ident], [tp])
        for s_ in range(2):
            CP("act", qTs.t[64 * s_:64 * s_ + 64, s_, :, :], tp.t[64 * s_:64 * s_ + 64, 0:4, 0:nt], [tp], [qTs])
        CP("dve", kTn.t[:, :, :], tp.t[:, 4:8, 0:nt], [tp], [kTn])
        MEMSET("pool", Vsn.t[:, :, 128:129], 1.0, [Vsn])
        CP("pool", Vsn.t[:, :, 0:128], vf.t[0:nt, :].rearrange("p (h n) -> p h n", h=4), [vf], [Vsn])
        if stop == "s1b":
            return
        stage_gla_gates(nt, True, st)
        if stop == "s2":
            return
        qz = st["qz"]; kh = st["kh"]; v_ = st["vg"]; ebL_ = st["ebL"]
        for b in range(4):
            for p in range(2):
                DMA("sp", S32s[b][p].t[:, :], sg[b, p], [], [S32s[b][p]])
                CP("pool", Sbfs[b][p].t[:, :], S32s[b][p].t[:, :], [S32s[b][p]], [Sbfs[b][p]])
        for h in range(4):
            for b in range(4):
                CP("pool", qm.t[:, h, b, 16 * b:16 * b + 16], qz.t[:, h, 16 * b:16 * b + 16], [qz], [qm])
        for b in range(4):
            TS("dve", khm.t[0:nt, b, :], kh.t[0:nt, 0, :], C["rm_s"].t[0:nt, b:b + 1], None, ALU.mult, None,
               [kh, C["rm_s"]], [khm])
        gla_intra(nt, st, C["maskA_s"])
        for h in range(4):
            p, r0 = h // 2, 64 * (h % 2)
            for b in range(4):
                MM(og.t[0:nt, h * 128:(h + 1) * 128], qm.t[:, h, b, :], Sbfs[b][p].t[:, :],
                   False, b == 3, [qm, Sbfs[b][p]], [og])
        oc = W["ocat"].next()
        gla_out_norm(nt, st, oc)
        if stop == "s3":
            return
        for b in range(4):
            for p in range(2):
                for hp in range(2):
                    h = 2 * p + hp
                    r0 = 64 * hp
                    MM(gU.t[r0:r0 + 64, p * 128:(p + 1) * 128], khm.t[0:nt, b, h * 64:(h + 1) * 64],
                       v_.t[0:nt, h * 128:(h + 1) * 128], True, True, [khm, v_], [gU])
            for p in range(2):
                STT("dve", S32s[b][p].t[:, :], S32s[b][p].t[:, :], ebL_.t[:, 4 * p + b:4 * p + b + 1],
                    gU.t[:, p * 128:(p + 1) * 128], ALU.mult, ALU.add, [S32s[b][p], ebL_, gU], [S32s[b][p]])
                DMA("sp", gs_d[b, p], S32s[b][p].t[:, :], [S32s[b][p]], [])
        if stop == "s4":
            return
        for hh in range(4):
            if stop == "s5" and hh == 1:
                return
            for b in range(4):
                for ch in range(NKT // 8):
                    c_ = W["cst"].next()
                    DMA("sp", c_.t[:, :, :], ck[b, ch * 1024:(ch + 1) * 1024, hh * 128:(hh + 1) * 128]
                        .rearrange("(k p) n -> p k n", p=128), [], [c_])
                    kc_ = W["kcs"].next()
                    CP("act", kc_.t[:, :, :], c_.t[:, :, :], [c_], [kc_])
                    tb = tp if ch % 2 == 0 else tpB
                    for k8 in range(8):
                        TR(tb.t[:, k8, :], kc_.t[:, k8, :], ident.t[:, :], [kc_, ident], [tb])
                    CP("dve", kTs[b].t[:, ch * 1024:(ch + 1) * 1024].rearrange("p (k n) -> p k n", k=8),
                       tb.t[:, :, :], [tb], [kTs[b]])
                    c_ = W["cst"].next()
                    DMA("sp", c_.t[:, :, :], cv[b, ch * 1024:(ch + 1) * 1024, hh * 128:(hh + 1) * 128]
                        .rearrange("(k p) n -> p k n", p=128), [], [c_])
                    CP("dve", Vs[b].t[:, ch * 8:(ch + 1) * 8, 0:128], c_.t[:, :, :], [c_], [Vs[b]])
                if hh == 0:
                    MEMSET("pool", Vs[b].t[:, :, 128:129], 1.0, [Vs[b]])
            acc = accb[hh % 2]
            first = True
            for kt in range(NKT):
                sc = zp.next()
                for b in range(4):
                    MM(sc.t[:, b * 32:(b + 1) * 32].rearrange("p (s q) -> p s q", s=2),
                       kTs[b].t[:, kt * 128:(kt + 1) * 128],
                       qTs.t[:, :, hh, 16 * b:16 * b + 16], True, True, [kTs[b], qTs], [sc])
                pt = W["PTs"].next()
                o_ap = bass.AP(AR.t, pt.off, [[NA, 128], [144, 4], [64, 2], [1, 16]])
                i_ap = sc.t[:, 0:128].rearrange("p (b s q) -> p b s q", b=4, s=2)
                ACT(o_ap, i_ap, AF.Exp, [sc, C["sbias"]], [pt],
                    bias=C["sbias"].t[:, hh * NKT + kt:hh * NKT + kt + 1])
                for b in range(4):
                    for s_ in range(2):
                        MM(acc.t[0:nt, s_ * 129:(s_ + 1) * 129], pt.t[:, b, s_, :], Vs[b].t[:, kt, :],
                           first, False, [pt, Vs[b]], [acc])
                        first = False
            sc = zp.next()
            for s_ in range(2):
                MM(sc.t[0:nt, s_ * 64:(s_ + 1) * 64], kTn.t[:, hh, :],
                   qTs.t[:, s_, hh, :], True, True, [kTn, qTs], [sc])
            sctmp = W["sctmp"]; PTn = W["PTn"]
            TT("dve", sctmp.t[:, :], sc.t[0:nt, 0:128], C["nbias"].t[:, hh * 128:(hh + 1) * 128], ALU.add,
               [sc, C["nbias"]], [sctmp])
            ACT(PTn.t[:, :], sctmp.t[:, :], AF.Exp, [sctmp], [PTn])
            for s_ in range(2):
                MM(acc.t[0:nt, s_ * 129:(s_ + 1) * 129], PTn.t[:, s_ * 64:(s_ + 1) * 64], Vsn.t[:, hh, :],
                   False, True, [PTn, Vsn], [acc])
            t_ = W["t1"].next(); o_ = W["od"].next()
            attn_epilogue(nt, acc, oc, hh, t_.t[0:nt, 0, :], o_.t[0:nt, 0, :], t_, o_)
        stage_wout(nt, oc, xs, T, NT)

    sample_phase()
    P.barrier()
    if stop in ("sample", "s1", "s1b", "s2", "s3", "s4", "s5"):
        return early()

    AR.off = A_FIX
    W.clear()
    kT2 = AR.alloc("kT2", [128, 4, T]).t
    Vaug = AR.alloc("Vaug", [128, NT, 4, 129]).t
    kTb = [Buf("kT%d" % g) for g in range(NG)]
    Vb = [Buf("V%d" % g) for g in range(NG)]
    RD_on[0] = True
    alloc_common()
    RD_on[0] = False
    wring("cvs", 1, [128, 512], F32); wring("cvo", 1, [128, 512])
    wring("qT2", 2, [128, 2, 4, QG])
    for t_ in W["qT2"].tiles:
        MEMSET("pool", t_.t[:, :, :, :], 0.0, [t_])
    wring("PT", 3, [128, 2, QG])
    import os as _os2
    if _os2.environ.get("KSIM"):
        print("phase P arena spare elems:", AR.n - AR.off)
    for ti_ in range(NT):
        MEMSET("pool", Vaug[:, ti_, :, 128:129], 1.0, [Vb[ti_ * 128 // QG]])

    def kv_store_prompt(g, ti, i, st, qa):
        qd = st["qd"]; kb = st["kb"]; kf = st["kf"]; vf = st["vf"]
        for hh in range(4):
            TR(tp.t[:, hh, :], qd.t[:, hh * 128:(hh + 1) * 128], ident.t[:, :], [qd, ident], [tp])
            TR(tp.t[:, 4 + hh, :], kb.t[:, hh * 128:(hh + 1) * 128], ident.t[:, :], [kb, ident], [tp])
        for s_ in range(2):
            CP("act", qa.t[64 * s_:64 * s_ + 64, s_, :, ti * 128:(ti + 1) * 128], tp.t[64 * s_:64 * s_ + 64, 0:4, :],
               [tp], [qa])
        CP("dve", kT2[:, :, i * 128:(i + 1) * 128], tp.t[:, 4:8, :], [tp], [kTb[g]])
        CP("dve", Vaug[:, i, :, 0:128], vf.t[:, :].rearrange("p (h n) -> p h n", h=4), [vf], [Vb[g]])
        DMA("sp", kp_d[i * 128:(i + 1) * 128, :], kf.t[:, :], [kf], [])
        DMA("sp", vp_d[i * 128:(i + 1) * 128, :], vf.t[:, :], [vf], [])

    NQT = QG // 128

    def attention_prompt(g, qa, ocs):
        nkt = NQT * (g + 1)
        btab = C["btab"]
        for hh in range(4):
            started = set()
            pend = []

            def pv(j, q0, pt):
                for s_ in range(2):
                    for qt in range(q0 // 128, NQT):
                        bk, sl = qt, s_
                        first = bk not in started
                        started.add(bk)
                        MM(accb[bk].t[:, sl * 129:(sl + 1) * 129], pt.t[:, s_, qt * 128:(qt + 1) * 128],
                           Vaug[:, j, hh, :], first, j == nkt - 1, [pt, Vb[j * 128 // QG]], [accb[bk]])

            for j in range(nkt):
                jj = j - NQT * g
                q0 = 128 * jj if jj >= 0 else 0
                sc = scr.next()
                sc3 = sc.t[:, :].rearrange("p (s q) -> p s q", s=2)
                for s_ in range(2):
                    MM(sc3[:, s_, q0:QG], kT2[:, hh, j * 128:(j + 1) * 128],
                       qa.t[:, s_, hh, q0:QG], True, jj < 0, [kTb[j * 128 // QG], qa], [sc])
                    if jj >= 0:
                        MM(sc3[:, s_, q0:q0 + 128], ident.t[:, :], C["corr"].t[:, hh * 128:(hh + 1) * 128],
                           False, True, [ident, C["corr"]], [sc])
                pt = W["PT"].next()
                bidx = NQT * g - j + 1
                ACT(pt.t[:, :, q0:QG], sc3[:, :, q0:QG], AF.Exp, [sc, btab], [pt],
                    bias=btab.t[:, hh * (NT + 2) + bidx:hh * (NT + 2) + bidx + 1])
                for a in pend:
                    pv(*a)
                pend = [(j, q0, pt)]
            for a in pend:
                pv(*a)
            t_ = W["t1"].next(); o_ = W["od"].next()
            for qt in range(NQT):
                attn_epilogue(128, accb[qt], ocs[qt], hh, t_.t[:, qt, :], o_.t[:, qt, :], t_, o_)

    conv_jobs = []
    for kc in range(8):
        for q8 in range(8):
            conv_jobs.append(("up", kc, kc * 4096 + q8 * 512))
    for c in range(32):
        for hf in range(2):
            conv_jobs.append(("dn", c, c * 1024 + hf * 512))
    conv_i = [0]
    wupbB = [Buf("wupbB%d" % kc) for kc in range(8)]
    NPRE = 6

    def conv_some(n):
        for _ in range(n):
            if conv_i[0] >= len(conv_jobs):
                return
            kind, kc, off = conv_jobs[conv_i[0]]
            e_ = ("dve", "act")[conv_i[0] % 2]
            conv_i[0] += 1
            s_ = W["cvs"].next(); o_ = W["cvo"].next()
            if kind == "up":
                DMA("sp", s_.t[:, :], wup_d[:, off:off + 512], [], [s_])
                TS(e_, o_.t[:, :], s_.t[:, :], gpf.t[:, kc:kc + 1], None, ALU.mult, None, [s_, gpf], [o_])
                DMA("sp", wupb[:, off:off + 512], o_.t[:, :], [o_], [wupbB[kc]])
            else:
                DMA("sp", s_.t[:, :], wdn_d[:, off:off + 512], [], [s_])
                CP(e_, o_.t[:, :], s_.t[:, :], [s_], [o_])
                DMA("sp", wdnb[:, off:off + 512], o_.t[:, :], [o_], [])

    per_group = (len(conv_jobs) + max(NG - 1, 1) - 1) // max(NG - 1, 1)
    for g in range(NG):
        qa = W["qT2"].next()
        ocs = []
        for ti in range(NQT):
            i = g * NQT + ti
            st = {}
            stage_AD(xp[i * 128:(i + 1) * 128, :], 128, st)
            kv_store_prompt(g, ti, i, st, qa)
            stage_gla_gates(128, False, st)
            oc = W["ocat"].next()
            stage_gla_prompt(st, oc)
            ocs.append(oc)
        attention_prompt(g, qa, ocs)
        for ti in range(NQT):
            i = g * NQT + ti
            stage_wout(128, ocs[ti], xp[i * 128:(i + 1) * 128, :], i * 128, i)
        if g < NG - 1 or NG == 1:
            conv_some(per_group)
    conv_some(len(conv_jobs))
    WupP = AR.ap[:, 0:32768].rearrange("p (k n) -> p k n", k=8)
    for kc in range(NPRE):
        for cb in range(4):
            DMA("sp", WupP[:, kc, cb * 1024:(cb + 1) * 1024],
                wupb[:, kc * 4096 + cb * 1024:kc * 4096 + (cb + 1) * 1024], [wupbB[kc]], [Win])
    for p in range(2):
        DMA("sp", gp_d[p], S32[p].t[:, :], [S32[p]], [])
    P.barrier()
    if stop == "pass1":
        return early()

    AR.off = 0
    W.clear()
    Wup = AR.alloc("Wup", [128, 8, 4096]); Wdn = AR.alloc("Wdn", [128, 32, 1024])
    WupB = [Buf("WupB%d" % i) for i in range(4)]
    WdnB = [Buf("WdnB%d" % i) for i in range(4)]
    wring("xn", 1, [128, 1024])
    wring("fT", 1, [128, 8, 512]); wring("upT", 1, [128, 32, 512])
    wring("xres", 2, [128, 1024], F32)
    wring("rstg", 1, [128, 512], F32)
    wring("x1r", 2, [128, 1024], F32); wring("ytmp", 1, [128, 1024], F32)
    gpo = Tl(gpm.t, "gpo")
    gpo.b = gpm.b
    DMA("sp", gpo.t[:, :], gpo_d, [], [gpo])
    for cb in range(4):
        for kc in range(NPRE, 8):
            DMA("sp", Wup.t[:, kc, cb * 1024:(cb + 1) * 1024], wupb[:, kc * 4096 + cb * 1024:kc * 4096 + (cb + 1) * 1024],
                [], [WupB[cb]])
    for cb in range(4):
        for c in range(cb * 8, cb * 8 + 8):
            DMA("sp", Wdn.t[:, c, :], wdnb[:, c * 1024:(c + 1) * 1024], [], [WdnB[cb]])
    upp = Ring([banks[i] for i in (3, 4, 5, 6)])

    def ffn_group(tiles):
        f_ = W["fT"].next(); u_ = W["upT"].next()
        col = 0
        cols = []
        for (nt, srow, out_ap, bi) in tiles:
            x1 = W["x1r"].next()
            DMA("sp", x1.t[0:nt, :], x1s[srow:srow + nt, :], [x1sb[bi]], [x1])
            sa = stat.next()
            xn_ = W["xn"].next()
            SQ(xn_.t[0:nt, :], xn_, x1.t[0:nt, :], [x1], sa, sa.t[0:nt, 0:1])
            rstd_from_ss(sa.t[0:nt, 0:1], 1024.0, nt, sa.t[0:nt, 1:2], sa)
            TS("dve", xn_.t[0:nt, :], x1.t[0:nt, :], sa.t[0:nt, 1:2], None, ALU.mult, None, [x1, sa], [xn_])
            for kc in range(8):
                TR(tp.t[:, kc, 0:nt], xn_.t[0:nt, kc * 128:(kc + 1) * 128], ident.t[0:nt, 0:nt], [xn_, ident], [tp])
            CP(alt2(), f_.t[:, :, col:col + nt], tp.t[:, :, 0:nt], [tp], [f_])
            cols.append(col)
            col += nt
        ntok = col
        for c in range(32):
            up = upp.next()
            for kc in range(8):
                MM(up.t[:, 0:ntok], Wup.t[:, kc, c * 128:(c + 1) * 128], f_.t[:, kc, 0:ntok], kc == 0, kc == 7,
                   [WupB[c // 8], f_], [up])
            r_ = W["rstg"].next()
            ACT(r_.t[:, 0:ntok], up.t[:, 0:ntok], AF.Relu, [up], [r_])
            TT("dve", u_.t[:, c, 0:ntok], r_.t[:, 0:ntok], r_.t[:, 0:ntok], ALU.mult, [r_], [u_])
        for k_, (nt, srow, out_ap, bi) in enumerate(tiles):
            c0 = cols[k_]
            x1 = W["xres"].next()
            DMA("sp", x1.t[0:nt, :], x1s[srow:srow + nt, :], [x1sb[bi]], [x1])
            ys_ = []
            for n in range(2):
                z = zp.next()
                for c in range(32):
                    MM(z.t[0:nt, :], u_.t[:, c, c0:c0 + nt], Wdn.t[:, c, n * 512:(n + 1) * 512], c == 0, c == 31,
                       [u_, WdnB[c // 8]], [z])
                ys_.append(z)
            sa = stat.next()
            yt = W["ytmp"].next()
            for n in range(2):
                SQ(yt.t[0:nt, n * 512:(n + 1) * 512], yt, ys_[n].t[0:nt, :], [ys_[n]], sa, sa.t[0:nt, n:n + 1])
            TT("dve", sa.t[0:nt, 2:3], sa.t[0:nt, 0:1], sa.t[0:nt, 1:2], ALU.add, [sa], [sa])
            rstd_from_ss(sa.t[0:nt, 2:3], 1024.0, nt, sa.t[0:nt, 3:4], sa)
            for n in range(2):
                STT("dve", yt.t[0:nt, n * 512:(n + 1) * 512], ys_[n].t[0:nt, :], sa.t[0:nt, 3:4],
                    gpo.t[0:nt, n * 512:(n + 1) * 512], ALU.mult, ALU.mult, [ys_[n], sa, gpo], [yt])
            TT("dve", yt.t[0:nt, :], yt.t[0:nt, :], x1.t[0:nt, :], ALU.add, [yt, x1], [yt])
            DMA("sp", out_ap, yt.t[0:nt, :], [yt], [])

    for i in range(0, NT, 4):
        ffn_group([(128, (i + k) * 128, yp_d[(i + k) * 128:(i + k + 1) * 128, :], i + k) for k in range(4)])
    ffn_group([(64, T, ys_d, NT)])
    P.finish()
    P.emit()
    return nc


def _core_inputs(c, T, PAST, I, consts):
    f = np.float32
    w_in = np.asarray(I["w_in"][0], f)
    perm = np.concatenate([w_in[:, 1536:1552], w_in[:, 0:1536], w_in[:, 1552:]], axis=1)
    m = {}
    m["xp"] = np.ascontiguousarray(I["x_prompt"][c], f)
    m["xs"] = np.ascontiguousarray(I["x_sample"][4 * c:4 * c + 4], f).reshape(64, 1024)
    m["ck"] = np.ascontiguousarray(I["cache_k"][0, 4 * c:4 * c + 4], f).reshape(4, PAST, 512)
    m["cv"] = np.ascontiguousarray(I["cache_v"][0, 4 * c:4 * c + 4], f).reshape(4, PAST, 512)
    m["sg"] = np.ascontiguousarray(I["state_gla"][0, 4 * c:4 * c + 4], f).reshape(4, 2, 128, 128)
    m["win"] = np.ascontiguousarray(perm.reshape(8, 128, 3088).transpose(1, 0, 2)).reshape(128, 8 * 3088)
    m["wout"] = np.ascontiguousarray(np.asarray(I["w_out"][0], f).reshape(8, 128, 1024).transpose(1, 0, 2)).reshape(128, 8192)
    m["wup"] = np.ascontiguousarray(np.asarray(I["w_ff_up"][0], f).reshape(8, 128, 4096).transpose(1, 0, 2)).reshape(128, 8 * 4096)
    m["wdn"] = np.ascontiguousarray(np.asarray(I["w_ff_down"][0], f).reshape(32, 128, 1024).transpose(1, 0, 2)).reshape(128, 32 * 1024)
    m["wgu"] = np.ascontiguousarray(np.concatenate([np.asarray(I["w_gate_up"][0], f), np.asarray(I["b_gate"][0], f)[None]], 0))
    m["gpre"] = np.ascontiguousarray(np.asarray(I["g_pre_mix"][0], f).reshape(8, 128).T)
    m["gpf"] = np.ascontiguousarray(np.asarray(I["g_pre_ffn"][0], f).reshape(8, 128).T)
    m["ggla"] = np.ascontiguousarray(np.broadcast_to(np.tile(np.asarray(I["g_gla_out"][0], f), 4)[None], (128, 512)))
    m["gsub"] = np.ascontiguousarray(np.broadcast_to(np.asarray(I["g_subln"][0], f)[None], (128, 128)))
    m["gpm"] = np.ascontiguousarray(np.broadcast_to(np.asarray(I["g_post_mix"][0], f)[None], (128, 1024)))
    m["gpo"] = np.ascontiguousarray(np.broadcast_to(np.asarray(I["g_post_ffn"][0], f)[None], (128, 1024)))
    lam = np.concatenate([np.asarray(I[k][0], f) for k in ("lam_q1", "lam_k1", "lam_q2", "lam_k2")])
    m["lam"] = np.ascontiguousarray(np.broadcast_to(lam[None], (128, 256)))
    for k, v in consts.items():
        m["c_" + k] = v
    return m


_CACHE = {}


def run_cores(I, T, PAST, ncores):
    key = (T, PAST)
    if key not in _CACHE:
        _CACHE[key] = (build_program(T, PAST), make_consts(T, PAST))
    nc, consts = _CACHE[key]
    in_maps = [_core_inputs(c, T, PAST, I, consts) for c in range(ncores)]
    res = run_bass_kernel_spmd(nc, in_maps, core_ids=list(range(ncores)))
    return res.results


def kernel(**inputs):
    T, PAST, NCORES = 4096, 2048, 8
    R = run_cores(inputs, T, PAST, NCORES)
    f = np.float32
    yp = np.stack([np.asarray(r["yp"], f) for r in R], 0)
    ys = np.concatenate([np.asarray(r["ys"], f).reshape(4, 16, 1024) for r in R], 0)
    kp = np.stack([np.asarray(r["kp"], f).reshape(T, 4, 128) for r in R], 0)[None]
    vp = np.stack([np.asarray(r["vp"], f).reshape(T, 4, 128) for r in R], 0)[None]
    gp = np.stack([np.asarray(r["gp"], f).reshape(4, 64, 128) for r in R], 0)[None]
    ksm = np.concatenate([np.asarray(r["ksm"], f).reshape(4, 16, 4, 128) for r in R], 0)[None]
    vsm = np.concatenate([np.asarray(r["vsm"], f).reshape(4, 16, 4, 128) for r in R], 0)[None]
    gs = np.concatenate([np.asarray(r["gs"], f).reshape(4, 4, 64, 128) for r in R], 0)[None]
    return (yp, ys, kp, vp, gp, ksm, vsm, gs)
```
